# Optimizing a Trainium2 kernel written in Bass

```python
import math
import jax, jax.numpy as jnp
from jax import lax
import numpy as np

D_MODEL = 1024
BATCH = 8
SEQ = 2048
DEPTH = 2

HEAD_DIM = 64
MOBA_HEADS = (D_MODEL // 2) // HEAD_DIM
MOBA_WIDTH = MOBA_HEADS * HEAD_DIM
DIFF_HEADS = (D_MODEL // 2) // (2 * HEAD_DIM)
DIFF_V_DIM = 2 * HEAD_DIM
DIFF_WIDTH = DIFF_HEADS * DIFF_V_DIM
MIX_WIDTH = MOBA_WIDTH + DIFF_WIDTH
IN_WIDTH = 3 * MOBA_WIDTH + 3 * DIFF_WIDTH
MOBA_BLOCK = 256
MOBA_TOPK = 3
MOBA_Q_CHUNK = 64
DENSE_Q_BLOCK = 128
ROPE_THETA = 500000.0
ROPE_DIM = HEAD_DIM // 4
D_FF = -(-(8 * D_MODEL) // (3 * 256)) * 256
N_MOD = 6
EPS = 1e-6

kernel_name = 'hybrid_moba_diffattn_adaln_block'


def rms_norm(x, g):
    xf = x.astype(jnp.float32)
    y = xf * lax.rsqrt(jnp.mean(xf * xf, axis=-1, keepdims=True) + EPS)
    return (y * g.astype(jnp.float32)).astype(x.dtype)


def rope_tables(positions):
    inv = ROPE_THETA ** (-jnp.arange(0, ROPE_DIM, 2, dtype=jnp.float32) / ROPE_DIM)
    ang = positions.astype(jnp.float32)[..., None] * inv
    return jnp.cos(ang)[:, None], jnp.sin(ang)[:, None]


def apply_partial_rope(x, cos, sin):
    half = ROPE_DIM // 2
    xf = x.astype(jnp.float32)
    x1 = xf[..., :half]
    x2 = xf[..., half:ROPE_DIM]
    out = jnp.concatenate([x1 * cos - x2 * sin, x2 * cos + x1 * sin, xf[..., ROPE_DIM:]], axis=-1)
    return out.astype(x.dtype)


def moba_attention(q, k, v):
    B, H, S, dh = q.shape
    nb = -(-S // MOBA_BLOCK)
    pad = nb * MOBA_BLOCK - S
    kp = jnp.pad(k, ((0, 0), (0, 0), (0, pad), (0, 0)))
    vp = jnp.pad(v, ((0, 0), (0, 0), (0, pad), (0, 0)))
    kb = kp.reshape(B, H, nb, MOBA_BLOCK, dh)
    vb = vp.reshape(B, H, nb, MOBA_BLOCK, dh)
    k_mean = jnp.mean(kb.astype(jnp.float32), axis=3)
    topk = min(MOBA_TOPK, nb)
    n_chunks = S // MOBA_Q_CHUNK
    scale = dh ** -0.5
    b_ids = jnp.repeat(jnp.arange(B, dtype=jnp.int32), n_chunks)
    c_ids = jnp.tile(jnp.arange(n_chunks, dtype=jnp.int32), B)
    h_ids = jnp.arange(H)[:, None, None]
    blk_ids = jnp.arange(nb)

    def one_chunk(args):
        b, ci = args
        q0 = ci * MOBA_Q_CHUNK
        qblk = q0 // MOBA_BLOCK
        qpos = q0 + jnp.arange(MOBA_Q_CHUNK)
        qc = lax.dynamic_slice_in_dim(q[b], q0, MOBA_Q_CHUNK, axis=1)
        gate = jnp.einsum('hqd,hnd->hqn', qc.astype(jnp.float32), k_mean[b])
        gate = jnp.where(blk_ids < qblk, gate, -jnp.inf)
        _, idx = lax.top_k(gate, topk)
        sel_valid = idx < qblk
        k_sel = kb[b][h_ids, idx]
        v_sel = vb[b][h_ids, idx]
        s_sel = jnp.einsum('hqd,hqjkd->hqjk', qc, k_sel).astype(jnp.float32) * scale
        s_sel = jnp.where(sel_valid[..., None], s_sel, -jnp.inf)
        k_own = lax.dynamic_slice_in_dim(kp[b], qblk * MOBA_BLOCK, MOBA_BLOCK, axis=1)
        v_own = lax.dynamic_slice_in_dim(vp[b], qblk * MOBA_BLOCK, MOBA_BLOCK, axis=1)
        s_own = jnp.einsum('hqd,hkd->hqk', qc, k_own).astype(jnp.float32) * scale
        kpos = qblk * MOBA_BLOCK + jnp.arange(MOBA_BLOCK)
        s_own = jnp.where(kpos[None, :] <= qpos[:, None], s_own, -jnp.inf)
        s = jnp.concatenate([s_sel.reshape(H, MOBA_Q_CHUNK, topk * MOBA_BLOCK), s_own], axis=-1)
        p = jax.nn.softmax(s, axis=-1).astype(v.dtype)
        p_sel = p[..., :topk * MOBA_BLOCK].reshape(H, MOBA_Q_CHUNK, topk, MOBA_BLOCK)
        p_own = p[..., topk * MOBA_BLOCK:]
        return (jnp.einsum('hqjk,hqjkd->hqd', p_sel, v_sel)
                + jnp.einsum('hqk,hkd->hqd', p_own, v_own))

    out = lax.map(one_chunk, (b_ids, c_ids))
    out = out.reshape(B, n_chunks, H, MOBA_Q_CHUNK, dh).transpose(0, 2, 1, 3, 4)
    return out.reshape(B, H, S, dh)


def diff_attention(q, k, v, lam):
    B, H, _, S, dh = q.shape
    n_qb = S // DENSE_Q_BLOCK
    scale = dh ** -0.5
    kpos = jnp.arange(S)

    def one_block(ci):
        q0 = ci * DENSE_Q_BLOCK
        qc = lax.dynamic_slice_in_dim(q, q0, DENSE_Q_BLOCK, axis=3)
        s = jnp.einsum('bhcqd,bhckd->bhcqk', qc, k).astype(jnp.float32) * scale
        qpos = q0 + jnp.arange(DENSE_Q_BLOCK)
        s = jnp.where(kpos[None, :] <= qpos[:, None], s, -jnp.inf)
        a = jax.nn.softmax(s, axis=-1)
        w = (a[:, :, 0] - lam * a[:, :, 1]).astype(v.dtype)
        return jnp.einsum('bhqk,bhkd->bhqd', w, v)

    out = lax.map(one_block, jnp.arange(n_qb, dtype=jnp.int32))
    return out.transpose(1, 2, 0, 3, 4).reshape(B, H, S, v.shape[-1])


def setup_inputs(seed: int = 0) -> dict:
    key = jax.random.key(seed)
    ks = jax.random.split(key, 20)
    f32 = jnp.float32
    L = DEPTH

    def nrm(k, shape, scale):
        return jax.random.normal(k, shape, f32) * scale

    def gain(k, shape):
        return 1.0 + 0.02 * jax.random.normal(k, shape, f32)

    return {
        'x': nrm(ks[0], (BATCH, SEQ, D_MODEL), 1.0),
        'c': nrm(ks[1], (BATCH, D_MODEL), 1.0),
        'positions': jnp.broadcast_to(jnp.arange(SEQ, dtype=jnp.int32)[None, :], (BATCH, SEQ)),
        'w_mod': nrm(ks[2], (L, D_MODEL, N_MOD * D_MODEL), 0.5 * D_MODEL ** -0.5),
        'b_mod': nrm(ks[3], (L, N_MOD * D_MODEL), 0.01),
        'norm_mix': gain(ks[4], (L, D_MODEL)),
        'w_in': nrm(ks[5], (L, D_MODEL, IN_WIDTH), D_MODEL ** -0.5),
        'moba_q_norm': gain(ks[6], (L, HEAD_DIM)),
        'moba_k_norm': gain(ks[7], (L, HEAD_DIM)),
        'moba_out_norm': gain(ks[8], (L, HEAD_DIM)),
        'diff_q_norm': gain(ks[9], (L, HEAD_DIM)),
        'diff_k_norm': gain(ks[10], (L, HEAD_DIM)),
        'diff_lambda': nrm(ks[11], (L, 4, HEAD_DIM), 0.1),
        'diff_subln': gain(ks[12], (L, DIFF_V_DIM)),
        'w_out': nrm(ks[13], (L, MIX_WIDTH, D_MODEL), MIX_WIDTH ** -0.5),
        'norm_ffn': gain(ks[14], (L, D_MODEL)),
        'w_gate': nrm(ks[15], (L, D_MODEL, D_FF), D_MODEL ** -0.5),
        'w_up': nrm(ks[16], (L, D_MODEL, D_FF), D_MODEL ** -0.5),
        'w_down': nrm(ks[17], (L, D_FF, D_MODEL), D_FF ** -0.5),
    }


def reference(x, c, positions, w_mod, b_mod, norm_mix, w_in, moba_q_norm, moba_k_norm,
              moba_out_norm, diff_q_norm, diff_k_norm, diff_lambda, diff_subln, w_out,
              norm_ffn, w_gate, w_up, w_down):
    B, S, _ = x.shape
    cos, sin = rope_tables(positions)
    cond = jax.nn.silu(c)
    splits = [MOBA_WIDTH, 2 * MOBA_WIDTH, 3 * MOBA_WIDTH,
              3 * MOBA_WIDTH + DIFF_WIDTH, 3 * MOBA_WIDTH + 2 * DIFF_WIDTH]

    def heads(t, n, d):
        return t.reshape(B, S, n, d).transpose(0, 2, 1, 3)

    for l in range(DEPTH):
        mod = (cond @ w_mod[l] + b_mod[l])[:, None, :]
        sh_a, sc_a, g_a, sh_f, sc_f, g_f = jnp.split(mod, N_MOD, axis=-1)

        h = rms_norm(x, norm_mix[l]) * (1 + sc_a) + sh_a
        proj = h @ w_in[l]
        mq, mk, mv, dq, dk, dv = jnp.split(proj, splits, axis=-1)

        mq = apply_partial_rope(rms_norm(heads(mq, MOBA_HEADS, HEAD_DIM), moba_q_norm[l]), cos, sin)
        mk = apply_partial_rope(rms_norm(heads(mk, MOBA_HEADS, HEAD_DIM), moba_k_norm[l]), cos, sin)
        mv = heads(mv, MOBA_HEADS, HEAD_DIM)
        o_m = rms_norm(moba_attention(mq, mk, mv), moba_out_norm[l])
        o_m = o_m.transpose(0, 2, 1, 3).reshape(B, S, MOBA_WIDTH)

        dq = apply_partial_rope(rms_norm(heads(dq, 2 * DIFF_HEADS, HEAD_DIM), diff_q_norm[l]), cos, sin)
        dk = apply_partial_rope(rms_norm(heads(dk, 2 * DIFF_HEADS, HEAD_DIM), diff_k_norm[l]), cos, sin)
        dq = dq.reshape(B, DIFF_HEADS, 2, S, HEAD_DIM)
        dk = dk.reshape(B, DIFF_HEADS, 2, S, HEAD_DIM)
        dv = heads(dv, DIFF_HEADS, DIFF_V_DIM)
        lam_init = 0.8 - 0.6 * math.exp(-0.3 * l)
        lp = diff_lambda[l].astype(jnp.float32)
        lam = jnp.exp(jnp.sum(lp[0] * lp[1])) - jnp.exp(jnp.sum(lp[2] * lp[3])) + lam_init
        o_d = rms_norm(diff_attention(dq, dk, dv, lam), diff_subln[l]) * (1.0 - lam_init)
        o_d = o_d.transpose(0, 2, 1, 3).reshape(B, S, DIFF_WIDTH)

        x = x + g_a * (jnp.concatenate([o_m, o_d], axis=-1) @ w_out[l])

        h = rms_norm(x, norm_ffn[l]) * (1 + sc_f) + sh_f
        x = x + g_f * ((jax.nn.silu(h @ w_gate[l]) * (h @ w_up[l])) @ w_down[l])
    return x
```

```python
import math
from contextlib import ExitStack
import numpy as np
import concourse.bass as bass
import concourse.mybir as mybir
from concourse.bass_utils import run_bass_kernel_spmd

F32 = mybir.dt.float32
BF16 = mybir.dt.bfloat16
I32 = mybir.dt.int32
ALU = mybir.AluOpType
AF = mybir.ActivationFunctionType
AX = mybir.AxisListType

D = 1024
DFF = 2816
NFF = DFF // 128
EPS = 1e-6
BIG = 30000.0
NDMA = {"sp": 20, "pool": 12}
COMPUTE = ("pe", "act", "dve", "pool")


class Op:
    __slots__ = ("eng", "fn", "reads", "writes", "dma", "waits", "ev")

    def __init__(self, eng, fn, reads, writes, dma):
        self.eng, self.fn, self.reads, self.writes, self.dma = eng, fn, reads, writes, dma
        self.waits = []
        self.ev = None


class Prog:
    def __init__(self):
        self.ops = []
        self.marks = {}

    def add(self, eng, name, args, kwargs, reads=(), writes=(), dma=False):
        def fn(e, name=name, args=args, kwargs=kwargs):
            return getattr(e, name)(*args, **kwargs)
        self.ops.append(Op(eng, fn, tuple(reads), tuple(writes), dma))

    def pe(self, name, *args, r=(), w=(), **kw):
        self.add("pe", name, args, kw, r, w)

    def act(self, name, *args, r=(), w=(), **kw):
        self.add("act", name, args, kw, r, w)

    def dve(self, name, *args, r=(), w=(), **kw):
        self.add("dve", name, args, kw, r, w)

    def pool(self, name, *args, r=(), w=(), **kw):
        self.add("pool", name, args, kw, r, w)

    def dma(self, q, out, in_, r=(), w=()):
        self.add(q, "dma_start", (), dict(out=out, in_=in_), r, w, dma=True)

    def barrier(self, mark=None):
        self.ops.append("BARRIER")
        if mark:
            self.marks[mark] = len(self.ops)

    def analyze(self):
        cnt = {e: 0 for e in COMPUTE}
        dcnt = {q: 0 for q in NDMA}
        last_w, readers = {}, {}
        seen = {e: {} for e in ("pe", "act", "dve", "pool", "sp")}
        all_events = {}

        def need(op, ev):
            if ev is None:
                return
            key, val = ev
            if key == "pe" and op.eng == "pe" and not op.dma:
                return
            s = seen[op.eng]
            if s.get(key, 0) >= val:
                return
            s[key] = val
            op.waits.append((key, val))

        out = []
        for op in self.ops:
            if op == "BARRIER":
                b = Op("all", None, (), (), False)
                b.waits = dict(all_events)
                for e in seen:
                    for k, v in all_events.items():
                        if seen[e].get(k, 0) < v:
                            seen[e][k] = v
                out.append(b)
                continue
            if op.dma:
                q = op.eng
                i = dcnt[q]
                dcnt[q] += 1
                key = ("dma", q, i % NDMA[q])
                val = 16 * (i // NDMA[q] + 1)
                if val > 16:
                    need(op, (key, val - 16))
                op.ev = (key, val)
            else:
                cnt[op.eng] += 1
                op.ev = (op.eng, cnt[op.eng])
            for r in op.reads:
                need(op, last_w.get(r))
            for w in op.writes:
                need(op, last_w.get(w))
                for ev in readers.get(w, ()):
                    need(op, ev)
            for r in op.reads:
                readers.setdefault(r, []).append(op.ev)
            for w in op.writes:
                last_w[w] = op.ev
                readers[w] = []
            all_events[op.ev[0]] = max(all_events.get(op.ev[0], 0), op.ev[1])
            out.append(op)
        self.sched = out
        self.final_events = dict(all_events)

    def bodies(self, sems):
        self.analyze()
        sched, final_events = self.sched, self.final_events

        def body_for(ename):
            def body(eng):
                seen = {}
                for op in sched:
                    if op.eng == "all":
                        for k, v in op.waits.items():
                            if k != ename and seen.get(k, 0) < v:
                                eng.wait_ge(sems[k], v)
                                seen[k] = v
                        continue
                    if op.eng != ename:
                        continue
                    for k, v in op.waits:
                        eng.wait_ge(sems[k], v)
                        seen[k] = max(seen.get(k, 0), v)
                    inst = op.fn(eng)
                    inst.then_inc(sems[op.ev[0]], 16 if op.dma else 1)
                if ename == "sp":
                    for k, v in final_events.items():
                        if seen.get(k, 0) < v:
                            eng.wait_ge(sems[k], v)
            return body

        return {e: body_for(e) for e in ("pe", "act", "dve", "pool", "sp")}


def all_sem_keys():
    keys = list(COMPUTE)
    for q, n in NDMA.items():
        keys += [("dma", q, s) for s in range(n)]
    return keys


def host_consts():
    f = np.arange(128)
    d = f % 64
    inv = (500000.0 ** (-np.arange(0, 16, 2, dtype=np.float32) / 16.0)).astype(np.float32)
    cst = np.zeros((128, 8), np.float32)
    cst[:, 0] = np.where(d < 16, inv[d % 8], 0.0)
    cst[:, 1] = np.where(d < 8, -1.0, np.where(d < 16, 1.0, 0.0))
    cst[:, 2] = EPS
    cst[:, 3] = 1.0
    pm = np.zeros((128, 128), np.float32)
    for m in range(128):
        dm = m % 64
        if dm < 8:
            pm[m + 8, m] = 1.0
        elif dm < 16:
            pm[m - 8, m] = 1.0
    bones = np.zeros((128, 128), np.float32)
    bones[:64, :64] = 1.0 / 64
    bones[64:, 64:] = 1.0 / 64
    return cst, pm, bones


def build(S=2048, L=2, dbg=(), stop=None):
    NT, NG, NB = S // 128, S // 512, S // 256
    VW = 8 * 65 + 4 * 129
    nc = bass.Bass("TRN2", target_bir_lowering=False)

    def din(name, shape, dt=F32):
        return nc.dram_tensor(name, list(shape), dt, kind="ExternalInput").ap()

    x_in = din("x", [S, D])
    cT_in = din("cT", [128, 8])
    pos_in = din("pos", [1, S], I32)
    w_mod = din("w_mod", [L, D, 6 * D])
    b_modT = din("b_modT", [L, 128, 48])
    b_mod = din("b_mod", [L, 1, 6 * D])
    nmixT = din("nmixT", [L, 128, 8])
    nffnT = din("nffnT", [L, 128, 8])
    w_in = din("w_in", [L, D, 3072])
    gcols_in = din("gcols", [L, 128, 4])
    mon_in = din("mon", [L, 1, 64])
    subln_in = din("subln", [L, 1, 128])
    dlam_in = din("dlam", [L, 1, 256])
    w_out = din("w_out", [L, D, D])
    w_gate = din("w_gate", [L, D, DFF])
    w_up = din("w_up", [L, D, DFF])
    w_down = din("w_down", [L, DFF, D])
    cst_in = din("cst", [128, 8])
    pm_in = din("pmat", [128, 128])
    bones_in = din("bones", [128, 128])
    y_out = nc.dram_tensor("y", [S, D], F32, kind="ExternalOutput").ap()
    xmid = [nc.dram_tensor(f"xmid{l}", [S, D], F32, kind="Internal").ap() for l in range(L)]
    xlay = [nc.dram_tensor(f"xlay{l}", [S, D], F32, kind="Internal").ap() for l in range(L - 1)]
    dbg_out = {}
    for name, shape, dt in dbg:
        dbg_out[name] = nc.dram_tensor(name, list(shape), dt, kind="ExternalOutput").ap()

    es = ExitStack()
    with es:
        def sb(name, shape, dt):
            return es.enter_context(nc.sbuf_tensor(name, list(shape), dt))

        def psum(name, shape, dt):
            return es.enter_context(nc.psum_tensor(name, list(shape), dt))

        R1 = sb("R1", [128, max(8 * S, NFF * 512)], BF16)
        R2N = max(16 * S + NT * VW, NFF * S, 8 * 6144 if S >= 512 else 0)
        R2 = sb("R2", [128, R2N], BF16)
        ctab = sb("ctab", [128, S], F32)
        stab = sb("stab", [128, S], F32)
        gate_a = sb("gate_a", [128, D], F32)
        gate_f = sb("gate_f", [128, D], F32)
        scrF = sb("scrF", [128, 2048], F32)
        scrB = sb("scrB", [128, 8192], BF16)
        WS = sb("WS", [128, 8192], BF16)
        identb = sb("identb", [128, 128], BF16)
        trimask = sb("trimask", [128, 128], BF16)
        pmb = sb("pmb", [128, 128], BF16)
        bonesb = sb("bonesb", [128, 128], BF16)
        cst = sb("cst_sb", [128, 8], F32)
        small = sb("small", [128, 512], F32)
        cond_rep = sb("cond_rep", [128, 8, 128], BF16)
        condb = sb("condb", [128, 8], BF16)
        kmT = sb("kmT", [128, 4, 8], BF16)
        pb = [psum(f"pb{i}", [128, 512], F32) for i in range(6)]
        pt = [psum(f"pt{i}", [128, 1024], BF16) for i in range(2)]
        sems = {}
        for k in all_sem_keys():
            nm = "s_" + ("_".join(map(str, k)) if isinstance(k, tuple) else k)
            sems[k] = es.enter_context(nc.semaphore(nm))

        P = Prog()
        hT = R1[:, 0:8 * S].rearrange("p (c s) -> p c s", c=8)
        qkT = R2[:, 0:16 * S].rearrange("p (c s) -> p c s", c=16)
        Vall = R2[:, 16 * S:16 * S + NT * VW].rearrange("p (t w) -> p t w", t=NT)
        Vm = Vall[:, :, 0:520].rearrange("p t (h e) -> p t h e", h=8)
        Vd = Vall[:, :, 520:VW].rearrange("p t (h e) -> p t h e", h=4)
        actT = R2[:, 0:NFF * S].rearrange("p (j s) -> p j s", j=NFF)
        wm = R2[:, 0:8 * 6144].rearrange("p (k n) -> p k n", k=8)
        modT = small[:, 0:48]
        sc1, bi1, sc2, bi2 = small[:, 48:56], small[:, 56:64], small[:, 64:72], small[:, 72:80]
        gcols = small[:, 80:84]
        lamcol = small[:, 84:85]
        tmpc = small[:, 85:96]
        nmx = small[:, 96:104]
        nfx = small[:, 104:112]
        bmt = small[:, 112:160]
        mon_bc = small[:, 160:224]
        subln_bc = small[:, 224:352]
        cond = small[:, 352:360]
        ctf = small[:, 360:368]
        lamt = small[:, 368:400]
        kmf = small[:, 400:432].rearrange("p (c n) -> p c n", c=4)
        eps_col = cst[:, 2:3]

        P.dma("sp", cst[:], cst_in, w=["cst"])
        P.dma("sp", ctf, cT_in, w=["ctf"])
        P.dma("pool", pmb[:], pm_in, w=["pmb"])
        P.dma("pool", bonesb[:], bones_in, w=["bonesb"])
        idf = scrF[:, 0:128]
        P.pool("memset", idf, 1.0, w=["idf"])
        P.pool("affine_select", out=idf, in_=idf, pattern=[[-1, 128]], compare_op=ALU.is_equal, fill=0.0,
                                         base=0, channel_multiplier=1, r=["idf"], w=["idf"])
        P.dve("tensor_copy", out=identb[:], in_=idf, r=["idf"], w=["identb"])
        trf = scrF[:, 128:256]
        P.pool("memset", trf, 1.0, w=["trf"])
        P.pool("affine_select", out=trf, in_=trf, pattern=[[1, 128]], compare_op=ALU.is_ge, fill=0.0,
                                         base=0, channel_multiplier=-1, r=["trf"], w=["trf"])
        P.dve("tensor_copy", out=trimask[:], in_=trf, r=["trf"], w=["trimask"])
        P.act("activation", out=cond, in_=ctf, func=AF.Silu, r=["ctf"], w=["cond"])
        P.dve("tensor_copy", out=condb[:], in_=cond, r=["cond"], w=["condb"])
        P.dve("tensor_copy", out=cond_rep[:], in_=cond.unsqueeze(2).broadcast_to([128, 8, 128]),
              r=["cond"], w=["cond_rep"])
        R1f = R1[:, 0:8 * S].bitcast(F32)
        tmpa = R1f[:, 0:S]
        tmpk = R1f[:, S:2 * S]
        tmpki = R1[:, 0:8 * S].bitcast(I32)[:, 2 * S:3 * S]
        tmpkf = R1f[:, 3 * S:4 * S]
        P.dma("sp", ctab[:].bitcast(I32), pos_in.partition_broadcast(128), w=["ctab"])
        P.dve("tensor_copy", out=stab[:], in_=ctab[:].bitcast(I32), r=["ctab"], w=["stab"])
        P.dve("tensor_scalar", out=stab[:], in0=stab[:], scalar1=cst[:, 0:1], scalar2=None, op0=ALU.mult,
              r=["stab", "cst"], w=["stab"])

        def range_reduce_sin(dst, shift, scale_ap):
            P.dve("tensor_scalar", out=tmpa, in0=stab[:], scalar1=float(shift), scalar2=None, op0=ALU.add,
                  r=["stab"], w=["tmpa"])
            P.dve("tensor_scalar", out=tmpk, in0=tmpa, scalar1=float(1.0 / (2 * math.pi)), scalar2=None,
                                            op0=ALU.mult, r=["tmpa"], w=["tmpk"])
            P.dve("tensor_copy", out=tmpki, in_=tmpk, r=["tmpk"], w=["tmpki"])
            P.dve("tensor_copy", out=tmpkf, in_=tmpki, r=["tmpki"], w=["tmpkf"])
            P.dve("scalar_tensor_tensor", out=tmpa, in0=tmpkf, scalar=float(-2 * math.pi), in1=tmpa,
                                                   op0=ALU.mult, op1=ALU.add, r=["tmpkf", "tmpa"], w=["tmpa"])
            P.dve("tensor_scalar", out=tmpk, in0=tmpa, scalar1=float(math.pi), scalar2=float(-2 * math.pi),
                                            op0=ALU.is_gt, op1=ALU.mult, r=["tmpa"], w=["tmpk"])
            P.dve("tensor_tensor", out=tmpa, in0=tmpa, in1=tmpk, op=ALU.add, r=["tmpa", "tmpk"], w=["tmpa"])
            P.dve("tensor_scalar", out=tmpk, in0=tmpa, scalar1=float(-math.pi), scalar2=float(2 * math.pi),
                                            op0=ALU.is_lt, op1=ALU.mult, r=["tmpa"], w=["tmpk"])
            P.dve("tensor_tensor", out=tmpa, in0=tmpa, in1=tmpk, op=ALU.add, r=["tmpa", "tmpk"], w=["tmpa"])
            if scale_ap is None:
                P.act("activation", out=dst, in_=tmpa, func=AF.Sin, r=["tmpa"], w=[dst.name if False else "tab"])
            else:
                P.act("activation", out=dst, in_=tmpa, func=AF.Sin, scale=scale_ap, r=["tmpa", "cst"], w=["tab"])

        range_reduce_sin(ctab[:], math.pi / 2, None)
        P.barrier("setup1")
        range_reduce_sin(stab[:], 0.0, cst[:, 1:2])
        P.barrier("setup2")

        def load_w(dst, src, key):
            P.dma("pool", dst, src, w=[key])

        def rstd_from(ss, n, out_col, tag):
            l1 = tmpc[:, 10:11]
            P.act("activation", out=l1, in_=ss, func=AF.Ln, scale=1.0 / n, bias=eps_col, r=[tag, "cst"], w=["l1"])
            P.act("activation", out=out_col, in_=l1, func=AF.Exp, scale=-0.5, r=["l1"], w=[tag + "r"])

        def ln_phase(xsrc, sc, bi, key_sc):
            xt = scrF[:, 0:1024]
            xnb = scrB[:, 0:1024]
            junk = scrB[:, 1024:2048]
            for t in range(NT):
                P.dma("sp", xt, xsrc[t * 128:(t + 1) * 128, :], w=["xt"])
                ss = tmpc[:, 0:1]
                P.act("activation", out=junk, in_=xt, func=AF.Square, accum_out=ss, r=["xt"], w=["junk", "ss"])
                rs = tmpc[:, 1:2]
                rstd_from(ss, D, rs, "ss")
                P.dve("tensor_scalar", out=xnb, in0=xt, scalar1=rs, scalar2=None, op0=ALU.mult,
                      r=["xt", "ssr"], w=["xnb"])
                ptt = pt[t % 2]
                pk = f"pt{t % 2}"
                import os
                lndbg = int(os.environ.get("LNDBG", "0"))
                if lndbg == 1:
                    continue
                for c in range(8):
                    P.pe("transpose", ptt[:, c * 128:(c + 1) * 128], xnb[:, c * 128:(c + 1) * 128], identb[:],
                         r=["xnb", "identb"], w=[pk])
                if lndbg == 2:
                    continue
                for c in range(8):
                    dst = hT[:, c, t * 128:(t + 1) * 128]
                    src = ptt[:, c * 128:(c + 1) * 128]
                    if lndbg == 3 and c % 2 == 1:
                        continue
                    if lndbg == 4 and c % 2 == 0:
                        continue
                    if t % 2 == 0:
                        P.act("activation", out=dst, in_=src, func=AF.Identity,
                                                                           scale=sc[:, c:c + 1], bias=bi[:, c:c + 1],
                              r=[pk, key_sc], w=[("h", c, t)])
                    else:
                        P.dve("tensor_scalar", out=dst, in0=src, scalar1=sc[:, c:c + 1],
                                                                              scalar2=bi[:, c:c + 1], op0=ALU.mult,
                                                                              op1=ALU.add,
                              r=[pk, key_sc], w=[("h", c, t)])

        def wpiece(w_ap, l, c0, ncol, nk=8):
            return w_ap[l].rearrange("(k p) n -> p k n", p=128)[:, :, c0:c0 + ncol]

        def dump(name, src, keys):
            if name in dbg_out:
                P.dma("sp", dbg_out[name], src, r=keys)

        for l in range(L):
            xsrc = x_in if l == 0 else xlay[l - 1]
            xdst = y_out if l == L - 1 else xlay[l]
            lam_init = 0.8 - 0.6 * math.exp(-0.3 * l)
            for kc in range(8):
                load_w(wm[:, kc, :], w_mod[l, kc * 128:(kc + 1) * 128, :], ("wm", kc))
            P.dma("sp", bmt, b_modT[l], w=["bmt"])
            P.dma("sp", nmx, nmixT[l], w=["nmx"])
            P.dma("sp", nfx, nffnT[l], w=["nfx"])
            P.dma("sp", gcols, gcols_in[l], w=["gcols"])
            P.dma("sp", mon_bc, mon_in[l].partition_broadcast(128), w=["mon_bc"])
            P.dma("sp", subln_bc, subln_in[l].partition_broadcast(128), w=["subln_bc"])
            P.dma("sp", scrF[:, 0:256], dlam_in[l].partition_broadcast(128), w=["dl"])
            for j in range(48):
                for kc in range(8):
                    P.pe("matmul", pb[0][:, j:j + 1], lhsT=wm[:, kc, j * 128:(j + 1) * 128],
                                                        rhs=condb[:, kc:kc + 1], start=(kc == 0), stop=(kc == 7),
                         r=[("wm", kc), "condb"], w=["pb0"])
            P.dve("tensor_tensor", out=modT, in0=pb[0][:, 0:48], in1=bmt, op=ALU.add, r=["pb0", "bmt"], w=["modT"])
            P.dve("scalar_tensor_tensor", out=sc1, in0=modT[:, 8:16], scalar=1.0, in1=nmx, op0=ALU.add, op1=ALU.mult,
                  r=["modT", "nmx"], w=["sc1"])
            P.dve("tensor_copy", out=bi1, in_=modT[:, 0:8], r=["modT"], w=["sc1"])
            P.dve("scalar_tensor_tensor", out=sc2, in0=modT[:, 32:40], scalar=1.0, in1=nfx, op0=ALU.add, op1=ALU.mult,
                  r=["modT", "nfx"], w=["sc2"])
            P.dve("tensor_copy", out=bi2, in_=modT[:, 24:32], r=["modT"], w=["sc2"])
            for gi, (gt, jj) in enumerate(((gate_a, 2), (gate_f, 5))):
                for half in range(2):
                    c0 = jj * D + half * 512
                    pbi = 1 + half
                    bb = scrF[:, 512 + half * 512:1024 + half * 512]
                    P.dma("sp", bb, b_mod[l][:, c0:c0 + 512].partition_broadcast(128), w=[("bb", half)])
                    for kc in range(8):
                        P.pe("matmul", pb[pbi][:], lhsT=cond_rep[:, kc, :],
                                                                      rhs=wm[:, kc, c0:c0 + 512], start=(kc == 0),
                                                                      stop=(kc == 7),
                             r=[("wm", kc), "cond_rep"], w=[f"pb{pbi}"])
                    P.dve("tensor_tensor",
                        out=gt[:, half * 512:(half + 1) * 512], in0=pb[pbi][:], in1=bb, op=ALU.add,
                        r=[f"pb{pbi}", ("bb", half)], w=[("gate", gi, half)])
            dl = scrF[:, 0:256]
            pr = scrF[:, 256:384]
            P.dve("tensor_tensor", out=pr[:, 0:64], in0=dl[:, 0:64], in1=dl[:, 64:128], op=ALU.mult, r=["dl"], w=["pr"])
            P.dve("tensor_tensor", out=pr[:, 64:128], in0=dl[:, 128:192], in1=dl[:, 192:256], op=ALU.mult,
                  r=["dl"], w=["pr"])
            P.dve("tensor_reduce", out=lamt[:, 0:2], in_=pr.rearrange("p (a b) -> p a b", a=2), axis=AX.X,
                                            op=ALU.add, r=["pr"], w=["lamt"])
            P.act("activation", out=lamt[:, 2:4], in_=lamt[:, 0:2], func=AF.Exp, r=["lamt"], w=["lamt"])
            P.dve("tensor_tensor", out=lamt[:, 4:5], in0=lamt[:, 2:3], in1=lamt[:, 3:4], op=ALU.subtract,
                  r=["lamt"], w=["lamt"])
            P.dve("tensor_scalar", out=lamcol, in0=lamt[:, 4:5], scalar1=float(lam_init), scalar2=None, op0=ALU.add,
                  r=["lamt"], w=["lamcol"])
            P.dve("tensor_scalar", out=subln_bc, in0=subln_bc, scalar1=float(1.0 - lam_init), scalar2=None,
                                            op0=ALU.mult, r=["subln_bc"], w=["subln_bc"])
            P.barrier("M")
            ln_phase(xsrc, sc1, bi1, "sc1")
            P.barrier("A")
            dump(f"d_hT{l}", R1[:], [])
            P.pool("memset", Vm[:, :, :, 64:65], 1.0, w=["Vones"])
            P.pool("memset", Vd[:, :, :, 128:129], 1.0, w=["Vones"])
            pieces = [(0, 0, 0), (512, 4, 1), (1536, 8, 2), (2048, 12, 3)]
            allcols = [0, 512, 1536, 2048, 1024, 2560]
            wAs = [WS[:, sl * 4096:(sl + 1) * 4096].rearrange("p (k n) -> p k n", k=8) for sl in range(2)]

            def load_piece(idx):
                if idx < len(allcols):
                    load_w(wAs[idx % 2], wpiece(w_in, l, allcols[idx], 512), ("wA", idx % 2))
            load_piece(0)
            tile_i = 0
            for pi, (col0, cb, gi) in enumerate(pieces):
                wA = wAs[pi % 2]
                wkey = ("wA", pi % 2)
                load_piece(pi + 1)
                for cc in range(4):
                    for tg in range(NG):
                        par = tile_i % 2
                        tile_i += 1
                        pq, pss, psw = pb[par], pb[2 + par], pb[4 + par]
                        kq, ks, kw = f"pb{par}", f"pb{2 + par}", f"pb{4 + par}"
                        tok = slice(tg * 512, (tg + 1) * 512)
                        for kc in range(8):
                            P.pe("matmul",
                                pq[:], lhsT=wA[:, kc, cc * 128:(cc + 1) * 128], rhs=hT[:, kc, tok], start=(kc == 0),
                                stop=(kc == 7),
                                r=[wkey] + [("h", kc, 4 * tg + i) for i in range(4)], w=[kq])
                        sqb = scrB[:, par * 512:(par + 1) * 512]
                        qgb = scrB[:, 1024 + par * 512:1024 + (par + 1) * 512]
                        lnr = scrF[:, 0:512]
                        rstd = lnr
                        qgf = scrF[:, 512:1024]
                        t1 = scrF[:, 1024:1536]
                        t2 = scrF[:, 1536:2048]
                        gcol = gcols[:, gi:gi + 1]
                        P.act("activation", out=sqb, in_=pq[:], func=AF.Square, r=[kq], w=[("sqb", par)])
                        P.act("activation", out=qgf, in_=pq[:], func=AF.Copy, scale=gcol, r=[kq, "gcols"], w=["qgf"])
                        P.dve("tensor_copy", out=qgb, in_=qgf, r=["qgf"], w=[("qgb", par)])
                        P.pe("matmul", pss[:], lhsT=bonesb[:], rhs=sqb, start=True, stop=True,
                             r=["bonesb", ("sqb", par)], w=[ks])
                        P.pe("matmul", psw[:], lhsT=pmb[:], rhs=qgb, start=True, stop=True,
                             r=["pmb", ("qgb", par)], w=[kw])
                        P.act("activation", out=lnr, in_=pss[:], func=AF.Ln, bias=eps_col, r=[ks, "cst"], w=["lnr"])
                        P.act("activation", out=rstd, in_=lnr, func=AF.Exp, scale=-0.5, r=["lnr"], w=["lnr"])
                        P.dve("tensor_tensor", out=t1, in0=qgf, in1=ctab[:, tok], op=ALU.mult, r=["qgf", "tab"], w=["t1"])
                        P.dve("tensor_tensor", out=t2, in0=psw[:], in1=stab[:, tok], op=ALU.mult,
                              r=[kw, "tab"], w=["t2"])
                        P.pool("tensor_tensor", out=t1, in0=t1, in1=t2, op=ALU.add, r=["t1", "t2"], w=["t1"])
                        dst = qkT[:, cb + cc, tok]
                        P.dve("tensor_tensor", out=dst, in0=t1, in1=rstd, op=ALU.mult,
                              r=["t1", "lnr"], w=[("qk", cb + cc, tg)])
            for vi, (col0, isd) in enumerate(((1024, False), (2560, True))):
                wA = wAs[vi % 2]
                wkey = ("wA", vi % 2)
                load_piece(4 + vi + 1)
                for t in range(NT):
                    par = t % 2
                    pv = pb[par]
                    for kc in range(8):
                        P.pe("matmul",
                            pv[:], lhsT=hT[:, kc, t * 128:(t + 1) * 128], rhs=wA[:, kc, :], start=(kc == 0), stop=(kc == 7),
                            r=[wkey, ("h", kc, t)], w=[f"pb{par}"])
                    if isd:
                        dst = Vd[:, t, :, 0:128]
                        src = pv[:].rearrange("p (h e) -> p h e", h=4)
                    else:
                        dst = Vm[:, t, :, 0:64]
                        src = pv[:].rearrange("p (h e) -> p h e", h=8)
                    if t % 2 == 0:
                        P.act("activation", out=dst, in_=src, func=AF.Copy,
                              r=[f"pb{par}"], w=[("V", t, isd)])
                    else:
                        P.dve("tensor_copy", out=dst, in_=src, r=[f"pb{par}"], w=[("V", t, isd)])
            for c in range(4):
                P.dve("tensor_reduce", out=kmf[:, c, 0:NB], in_=qkT[:, 4 + c, :].rearrange("p (b t) -> p b t", t=256),
                                                     axis=AX.X, op=ALU.add,
                      r=[("qk", 4 + c, tg) for tg in range(NG)], w=["kmf"])
            P.dve("tensor_scalar", out=kmT[:, :, 0:NB], in0=kmf[:, :, 0:NB], scalar1=1.0 / 256, scalar2=None,
                                            op0=ALU.mult, r=["kmf"], w=["kmT"])
            P.barrier("B")
            dump(f"d_qk{l}", R2[:, 0:16 * S], [])
            dump(f"d_V{l}", R2[:, 16 * S:16 * S + NT * VW], [])
            oT = hT
            otok = scrB[:, 0:4096].rearrange("p (j f) -> p j f", j=4)
            pTb = [scrB[:, 4096 + i * 512:4096 + (i + 1) * 512] for i in range(4)]
            biasT = scrB[:, 6144:6656]
            biasq2 = scrB[:, 6656:6784].rearrange("p (d h n) -> p d h n", d=2, h=8)
            biasq = biasq2[:, 0]
            ocs = scrF[:, 0:1024].rearrange("p (j c e) -> p j c e", j=4, c=2)
            osm = scrF[:, 1024:1280]
            gs = scrF[:, 1280:1344].rearrange("p (h n) -> p h n", h=8)
            cnt = scrF[:, 1344:1408].rearrange("p (h n) -> p h n", h=8)
            cmpb = scrF[:, 1408:1920]
            junkf = scrF[:, 1920:2048]
            pt_i = 0
            st_i = 0
            for g in range(NG):
                masked_g = (2 * g + 1) >= 4
                import os
                atdbg = int(os.environ.get("ATDBG", "0"))
                if masked_g and atdbg in (2,):
                    P.dve("memset", biasT, 0.0, w=["biasT"])
                if masked_g and atdbg not in (2, 3):
                    for j in range(4):
                        qt = 4 * g + j
                        qb = qt // 2
                        for h in range(8):
                            base = (h % 2) * 64
                            P.pe("matmul", pb[h % 2][:, h * 8:h * 8 + NB],
                                 lhsT=qkT[base:base + 64, h // 2, qt * 128:(qt + 1) * 128],
                                 rhs=kmT[base:base + 64, h // 2, 0:NB], start=True, stop=True,
                                 r=[("qk", h // 2, g), "kmT"], w=[f"pb{h % 2}"])
                        for par2 in range(2):
                            P.dve("tensor_copy", out=gs[:, par2::2, :],
                                  in_=pb[par2][:, 0:64].rearrange("p (h n) -> p h n", h=8)[:, par2::2, :],
                                  r=[f"pb{par2}"], w=["gs"])
                        cmpv = cmpb[:, 0:8 * qb * qb].rearrange("p (h n m) -> p h n m", h=8, n=qb)
                        in0 = gs[:, :, 0:qb].unsqueeze(2).broadcast_to([128, 8, qb, qb])
                        in1 = gs[:, :, 0:qb].unsqueeze(3).broadcast_to([128, 8, qb, qb])
                        P.dve("tensor_tensor", out=cmpv, in0=in0, in1=in1, op=ALU.is_gt,
                              r=["gs"], w=["cmpb"])
                        P.dve("tensor_reduce", out=cnt[:, :, 0:qb], in_=cmpv, axis=AX.X, op=ALU.add,
                              r=["cmpb"], w=["cnt"])
                        P.dve("memset", scrB[:, 6656:6784], 0.0, w=["biasq"])
                        for d2 in range(2):
                            P.dve("tensor_scalar", out=biasq2[:, d2, :, 0:qb], in0=cnt[:, :, 0:qb], scalar1=2.5,
                                  scalar2=-BIG, op0=ALU.is_gt, op1=ALU.mult, r=["cnt"], w=["biasq"])
                        P.pe("transpose", pt[0][:, 0:128], scrB[:, 6656:6784], identb[:],
                             r=["biasq", "identb"], w=["pt0"])
                        P.dve("tensor_copy", out=biasT[:, j * 128:(j + 1) * 128], in_=pt[0][:, 0:128],
                              r=["pt0"], w=["biasT"])
                for hp in range(16):
                    isd = hp >= 8
                    hh = hp - 8 if isd else hp
                    base = (hh % 2) * 64
                    qc = (8 if isd else 0) + hh // 2
                    kc_ = (12 if isd else 4) + hh // 2
                    if isd:
                        vh, comp, dv = hh // 2, hh % 2, 128
                    else:
                        vh, comp, dv = hh, 0, 64
                    nkt = 4 * g + 4
                    for kt in range(nkt):
                        i = kt - 4 * g
                        c0 = 128 * max(i, 0)
                        ncol = 512 - c0
                        sp_ = st_i % 2
                        st_i += 1
                        ps_ = pb[sp_]
                        pk = f"pb{sp_}"
                        mask_here = (not isd) and masked_g and (kt // 2) < (2 * g + 1)
                        P.pe("matmul", ps_[:, 0:ncol], lhsT=qkT[base:base + 64, kc_, kt * 128:(kt + 1) * 128],
                                      rhs=qkT[base:base + 64, qc, g * 512 + c0:(g + 1) * 512], start=True, stop=not mask_here,
                             r=[("qk", kc_, kt // 4), ("qk", qc, g)], w=[pk])
                        if mask_here and atdbg not in (1, 3):
                            n = kt // 2
                            col = base + 8 * hh + n
                            P.pe("matmul", ps_[:, 0:ncol],
                                 lhsT=identb[base:base + 64, col:col + 1].broadcast_to([64, 128]),
                                 rhs=biasT[base:base + 64, c0:512], start=False, stop=True,
                                 r=["identb", "biasT"], w=[pk])
                        pT = pTb[pt_i % 4]
                        ptk = ("pT", pt_i % 4)
                        pt_i += 1
                        P.act("activation", out=pT[:, 0:ncol], in_=ps_[:, 0:ncol],
                                                                                func=AF.Exp, scale=0.125,
                              r=[pk], w=[ptk])
                        if i >= 0:
                            P.dve("tensor_tensor", out=pT[:, 0:128], in0=pT[:, 0:128], in1=trimask[:], op=ALU.mult,
                                  r=[ptk, "trimask"], w=[ptk])
                        for j in range(max(i, 0), 4):
                            off = (j - max(i, 0)) * 128
                            if isd:
                                vr = Vd[:, kt, vh, :]
                            else:
                                vr = Vm[:, kt, vh, :]
                            P.pe("matmul",
                                pb[2 + j][:, 0:dv + 1], lhsT=pT[:, off:off + 128], rhs=vr, start=(kt == 0),
                                stop=(kt == 4 * g + j),
                                r=[ptk, ("V", kt, isd)], w=[f"pb{2 + j}"])
                    for j in range(4):
                        acc = pb[2 + j]
                        ak = f"pb{2 + j}"
                        rd = tmpc[:, 2:3]
                        P.dve("reciprocal", out=rd, in_=acc[:, dv:dv + 1], r=[ak], w=["rd"])
                        if not isd:
                            o = osm[:, 0:64]
                            P.dve("tensor_scalar", out=o, in0=acc[:, 0:64], scalar1=rd, scalar2=None,
                                                                          op0=ALU.mult, r=[ak, "rd"], w=["osm"])
                            ss = tmpc[:, 3:4]
                            P.act("activation", out=junkf[:, 0:64], in_=o, func=AF.Square, accum_out=ss,
                                  r=["osm"], w=["junkf", "ss2"])
                            rs = tmpc[:, 4:5]
                            rstd_from(ss, 64, rs, "ss2")
                            dst = otok[:, j, hh * 64:(hh + 1) * 64]
                            P.dve("scalar_tensor_tensor", out=dst, in0=o, scalar=rs, in1=mon_bc,
                                                                               op0=ALU.mult, op1=ALU.mult,
                                  r=["osm", "ss2r", "mon_bc"], w=[("otok", j)])
                        else:
                            if comp == 1:
                                P.dve("tensor_tensor", out=rd, in0=rd, in1=lamcol, op=ALU.mult, r=["rd", "lamcol"], w=["rd"])
                            oc = ocs[:, j, comp, :]
                            P.dve("tensor_scalar", out=oc, in0=acc[:, 0:128], scalar1=rd, scalar2=None,
                                                                            op0=ALU.mult, r=[ak, "rd"], w=[("ocs", j, comp)])
                            if comp == 1:
                                o = osm[:, 0:128]
                                P.dve("tensor_tensor", out=o, in0=ocs[:, j, 0, :], in1=ocs[:, j, 1, :],
                                                                          op=ALU.subtract,
                                      r=[("ocs", j, 0), ("ocs", j, 1)], w=["osm"])
                                ss = tmpc[:, 3:4]
                                P.act("activation", out=junkf, in_=o, func=AF.Square, accum_out=ss,
                                      r=["osm"], w=["junkf", "ss2"])
                                rs = tmpc[:, 4:5]
                                rstd_from(ss, 128, rs, "ss2")
                                dst = otok[:, j, 512 + vh * 128:512 + (vh + 1) * 128]
                                P.dve("scalar_tensor_tensor", out=dst, in0=o, scalar=rs, in1=subln_bc,
                                                                                   op0=ALU.mult, op1=ALU.mult,
                                      r=["osm", "ss2r", "subln_bc"], w=[("otok", j)])
                for j in range(4):
                    t = 4 * g + j
                    for c in range(8):
                        P.pe("transpose", pt[1][:, c * 128:(c + 1) * 128], otok[:, j, c * 128:(c + 1) * 128],
                                                             identb[:], r=[("otok", j), "identb"], w=["pt1"])
                    P.act("activation", out=oT[:, :, t * 128:(t + 1) * 128],
                                                      in_=pt[1][:].rearrange("p (c s) -> p c s", c=8), func=AF.Copy,
                          r=["pt1"], w=[("o", t)])
            P.barrier("C")
            dump(f"d_oT{l}", R1[:], [])
            wO = [WS[:, hf * 4096:(hf + 1) * 4096].rearrange("p (k n) -> p k n", k=8) for hf in range(2)]
            for hf in range(2):
                load_w(wO[hf], wpiece(w_out, l, hf * 512, 512), ("wA", hf))
            xt = scrF[:, 0:1024]
            ytmp = scrF[:, 1024:2048]
            for t in range(NT):
                P.dma("sp", xt, xsrc[t * 128:(t + 1) * 128, :], w=["xt"])
                for hf in range(2):
                    py = pb[hf]
                    for kc in range(8):
                        P.pe("matmul",
                            py[:], lhsT=oT[:, kc, t * 128:(t + 1) * 128], rhs=wO[hf][:, kc, :], start=(kc == 0), stop=(kc == 7),
                            r=[("wA", hf), ("o", t)], w=[f"pb{hf}"])
                    P.dve("tensor_tensor", out=ytmp[:, hf * 512:(hf + 1) * 512], in0=py[:],
                                                                  in1=gate_a[:, hf * 512:(hf + 1) * 512], op=ALU.mult,
                          r=[f"pb{hf}", ("gate", 0, hf)], w=[("ytmp", hf)])
                    P.pool("tensor_tensor", out=xt[:, hf * 512:(hf + 1) * 512], in0=xt[:, hf * 512:(hf + 1) * 512],
                                                            in1=ytmp[:, hf * 512:(hf + 1) * 512], op=ALU.add,
                           r=["xt", ("ytmp", hf)], w=["xt"])
                P.dma("sp", xmid[l][t * 128:(t + 1) * 128, :], xt, r=["xt"], w=[("xm", t)])
            P.barrier("D")
            ln_phase(xmid[l], sc2, bi2, "sc2")
            P.barrier("E")
            npiece = DFF // 256
            def wgu(slot):
                return (WS[:, slot * 4096:slot * 4096 + 2048].rearrange("p (k n) -> p k n", k=8),
                        WS[:, slot * 4096 + 2048:slot * 4096 + 4096].rearrange("p (k n) -> p k n", k=8))

            def load_gu(pi):
                if pi < npiece:
                    a, b = wgu(pi % 2)
                    load_w(a, wpiece(w_gate, l, pi * 256, 256), ("wg", pi % 2))
                    load_w(b, wpiece(w_up, l, pi * 256, 256), ("wu", pi % 2))
            load_gu(0)
            for pi in range(npiece):
                slot = pi % 2
                wg, wu = wgu(slot)
                load_gu(pi + 1)
                for cc in range(2):
                    jf = 2 * pi + cc
                    for tg in range(NG):
                        par = (jf * NG + tg) % 2
                        pg, pu = pb[par], pb[2 + par]
                        tok = slice(tg * 512, (tg + 1) * 512)
                        for kc in range(8):
                            P.pe("matmul",
                                pg[:], lhsT=wg[:, kc, cc * 128:(cc + 1) * 128], rhs=hT[:, kc, tok], start=(kc == 0), stop=(kc == 7),
                                r=[("wg", slot)] + [("h", kc, 4 * tg + i) for i in range(4)], w=[f"pb{par}"])
                        for kc in range(8):
                            P.pe("matmul",
                                pu[:], lhsT=wu[:, kc, cc * 128:(cc + 1) * 128], rhs=hT[:, kc, tok], start=(kc == 0), stop=(kc == 7),
                                r=[("wu", slot)] + [("h", kc, 4 * tg + i) for i in range(4)], w=[f"pb{2 + par}"])
                        sg = scrB[:, par * 512:(par + 1) * 512]
                        P.act("activation", out=sg, in_=pg[:], func=AF.Silu, r=[f"pb{par}"], w=[("sg", par)])
                        P.dve("tensor_tensor", out=actT[:, jf, tok], in0=pu[:], in1=sg, op=ALU.mult,
                              r=[f"pb{2 + par}", ("sg", par)], w=[("a", jf, tg)])
            P.barrier("F")
            wD = R1[:, 0:NFF * 512].rearrange("p (j n) -> p j n", j=NFF)
            for hf in range(2):
                load_w(wD, w_down[l].rearrange("(j p) n -> p j n", p=128)[:, :, hf * 512:(hf + 1) * 512], "wD")
                for t in range(NT):
                    xh = scrF[:, (t % 2) * 512:(t % 2 + 1) * 512]
                    yh = scrF[:, 1024 + (t % 2) * 512:1024 + (t % 2 + 1) * 512]
                    par = t % 2
                    P.dma("sp", xh, xmid[l][t * 128:(t + 1) * 128, hf * 512:(hf + 1) * 512], r=[("xm", t)], w=[("xh", par)])
                    py = pb[par]
                    for jf in range(NFF):
                        P.pe("matmul",
                            py[:], lhsT=actT[:, jf, t * 128:(t + 1) * 128], rhs=wD[:, jf, :], start=(jf == 0), stop=(jf == NFF - 1),
                            r=["wD", ("a", jf, t // 4)], w=[f"pb{par}"])
                    P.dve("tensor_tensor", out=yh, in0=py[:], in1=gate_f[:, hf * 512:(hf + 1) * 512],
                                                                         op=ALU.mult,
                          r=[f"pb{par}", ("gate", 1, hf)], w=[("yh", par)])
                    P.pool("tensor_tensor", out=xh, in0=xh, in1=yh, op=ALU.add,
                           r=[("xh", par), ("yh", par)], w=[("xh", par)])
                    P.dma("sp", xdst[t * 128:(t + 1) * 128, hf * 512:(hf + 1) * 512], xh, r=[("xh", par)], w=[("xo", t, hf)])
            P.barrier("G")

        if stop:
            P.ops = P.ops[:P.marks[stop]]
        with nc.Block() as block:
            bodies = P.bodies(sems)
            block.sync(bodies["sp"])
            block.tensor(bodies["pe"])
            block.scalar(bodies["act"])
            block.vector(bodies["dve"])
            block.gpsimd(bodies["pool"])
    return nc


def make_in_maps(inputs, S=2048, L=2, cores=8):
    f32 = np.float32
    x = np.asarray(inputs["x"], f32)
    c = np.asarray(inputs["c"], f32)
    pos = np.asarray(inputs["positions"], np.int32)
    cst, pm, bones = host_consts()

    def colsT(v, n):
        v = np.asarray(v, f32)
        return np.ascontiguousarray(v.reshape(v.shape[0], n, 128).transpose(0, 2, 1))

    def tile2(v):
        v = np.asarray(v, f32)
        return np.concatenate([v, v], axis=1)

    gcols = np.stack([tile2(inputs["moba_q_norm"]), tile2(inputs["moba_k_norm"]),
                      tile2(inputs["diff_q_norm"]), tile2(inputs["diff_k_norm"])], axis=2)
    shared = {
        "w_mod": np.ascontiguousarray(np.asarray(inputs["w_mod"], f32)[:L]),
        "b_modT": colsT(np.asarray(inputs["b_mod"])[:L], 48),
        "b_mod": np.ascontiguousarray(np.asarray(inputs["b_mod"], f32)[:L, None, :]),
        "nmixT": colsT(np.asarray(inputs["norm_mix"])[:L], 8),
        "nffnT": colsT(np.asarray(inputs["norm_ffn"])[:L], 8),
        "w_in": np.ascontiguousarray(np.asarray(inputs["w_in"], f32)[:L]),
        "gcols": np.ascontiguousarray(gcols[:L]),
        "mon": np.ascontiguousarray(np.asarray(inputs["moba_out_norm"], f32)[:L, None, :]),
        "subln": np.ascontiguousarray(np.asarray(inputs["diff_subln"], f32)[:L, None, :]),
        "dlam": np.ascontiguousarray(np.asarray(inputs["diff_lambda"], f32)[:L].reshape(L, 1, 256)),
        "w_out": np.ascontiguousarray(np.asarray(inputs["w_out"], f32)[:L]),
        "w_gate": np.ascontiguousarray(np.asarray(inputs["w_gate"], f32)[:L]),
        "w_up": np.ascontiguousarray(np.asarray(inputs["w_up"], f32)[:L]),
        "w_down": np.ascontiguousarray(np.asarray(inputs["w_down"], f32)[:L]),
        "cst": cst, "pmat": pm, "bones": bones,
    }
    maps = []
    for b in range(cores):
        m = dict(shared)
        m["x"] = np.ascontiguousarray(x[b, :S])
        m["cT"] = np.ascontiguousarray(c[b].reshape(8, 128).T)
        m["pos"] = np.ascontiguousarray(pos[b:b + 1, :S])
        maps.append(m)
    return maps


def kernel(**inputs):
    S, L = 2048, 2
    nc = build(S, L)
    maps = make_in_maps(inputs, S, L, 8)
    res = run_bass_kernel_spmd(nc, maps, core_ids=list(range(8)))
    return np.stack([np.asarray(r["y"], np.float32) for r in res.results], axis=0)
```

```python
import math
from contextlib import ExitStack
import numpy as np
import concourse.bass as bass
import concourse.mybir as mybir
from concourse.bass_utils import run_bass_kernel_spmd

F32 = mybir.dt.float32
BF16 = mybir.dt.bfloat16
I32 = mybir.dt.int32
ALU = mybir.AluOpType
AF = mybir.ActivationFunctionType
AX = mybir.AxisListType

D = 1024
DFF = 2816
NFF = DFF // 128
EPS = 1e-6
BIG = 30000.0
NDMA = {"sp": 20, "pool": 12}
COMPUTE = ("pe", "act", "dve", "pool")


class Op:
    __slots__ = ("eng", "fn", "reads", "writes", "dma", "waits", "ev")

    def __init__(self, eng, fn, reads, writes, dma):
        self.eng, self.fn, self.reads, self.writes, self.dma = eng, fn, reads, writes, dma
        self.waits = []
        self.ev = None


class Prog:
    def __init__(self):
        self.ops = []
        self.marks = {}

    def add(self, eng, name, args, kwargs, reads=(), writes=(), dma=False):
        def fn(e, name=name, args=args, kwargs=kwargs):
            return getattr(e, name)(*args, **kwargs)
        self.ops.append(Op(eng, fn, tuple(reads), tuple(writes), dma))

    def pe(self, name, *args, r=(), w=(), **kw):
        self.add("pe", name, args, kw, r, w)

    def act(self, name, *args, r=(), w=(), **kw):
        self.add("act", name, args, kw, r, w)

    def dve(self, name, *args, r=(), w=(), **kw):
        self.add("dve", name, args, kw, r, w)

    def pool(self, name, *args, r=(), w=(), **kw):
        self.add("pool", name, args, kw, r, w)

    def dma(self, q, out, in_, r=(), w=()):
        self.add(q, "dma_start", (), dict(out=out, in_=in_), r, w, dma=True)

    def barrier(self, mark=None):
        self.ops.append("BARRIER")
        if mark:
            self.marks[mark] = len(self.ops)

    def analyze(self):
        cnt = {e: 0 for e in COMPUTE}
        dcnt = {q: 0 for q in NDMA}
        last_w, readers = {}, {}
        seen = {e: {} for e in ("pe", "act", "dve", "pool", "sp")}
        all_events = {}

        def need(op, ev):
            if ev is None:
                return
            key, val = ev
            if key == "pe" and op.eng == "pe" and not op.dma:
                return
            s = seen[op.eng]
            if s.get(key, 0) >= val:
                return
            s[key] = val
            op.waits.append((key, val))

        out = []
        for op in self.ops:
            if op == "BARRIER":
                b = Op("all", None, (), (), False)
                b.waits = dict(all_events)
                for e in seen:
                    for k, v in all_events.items():
                        if seen[e].get(k, 0) < v:
                            seen[e][k] = v
                out.append(b)
                continue
            if op.dma:
                q = op.eng
                i = dcnt[q]
                dcnt[q] += 1
                key = ("dma", q, i % NDMA[q])
                val = 16 * (i // NDMA[q] + 1)
                if val > 16:
                    need(op, (key, val - 16))
                op.ev = (key, val)
            else:
                cnt[op.eng] += 1
                op.ev = (op.eng, cnt[op.eng])
            for r in op.reads:
                need(op, last_w.get(r))
            for w in op.writes:
                need(op, last_w.get(w))
                for ev in readers.get(w, ()):
                    need(op, ev)
            for r in op.reads:
                readers.setdefault(r, []).append(op.ev)
            for w in op.writes:
                last_w[w] = op.ev
                readers[w] = []
            all_events[op.ev[0]] = max(all_events.get(op.ev[0], 0), op.ev[1])
            out.append(op)
        self.sched = out
        self.final_events = dict(all_events)

    def bodies(self, sems):
        self.analyze()
        sched, final_events = self.sched, self.final_events

        def body_for(ename):
            def body(eng):
                seen = {}
                for op in sched:
                    if op.eng == "all":
                        for k, v in op.waits.items():
                            if k != ename and seen.get(k, 0) < v:
                                eng.wait_ge(sems[k], v)
                                seen[k] = v
                        continue
                    if op.eng != ename:
                        continue
                    for k, v in op.waits:
                        eng.wait_ge(sems[k], v)
                        seen[k] = max(seen.get(k, 0), v)
                    inst = op.fn(eng)
                    inst.then_inc(sems[op.ev[0]], 16 if op.dma else 1)
                if ename == "sp":
                    for k, v in final_events.items():
                        if seen.get(k, 0) < v:
                            eng.wait_ge(sems[k], v)
            return body

        return {e: body_for(e) for e in ("pe", "act", "dve", "pool", "sp")}


def all_sem_keys():
    keys = list(COMPUTE)
    for q, n in NDMA.items():
        keys += [("dma", q, s) for s in range(n)]
    return keys


def host_consts():
    f = np.arange(128)
    d = f % 64
    inv = (500000.0 ** (-np.arange(0, 16, 2, dtype=np.float32) / 16.0)).astype(np.float32)
    cst = np.zeros((128, 8), np.float32)
    cst[:, 0] = np.where(d < 16, inv[d % 8], 0.0)
    cst[:, 1] = np.where(d < 8, -1.0, np.where(d < 16, 1.0, 0.0))
    cst[:, 2] = EPS
    cst[:, 3] = 1.0
    pm = np.zeros((128, 128), np.float32)
    for m in range(128):
        dm = m % 64
        if dm < 8:
            pm[m + 8, m] = 1.0
        elif dm < 16:
            pm[m - 8, m] = 1.0
    bones = np.zeros((128, 128), np.float32)
    bones[:64, :64] = 1.0 / 64
    bones[64:, 64:] = 1.0 / 64
    return cst, pm, bones


def build(S=2048, L=2, dbg=(), stop=None):
    NT, NG, NB = S // 128, S // 512, S // 256
    VW = 8 * 65 + 4 * 129
    nc = bass.Bass("TRN2", target_bir_lowering=False)

    def din(name, shape, dt=F32):
        return nc.dram_tensor(name, list(shape), dt, kind="ExternalInput").ap()

    x_in = din("x", [S, D])
    cT_in = din("cT", [128, 8])
    pos_in = din("pos", [1, S], I32)
    w_mod = din("w_mod", [L, D, 6 * D])
    b_modT = din("b_modT", [L, 128, 48])
    b_mod = din("b_mod", [L, 1, 6 * D])
    nmixT = din("nmixT", [L, 128, 8])
    nffnT = din("nffnT", [L, 128, 8])
    w_in = din("w_in", [L, D, 3072])
    gcols_in = din("gcols", [L, 128, 4])
    mon_in = din("mon", [L, 1, 64])
    subln_in = din("subln", [L, 1, 128])
    dlam_in = din("dlam", [L, 1, 256])
    w_out = din("w_out", [L, D, D])
    w_gate = din("w_gate", [L, D, DFF])
    w_up = din("w_up", [L, D, DFF])
    w_down = din("w_down", [L, DFF, D])
    cst_in = din("cst", [128, 8])
    pm_in = din("pmat", [128, 128])
    bones_in = din("bones", [128, 128])
    y_out = nc.dram_tensor("y", [S, D], F32, kind="ExternalOutput").ap()
    xmid = [nc.dram_tensor(f"xmid{l}", [S, D], F32, kind="Internal").ap() for l in range(L)]
    xlay = [nc.dram_tensor(f"xlay{l}", [S, D], F32, kind="Internal").ap() for l in range(L - 1)]
    dbg_out = {}
    for name, shape, dt in dbg:
        dbg_out[name] = nc.dram_tensor(name, list(shape), dt, kind="ExternalOutput").ap()

    es = ExitStack()
    with es:
        def sb(name, shape, dt):
            return es.enter_context(nc.sbuf_tensor(name, list(shape), dt))

        def psum(name, shape, dt):
            return es.enter_context(nc.psum_tensor(name, list(shape), dt))

        R1 = sb("R1", [128, max(8 * S, NFF * 512)], BF16)
        R2N = max(16 * S + NT * VW, NFF * S, 8 * 6144 if S >= 512 else 0)
        R2 = sb("R2", [128, R2N], BF16)
        ctab = sb("ctab", [128, S], F32)
        stab = sb("stab", [128, S], F32)
        gate_a = sb("gate_a", [128, D], F32)
        gate_f = sb("gate_f", [128, D], F32)
        scrF = sb("scrF", [128, 3072], F32)
        scrB = sb("scrB", [128, 8192], BF16)
        WS = sb("WS", [128, 8192], BF16)
        identb = sb("identb", [128, 128], BF16)
        trimask = sb("trimask", [128, 128], BF16)
        pmb = sb("pmb", [128, 128], BF16)
        bonesb = sb("bonesb", [128, 128], BF16)
        cst = sb("cst_sb", [128, 8], F32)
        small = sb("small", [128, 512], F32)
        cond_rep = sb("cond_rep", [128, 8, 128], BF16)
        condb = sb("condb", [128, 8], BF16)
        kmT = sb("kmT", [128, 4, 8], BF16)
        pb = [psum(f"pb{i}", [128, 512], F32) for i in range(6)]
        pt = [psum(f"pt{i}", [128, 1024], BF16) for i in range(2)]
        sems = {}
        for k in all_sem_keys():
            nm = "s_" + ("_".join(map(str, k)) if isinstance(k, tuple) else k)
            sems[k] = es.enter_context(nc.semaphore(nm))

        P = Prog()
        hT = R1[:, 0:8 * S].rearrange("p (c s) -> p c s", c=8)
        qkT = R2[:, 0:16 * S].rearrange("p (c s) -> p c s", c=16)
        Vall = R2[:, 16 * S:16 * S + NT * VW].rearrange("p (t w) -> p t w", t=NT)
        Vm = Vall[:, :, 0:520].rearrange("p t (h e) -> p t h e", h=8)
        Vd = Vall[:, :, 520:VW].rearrange("p t (h e) -> p t h e", h=4)
        actT = R2[:, 0:NFF * S].rearrange("p (j s) -> p j s", j=NFF)
        wm = R2[:, 0:8 * 6144].rearrange("p (k n) -> p k n", k=8)
        modT = small[:, 0:48]
        sc1, bi1, sc2, bi2 = small[:, 48:56], small[:, 56:64], small[:, 64:72], small[:, 72:80]
        gcols = small[:, 80:84]
        lamcol = small[:, 84:85]
        tmpc = small[:, 85:96]
        nmx = small[:, 96:104]
        nfx = small[:, 104:112]
        bmt = small[:, 112:160]
        mon_bc = small[:, 160:224]
        subln_bc = small[:, 224:352]
        cond = small[:, 352:360]
        ctf = small[:, 360:368]
        lamt = small[:, 368:400]
        kmf = small[:, 400:432].rearrange("p (c n) -> p c n", c=4)
        eps_col = cst[:, 2:3]

        P.dma("sp", cst[:], cst_in, w=["cst"])
        P.dma("sp", ctf, cT_in, w=["ctf"])
        P.dma("pool", pmb[:], pm_in, w=["pmb"])
        P.dma("pool", bonesb[:], bones_in, w=["bonesb"])
        idf = scrF[:, 0:128]
        P.pool("memset", idf, 1.0, w=["idf"])
        P.pool("affine_select", out=idf, in_=idf, pattern=[[-1, 128]], compare_op=ALU.is_equal, fill=0.0,
                                         base=0, channel_multiplier=1, r=["idf"], w=["idf"])
        P.dve("tensor_copy", out=identb[:], in_=idf, r=["idf"], w=["identb"])
        trf = scrF[:, 128:256]
        P.pool("memset", trf, 1.0, w=["trf"])
        P.pool("affine_select", out=trf, in_=trf, pattern=[[1, 128]], compare_op=ALU.is_ge, fill=0.0,
                                         base=0, channel_multiplier=-1, r=["trf"], w=["trf"])
        P.dve("tensor_copy", out=trimask[:], in_=trf, r=["trf"], w=["trimask"])
        P.act("activation", out=cond, in_=ctf, func=AF.Silu, r=["ctf"], w=["cond"])
        P.dve("tensor_copy", out=condb[:], in_=cond, r=["cond"], w=["condb"])
        P.dve("tensor_copy", out=cond_rep[:], in_=cond.unsqueeze(2).broadcast_to([128, 8, 128]),
              r=["cond"], w=["cond_rep"])
        R1f = R1[:, 0:8 * S].bitcast(F32)
        tmpa = R1f[:, 0:S]
        tmpk = R1f[:, S:2 * S]
        tmpki = R1[:, 0:8 * S].bitcast(I32)[:, 2 * S:3 * S]
        tmpkf = R1f[:, 3 * S:4 * S]
        P.dma("sp", ctab[:].bitcast(I32), pos_in.partition_broadcast(128), w=["ctab"])
        P.dve("tensor_copy", out=stab[:], in_=ctab[:].bitcast(I32), r=["ctab"], w=["stab"])
        P.dve("tensor_scalar", out=stab[:], in0=stab[:], scalar1=cst[:, 0:1], scalar2=None, op0=ALU.mult,
              r=["stab", "cst"], w=["stab"])

        def range_reduce_sin(dst, shift, scale_ap):
            P.dve("tensor_scalar", out=tmpa, in0=stab[:], scalar1=float(shift), scalar2=None, op0=ALU.add,
                  r=["stab"], w=["tmpa"])
            P.dve("tensor_scalar", out=tmpk, in0=tmpa, scalar1=float(1.0 / (2 * math.pi)), scalar2=None,
                                            op0=ALU.mult, r=["tmpa"], w=["tmpk"])
            P.dve("tensor_copy", out=tmpki, in_=tmpk, r=["tmpk"], w=["tmpki"])
            P.dve("tensor_copy", out=tmpkf, in_=tmpki, r=["tmpki"], w=["tmpkf"])
            P.dve("scalar_tensor_tensor", out=tmpa, in0=tmpkf, scalar=float(-2 * math.pi), in1=tmpa,
                                                   op0=ALU.mult, op1=ALU.add, r=["tmpkf", "tmpa"], w=["tmpa"])
            P.dve("tensor_scalar", out=tmpk, in0=tmpa, scalar1=float(math.pi), scalar2=float(-2 * math.pi),
                                            op0=ALU.is_gt, op1=ALU.mult, r=["tmpa"], w=["tmpk"])
            P.dve("tensor_tensor", out=tmpa, in0=tmpa, in1=tmpk, op=ALU.add, r=["tmpa", "tmpk"], w=["tmpa"])
            P.dve("tensor_scalar", out=tmpk, in0=tmpa, scalar1=float(-math.pi), scalar2=float(2 * math.pi),
                                            op0=ALU.is_lt, op1=ALU.mult, r=["tmpa"], w=["tmpk"])
            P.dve("tensor_tensor", out=tmpa, in0=tmpa, in1=tmpk, op=ALU.add, r=["tmpa", "tmpk"], w=["tmpa"])
            if scale_ap is None:
                P.act("activation", out=dst, in_=tmpa, func=AF.Sin, r=["tmpa"], w=[dst.name if False else "tab"])
            else:
                P.act("activation", out=dst, in_=tmpa, func=AF.Sin, scale=scale_ap, r=["tmpa", "cst"], w=["tab"])

        range_reduce_sin(ctab[:], math.pi / 2, None)
        P.barrier("setup1")
        range_reduce_sin(stab[:], 0.0, cst[:, 1:2])
        P.barrier("setup2")

        def load_w(dst, src, key):
            P.dma("pool", dst, src, w=[key])

        def rstd_from(ss, n, out_col, tag):
            l1 = tmpc[:, 10:11]
            P.act("activation", out=l1, in_=ss, func=AF.Ln, scale=1.0 / n, bias=eps_col, r=[tag, "cst"], w=["l1"])
            P.act("activation", out=out_col, in_=l1, func=AF.Exp, scale=-0.5, r=["l1"], w=[tag + "r"])

        def ln_phase(xsrc, sc, bi, key_sc):
            xt = scrF[:, 0:1024]
            xnb = scrB[:, 0:1024]
            junk = scrB[:, 1024:2048]
            for t in range(NT):
                P.dma("sp", xt, xsrc[t * 128:(t + 1) * 128, :], w=["xt"])
                ss = tmpc[:, 0:1]
                P.act("activation", out=junk, in_=xt, func=AF.Square, accum_out=ss, r=["xt"], w=["junk", "ss"])
                rs = tmpc[:, 1:2]
                rstd_from(ss, D, rs, "ss")
                P.dve("tensor_scalar", out=xnb, in0=xt, scalar1=rs, scalar2=None, op0=ALU.mult,
                      r=["xt", "ssr"], w=["xnb"])
                ptt = pt[t % 2]
                pk = f"pt{t % 2}"
                import os
                lndbg = int(os.environ.get("LNDBG", "0"))
                if lndbg == 1:
                    continue
                for c in range(8):
                    P.pe("transpose", ptt[:, c * 128:(c + 1) * 128], xnb[:, c * 128:(c + 1) * 128], identb[:],
                         r=["xnb", "identb"], w=[pk])
                if lndbg == 2:
                    continue
                for c in range(8):
                    dst = hT[:, c, t * 128:(t + 1) * 128]
                    src = ptt[:, c * 128:(c + 1) * 128]
                    if lndbg == 3 and c % 2 == 1:
                        continue
                    if lndbg == 4 and c % 2 == 0:
                        continue
                    if t % 2 == 0:
                        P.act("activation", out=dst, in_=src, func=AF.Identity,
                                                                           scale=sc[:, c:c + 1], bias=bi[:, c:c + 1],
                              r=[pk, key_sc], w=[("h", c, t)])
                    else:
                        P.dve("tensor_scalar", out=dst, in0=src, scalar1=sc[:, c:c + 1],
                                                                              scalar2=bi[:, c:c + 1], op0=ALU.mult,
                                                                              op1=ALU.add,
                              r=[pk, key_sc], w=[("h", c, t)])

        def wpiece(w_ap, l, c0, ncol, nk=8):
            return w_ap[l].rearrange("(k p) n -> p k n", p=128)[:, :, c0:c0 + ncol]

        def dump(name, src, keys):
            if name in dbg_out:
                P.dma("sp", dbg_out[name], src, r=keys)

        for l in range(L):
            xsrc = x_in if l == 0 else xlay[l - 1]
            xdst = y_out if l == L - 1 else xlay[l]
            lam_init = 0.8 - 0.6 * math.exp(-0.3 * l)
            for kc in range(8):
                load_w(wm[:, kc, :], w_mod[l, kc * 128:(kc + 1) * 128, :], ("wm", kc))
            P.dma("sp", bmt, b_modT[l], w=["bmt"])
            P.dma("sp", nmx, nmixT[l], w=["nmx"])
            P.dma("sp", nfx, nffnT[l], w=["nfx"])
            P.dma("sp", gcols, gcols_in[l], w=["gcols"])
            P.dma("sp", mon_bc, mon_in[l].partition_broadcast(128), w=["mon_bc"])
            P.dma("sp", subln_bc, subln_in[l].partition_broadcast(128), w=["subln_bc"])
            P.dma("sp", scrF[:, 0:256], dlam_in[l].partition_broadcast(128), w=["dl"])
            for j in range(48):
                for kc in range(8):
                    P.pe("matmul", pb[0][:, j:j + 1], lhsT=wm[:, kc, j * 128:(j + 1) * 128],
                                                        rhs=condb[:, kc:kc + 1], start=(kc == 0), stop=(kc == 7),
                         r=[("wm", kc), "condb"], w=["pb0"])
            P.dve("tensor_tensor", out=modT, in0=pb[0][:, 0:48], in1=bmt, op=ALU.add, r=["pb0", "bmt"], w=["modT"])
            P.dve("scalar_tensor_tensor", out=sc1, in0=modT[:, 8:16], scalar=1.0, in1=nmx, op0=ALU.add, op1=ALU.mult,
                  r=["modT", "nmx"], w=["sc1"])
            P.dve("tensor_copy", out=bi1, in_=modT[:, 0:8], r=["modT"], w=["sc1"])
            P.dve("scalar_tensor_tensor", out=sc2, in0=modT[:, 32:40], scalar=1.0, in1=nfx, op0=ALU.add, op1=ALU.mult,
                  r=["modT", "nfx"], w=["sc2"])
            P.dve("tensor_copy", out=bi2, in_=modT[:, 24:32], r=["modT"], w=["sc2"])
            for gi, (gt, jj) in enumerate(((gate_a, 2), (gate_f, 5))):
                for half in range(2):
                    c0 = jj * D + half * 512
                    pbi = 1 + half
                    bb = scrF[:, 512 + half * 512:1024 + half * 512]
                    P.dma("sp", bb, b_mod[l][:, c0:c0 + 512].partition_broadcast(128), w=[("bb", half)])
                    for kc in range(8):
                        P.pe("matmul", pb[pbi][:], lhsT=cond_rep[:, kc, :],
                                                                      rhs=wm[:, kc, c0:c0 + 512], start=(kc == 0),
                                                                      stop=(kc == 7),
                             r=[("wm", kc), "cond_rep"], w=[f"pb{pbi}"])
                    P.dve("tensor_tensor",
                        out=gt[:, half * 512:(half + 1) * 512], in0=pb[pbi][:], in1=bb, op=ALU.add,
                        r=[f"pb{pbi}", ("bb", half)], w=[("gate", gi, half)])
            dl = scrF[:, 0:256]
            pr = scrF[:, 256:384]
            P.dve("tensor_tensor", out=pr[:, 0:64], in0=dl[:, 0:64], in1=dl[:, 64:128], op=ALU.mult, r=["dl"], w=["pr"])
            P.dve("tensor_tensor", out=pr[:, 64:128], in0=dl[:, 128:192], in1=dl[:, 192:256], op=ALU.mult,
                  r=["dl"], w=["pr"])
            P.dve("tensor_reduce", out=lamt[:, 0:2], in_=pr.rearrange("p (a b) -> p a b", a=2), axis=AX.X,
                                            op=ALU.add, r=["pr"], w=["lamt"])
            P.act("activation", out=lamt[:, 2:4], in_=lamt[:, 0:2], func=AF.Exp, r=["lamt"], w=["lamt"])
            P.dve("tensor_tensor", out=lamt[:, 4:5], in0=lamt[:, 2:3], in1=lamt[:, 3:4], op=ALU.subtract,
                  r=["lamt"], w=["lamt"])
            P.dve("tensor_scalar", out=lamcol, in0=lamt[:, 4:5], scalar1=float(lam_init), scalar2=None, op0=ALU.add,
                  r=["lamt"], w=["lamcol"])
            P.dve("tensor_scalar", out=subln_bc, in0=subln_bc, scalar1=float(1.0 - lam_init), scalar2=None,
                                            op0=ALU.mult, r=["subln_bc"], w=["subln_bc"])
            P.barrier("M")
            ln_phase(xsrc, sc1, bi1, "sc1")
            P.barrier("A")
            dump(f"d_hT{l}", R1[:], [])
            P.pool("memset", Vm[:, :, :, 64:65], 1.0, w=["Vones"])
            P.pool("memset", Vd[:, :, :, 128:129], 1.0, w=["Vones"])
            pieces = [(0, 0, 0), (512, 4, 1), (1536, 8, 2), (2048, 12, 3)]
            allcols = [0, 512, 1536, 2048, 1024, 2560]
            wAs = [WS[:, sl * 4096:(sl + 1) * 4096].rearrange("p (k n) -> p k n", k=8) for sl in range(2)]

            def load_piece(idx):
                if idx < len(allcols):
                    load_w(wAs[idx % 2], wpiece(w_in, l, allcols[idx], 512), ("wA", idx % 2))
            load_piece(0)
            tile_i = 0
            for pi, (col0, cb, gi) in enumerate(pieces):
                wA = wAs[pi % 2]
                wkey = ("wA", pi % 2)
                load_piece(pi + 1)
                for cc in range(4):
                    for tg in range(NG):
                        par = tile_i % 2
                        tile_i += 1
                        pq, pss, psw = pb[par], pb[2 + par], pb[4 + par]
                        kq, ks, kw = f"pb{par}", f"pb{2 + par}", f"pb{4 + par}"
                        tok = slice(tg * 512, (tg + 1) * 512)
                        for kc in range(8):
                            P.pe("matmul",
                                pq[:], lhsT=wA[:, kc, cc * 128:(cc + 1) * 128], rhs=hT[:, kc, tok], start=(kc == 0),
                                stop=(kc == 7),
                                r=[wkey] + [("h", kc, 4 * tg + i) for i in range(4)], w=[kq])
                        sqb = scrB[:, par * 512:(par + 1) * 512]
                        qgb = scrB[:, 1024 + par * 512:1024 + (par + 1) * 512]
                        lnr = scrF[:, par * 1536:par * 1536 + 512]
                        rstd = lnr
                        qgf = scrF[:, par * 1536 + 512:par * 1536 + 1024]
                        t1 = qgf
                        t2 = scrF[:, par * 1536 + 1024:par * 1536 + 1536]
                        gcol = gcols[:, gi:gi + 1]
                        P.act("activation", out=sqb, in_=pq[:], func=AF.Square, r=[kq], w=[("sqb", par)])
                        P.act("activation", out=qgf, in_=pq[:], func=AF.Copy, scale=gcol, r=[kq, "gcols"], w=[("qgf", par)])
                        P.dve("tensor_copy", out=qgb, in_=qgf, r=[("qgf", par)], w=[("qgb", par)])
                        P.pe("matmul", pss[:], lhsT=bonesb[:], rhs=sqb, start=True, stop=True,
                             r=["bonesb", ("sqb", par)], w=[ks])
                        P.pe("matmul", psw[:], lhsT=pmb[:], rhs=qgb, start=True, stop=True,
                             r=["pmb", ("qgb", par)], w=[kw])
                        P.act("activation", out=lnr, in_=pss[:], func=AF.Ln, bias=eps_col, r=[ks, "cst"], w=[("lnr", par)])
                        P.act("activation", out=rstd, in_=lnr, func=AF.Exp, scale=-0.5, r=[("lnr", par)], w=[("lnr", par)])
                        P.dve("tensor_tensor", out=t1, in0=qgf, in1=ctab[:, tok], op=ALU.mult, r=[("qgf", par), "tab"], w=[("qgf", par)])
                        P.dve("tensor_tensor", out=t2, in0=psw[:], in1=stab[:, tok], op=ALU.mult,
                              r=[kw, "tab"], w=[("t2", par)])
                        P.pool("tensor_tensor", out=t1, in0=t1, in1=t2, op=ALU.add, r=[("qgf", par), ("t2", par)], w=[("qgf", par)])
                        dst = qkT[:, cb + cc, tok]
                        P.dve("tensor_tensor", out=dst, in0=t1, in1=rstd, op=ALU.mult,
                              r=[("qgf", par), ("lnr", par)], w=[("qk", cb + cc, tg)])
            for vi, (col0, isd) in enumerate(((1024, False), (2560, True))):
                wA = wAs[vi % 2]
                wkey = ("wA", vi % 2)
                load_piece(4 + vi + 1)
                for t in range(NT):
                    par = t % 2
                    pv = pb[par]
                    for kc in range(8):
                        P.pe("matmul",
                            pv[:], lhsT=hT[:, kc, t * 128:(t + 1) * 128], rhs=wA[:, kc, :], start=(kc == 0), stop=(kc == 7),
                            r=[wkey, ("h", kc, t)], w=[f"pb{par}"])
                    if isd:
                        dst = Vd[:, t, :, 0:128]
                        src = pv[:].rearrange("p (h e) -> p h e", h=4)
                    else:
                        dst = Vm[:, t, :, 0:64]
                        src = pv[:].rearrange("p (h e) -> p h e", h=8)
                    if t % 2 == 0:
                        P.act("activation", out=dst, in_=src, func=AF.Copy,
                              r=[f"pb{par}"], w=[("V", t, isd)])
                    else:
                        P.dve("tensor_copy", out=dst, in_=src, r=[f"pb{par}"], w=[("V", t, isd)])
            for c in range(4):
                P.dve("tensor_reduce", out=kmf[:, c, 0:NB], in_=qkT[:, 4 + c, :].rearrange("p (b t) -> p b t", t=256),
                                                     axis=AX.X, op=ALU.add,
                      r=[("qk", 4 + c, tg) for tg in range(NG)], w=["kmf"])
            P.dve("tensor_scalar", out=kmT[:, :, 0:NB], in0=kmf[:, :, 0:NB], scalar1=1.0 / 256, scalar2=None,
                                            op0=ALU.mult, r=["kmf"], w=["kmT"])
            P.barrier("B")
            dump(f"d_qk{l}", R2[:, 0:16 * S], [])
            dump(f"d_V{l}", R2[:, 16 * S:16 * S + NT * VW], [])
            oT = hT
            otok = scrB[:, 0:4096].rearrange("p (j f) -> p j f", j=4)
            pTb = [scrB[:, 4096 + i * 512:4096 + (i + 1) * 512] for i in range(4)]
            biasT = scrB[:, 6144:6656]
            biasq2 = scrB[:, 6656:6784].rearrange("p (d h n) -> p d h n", d=2, h=8)
            biasq = biasq2[:, 0]
            ocs = scrF[:, 0:1024].rearrange("p (j c e) -> p j c e", j=4, c=2)
            gs = scrF[:, 1280:1344].rearrange("p (h n) -> p h n", h=8)
            cnt = scrF[:, 1344:1408].rearrange("p (h n) -> p h n", h=8)
            cmpb = scrF[:, 1408:1920]
            ss4, ls4, rs4, rd4 = small[:, 432:436], small[:, 436:440], small[:, 440:444], small[:, 444:448]
            pt_i = 0
            st_i = 0
            for g in range(NG):
                masked_g = (2 * g + 1) >= 4
                import os
                atdbg = int(os.environ.get("ATDBG", "0"))
                if masked_g and atdbg in (2,):
                    P.dve("memset", biasT, 0.0, w=["biasT"])
                if masked_g and atdbg not in (2, 3):
                    for j in range(4):
                        qt = 4 * g + j
                        qb = qt // 2
                        for h in range(8):
                            base = (h % 2) * 64
                            P.pe("matmul", pb[h % 2][:, h * 8:h * 8 + NB],
                                 lhsT=qkT[base:base + 64, h // 2, qt * 128:(qt + 1) * 128],
                                 rhs=kmT[base:base + 64, h // 2, 0:NB], start=True, stop=True,
                                 r=[("qk", h // 2, g), "kmT"], w=[f"pb{h % 2}"])
                        for par2 in range(2):
                            P.dve("tensor_copy", out=gs[:, par2::2, :],
                                  in_=pb[par2][:, 0:64].rearrange("p (h n) -> p h n", h=8)[:, par2::2, :],
                                  r=[f"pb{par2}"], w=["gs"])
                        cmpv = cmpb[:, 0:8 * qb * qb].rearrange("p (h n m) -> p h n m", h=8, n=qb)
                        in0 = gs[:, :, 0:qb].unsqueeze(2).broadcast_to([128, 8, qb, qb])
                        in1 = gs[:, :, 0:qb].unsqueeze(3).broadcast_to([128, 8, qb, qb])
                        P.dve("tensor_tensor", out=cmpv, in0=in0, in1=in1, op=ALU.is_gt,
                              r=["gs"], w=["cmpb"])
                        P.dve("tensor_reduce", out=cnt[:, :, 0:qb], in_=cmpv, axis=AX.X, op=ALU.add,
                              r=["cmpb"], w=["cnt"])
                        P.dve("memset", scrB[:, 6656:6784], 0.0, w=["biasq"])
                        for d2 in range(2):
                            P.dve("tensor_scalar", out=biasq2[:, d2, :, 0:qb], in0=cnt[:, :, 0:qb], scalar1=2.5,
                                  scalar2=-BIG, op0=ALU.is_gt, op1=ALU.mult, r=["cnt"], w=["biasq"])
                        P.pe("transpose", pt[0][:, 0:128], scrB[:, 6656:6784], identb[:],
                             r=["biasq", "identb"], w=["pt0"])
                        P.dve("tensor_copy", out=biasT[:, j * 128:(j + 1) * 128], in_=pt[0][:, 0:128],
                              r=["pt0"], w=["biasT"])
                nkt = 4 * g + 4
                items = [(hp, kt) for hp in range(16) for kt in range(nkt)]
                info = {}

                def pass_cfg(hp):
                    isd = hp >= 8
                    hh = hp - 8 if isd else hp
                    base = (hh % 2) * 64
                    qc = (8 if isd else 0) + hh // 2
                    kc_ = (12 if isd else 4) + hh // 2
                    if isd:
                        vh, comp, dv = hh // 2, hh % 2, 128
                    else:
                        vh, comp, dv = hh, 0, 64
                    return isd, hh, base, qc, kc_, vh, comp, dv

                def emit_qk(idx):
                    nonlocal st_i, pt_i
                    hp, kt = items[idx]
                    isd, hh, base, qc, kc_, vh, comp, dv = pass_cfg(hp)
                    i = kt - 4 * g
                    c0 = 128 * max(i, 0)
                    ncol = 512 - c0
                    sp_ = st_i % 2
                    st_i += 1
                    ps_ = pb[sp_]
                    pk = f"pb{sp_}"
                    mask_here = (not isd) and masked_g and (kt // 2) < (2 * g + 1) and atdbg not in (1, 3)
                    P.pe("matmul", ps_[:, 0:ncol], lhsT=qkT[base:base + 64, kc_, kt * 128:(kt + 1) * 128],
                         rhs=qkT[base:base + 64, qc, g * 512 + c0:(g + 1) * 512], start=True, stop=not mask_here,
                         r=[("qk", kc_, kt // 4), ("qk", qc, g)], w=[pk])
                    if mask_here:
                        n = kt // 2
                        col = base + 8 * hh + n
                        P.pe("matmul", ps_[:, 0:ncol],
                             lhsT=identb[base:base + 64, col:col + 1].broadcast_to([64, 128]),
                             rhs=biasT[base:base + 64, c0:512], start=False, stop=True,
                             r=["identb", "biasT"], w=[pk])
                    info[idx] = (ps_, pk, pt_i % 4, i, c0, ncol)
                    pt_i += 1

                def finalize(hp):
                    isd, hh, base, qc, kc_, vh, comp, dv = pass_cfg(hp)
                    for j in range(4):
                        acc = pb[2 + j]
                        ak = f"pb{2 + j}"
                        rd = rd4[:, j:j + 1]
                        P.dve("reciprocal", out=rd, in_=acc[:, dv:dv + 1], r=[ak], w=[("rd", j)])
                        if isd and comp == 1:
                            P.dve("tensor_tensor", out=rd, in0=rd, in1=lamcol, op=ALU.mult, r=[("rd", j), "lamcol"], w=[("rd", j)])
                        P.dve("tensor_scalar", out=ocs[:, j, comp, 0:dv], in0=acc[:, 0:dv], scalar1=rd, scalar2=None,
                              op0=ALU.mult, r=[ak, ("rd", j)], w=[("ocs", comp)])
                    if isd and comp == 0:
                        return
                    o0 = ocs[:, :, 0, 0:dv]
                    o1 = ocs[:, :, 1, 0:dv]
                    if isd:
                        P.dve("tensor_tensor", out=o0, in0=o0, in1=o1, op=ALU.subtract, r=[("ocs", 0), ("ocs", 1)], w=[("ocs", 0)])
                    P.dve("tensor_tensor", out=o1, in0=o0, in1=o0, op=ALU.mult, r=[("ocs", 0)], w=[("ocs", 1)])
                    P.dve("tensor_reduce", out=ss4, in_=o1, axis=AX.X, op=ALU.add, r=[("ocs", 1)], w=["ss4"])
                    P.act("activation", out=ls4, in_=ss4, func=AF.Ln, scale=1.0 / dv, bias=eps_col, r=["ss4", "cst"], w=["ls4"])
                    P.act("activation", out=rs4, in_=ls4, func=AF.Exp, scale=-0.5, r=["ls4"], w=["rs4"])
                    P.dve("tensor_tensor", out=o1, in0=o0, in1=rs4.unsqueeze(2).broadcast_to([128, 4, dv]), op=ALU.mult,
                          r=[("ocs", 0), "rs4"], w=[("ocs", 1)])
                    if isd:
                        dst = otok[:, :, 512 + vh * 128:512 + (vh + 1) * 128]
                        gbc = subln_bc.unsqueeze(1).broadcast_to([128, 4, 128])
                        gk = "subln_bc"
                    else:
                        dst = otok[:, :, hh * 64:(hh + 1) * 64]
                        gbc = mon_bc.unsqueeze(1).broadcast_to([128, 4, 64])
                        gk = "mon_bc"
                    P.dve("tensor_tensor", out=dst, in0=o1, in1=gbc, op=ALU.mult, r=[("ocs", 1), gk], w=["otok"])

                def emit_rest(idx):
                    hp, kt = items[idx]
                    isd, hh, base, qc, kc_, vh, comp, dv = pass_cfg(hp)
                    ps_, pk, pti, i, c0, ncol = info[idx]
                    pT = pTb[pti]
                    ptk = ("pT", pti)
                    P.act("activation", out=pT[:, 0:ncol], in_=ps_[:, 0:ncol], func=AF.Exp, scale=0.125, r=[pk], w=[ptk])
                    if i >= 0:
                        P.dve("tensor_tensor", out=pT[:, 0:128], in0=pT[:, 0:128], in1=trimask[:], op=ALU.mult,
                              r=[ptk, "trimask"], w=[ptk])
                    for j in range(max(i, 0), 4):
                        off = (j - max(i, 0)) * 128
                        vr = Vd[:, kt, vh, :] if isd else Vm[:, kt, vh, :]
                        P.pe("matmul", pb[2 + j][:, 0:dv + 1], lhsT=pT[:, off:off + 128], rhs=vr, start=(kt == 0),
                             stop=(kt == 4 * g + j), r=[ptk, ("V", kt, isd)], w=[f"pb{2 + j}"])
                    if kt == nkt - 1:
                        finalize(hp)

                emit_qk(0)
                for idx in range(len(items)):
                    if idx + 1 < len(items):
                        emit_qk(idx + 1)
                    emit_rest(idx)
                for j in range(4):
                    t = 4 * g + j
                    for c in range(8):
                        P.pe("transpose", pt[1][:, c * 128:(c + 1) * 128], otok[:, j, c * 128:(c + 1) * 128],
                                                             identb[:], r=["otok", "identb"], w=["pt1"])
                    P.act("activation", out=oT[:, :, t * 128:(t + 1) * 128],
                                                      in_=pt[1][:].rearrange("p (c s) -> p c s", c=8), func=AF.Copy,
                          r=["pt1"], w=[("o", t)])
            P.barrier("C")
            dump(f"d_oT{l}", R1[:], [])
            wO = [WS[:, hf * 4096:(hf + 1) * 4096].rearrange("p (k n) -> p k n", k=8) for hf in range(2)]
            for hf in range(2):
                load_w(wO[hf], wpiece(w_out, l, hf * 512, 512), ("wA", hf))
            for hf in range(2):
                for t in range(NT):
                    par = t % 2
                    xh = scrF[:, par * 512:(par + 1) * 512]
                    yh = scrF[:, 1024 + par * 512:1024 + (par + 1) * 512]
                    P.dma("sp", xh, xsrc[t * 128:(t + 1) * 128, hf * 512:(hf + 1) * 512], w=[("xh", par)])
                    py = pb[par]
                    for kc in range(8):
                        P.pe("matmul", py[:], lhsT=oT[:, kc, t * 128:(t + 1) * 128], rhs=wO[hf][:, kc, :], start=(kc == 0),
                             stop=(kc == 7), r=[("wA", hf), ("o", t)], w=[f"pb{par}"])
                    P.dve("tensor_tensor", out=yh, in0=py[:], in1=gate_a[:, hf * 512:(hf + 1) * 512], op=ALU.mult,
                          r=[f"pb{par}", ("gate", 0, hf)], w=[("yh", par)])
                    P.pool("tensor_tensor", out=xh, in0=xh, in1=yh, op=ALU.add, r=[("xh", par), ("yh", par)], w=[("xh", par)])
                    P.dma("sp", xmid[l][t * 128:(t + 1) * 128, hf * 512:(hf + 1) * 512], xh, r=[("xh", par)], w=[("xm", t, hf)])
            P.barrier("D")
            ln_phase(xmid[l], sc2, bi2, "sc2")
            P.barrier("E")
            npiece = DFF // 256
            def wgu(slot):
                return (WS[:, slot * 4096:slot * 4096 + 2048].rearrange("p (k n) -> p k n", k=8),
                        WS[:, slot * 4096 + 2048:slot * 4096 + 4096].rearrange("p (k n) -> p k n", k=8))

            def load_gu(pi):
                if pi < npiece:
                    a, b = wgu(pi % 2)
                    load_w(a, wpiece(w_gate, l, pi * 256, 256), ("wg", pi % 2))
                    load_w(b, wpiece(w_up, l, pi * 256, 256), ("wu", pi % 2))
            load_gu(0)
            for pi in range(npiece):
                slot = pi % 2
                wg, wu = wgu(slot)
                load_gu(pi + 1)
                for cc in range(2):
                    jf = 2 * pi + cc
                    for tg in range(NG):
                        par = (jf * NG + tg) % 2
                        pg, pu = pb[par], pb[2 + par]
                        tok = slice(tg * 512, (tg + 1) * 512)
                        for kc in range(8):
                            P.pe("matmul",
                                pg[:], lhsT=wg[:, kc, cc * 128:(cc + 1) * 128], rhs=hT[:, kc, tok], start=(kc == 0), stop=(kc == 7),
                                r=[("wg", slot)] + [("h", kc, 4 * tg + i) for i in range(4)], w=[f"pb{par}"])
                        for kc in range(8):
                            P.pe("matmul",
                                pu[:], lhsT=wu[:, kc, cc * 128:(cc + 1) * 128], rhs=hT[:, kc, tok], start=(kc == 0), stop=(kc == 7),
                                r=[("wu", slot)] + [("h", kc, 4 * tg + i) for i in range(4)], w=[f"pb{2 + par}"])
                        sg = scrB[:, par * 512:(par + 1) * 512]
                        P.act("activation", out=sg, in_=pg[:], func=AF.Silu, r=[f"pb{par}"], w=[("sg", par)])
                        P.dve("tensor_tensor", out=actT[:, jf, tok], in0=pu[:], in1=sg, op=ALU.mult,
                              r=[f"pb{2 + par}", ("sg", par)], w=[("a", jf, tg)])
            P.barrier("F")
            wD = R1[:, 0:NFF * 512].rearrange("p (j n) -> p j n", j=NFF)
            for hf in range(2):
                load_w(wD, w_down[l].rearrange("(j p) n -> p j n", p=128)[:, :, hf * 512:(hf + 1) * 512], "wD")
                for t in range(NT):
                    xh = scrF[:, (t % 2) * 512:(t % 2 + 1) * 512]
                    yh = scrF[:, 1024 + (t % 2) * 512:1024 + (t % 2 + 1) * 512]
                    par = t % 2
                    P.dma("sp", xh, xmid[l][t * 128:(t + 1) * 128, hf * 512:(hf + 1) * 512], r=[("xm", t, hf)], w=[("xh", par)])
                    py = pb[par]
                    for jf in range(NFF):
                        P.pe("matmul",
                            py[:], lhsT=actT[:, jf, t * 128:(t + 1) * 128], rhs=wD[:, jf, :], start=(jf == 0), stop=(jf == NFF - 1),
                            r=["wD", ("a", jf, t // 4)], w=[f"pb{par}"])
                    P.dve("tensor_tensor", out=yh, in0=py[:], in1=gate_f[:, hf * 512:(hf + 1) * 512],
                                                                         op=ALU.mult,
                          r=[f"pb{par}", ("gate", 1, hf)], w=[("yh", par)])
                    P.pool("tensor_tensor", out=xh, in0=xh, in1=yh, op=ALU.add,
                           r=[("xh", par), ("yh", par)], w=[("xh", par)])
                    P.dma("sp", xdst[t * 128:(t + 1) * 128, hf * 512:(hf + 1) * 512], xh, r=[("xh", par)], w=[("xo", t, hf)])
            P.barrier("G")

        if stop:
            P.ops = P.ops[:P.marks[stop]]
        with nc.Block() as block:
            bodies = P.bodies(sems)
            block.sync(bodies["sp"])
            block.tensor(bodies["pe"])
            block.scalar(bodies["act"])
            block.vector(bodies["dve"])
            block.gpsimd(bodies["pool"])
    return nc


def make_in_maps(inputs, S=2048, L=2, cores=8):
    f32 = np.float32
    x = np.asarray(inputs["x"], f32)
    c = np.asarray(inputs["c"], f32)
    pos = np.asarray(inputs["positions"], np.int32)
    cst, pm, bones = host_consts()

    def colsT(v, n):
        v = np.asarray(v, f32)
        return np.ascontiguousarray(v.reshape(v.shape[0], n, 128).transpose(0, 2, 1))

    def tile2(v):
        v = np.asarray(v, f32)
        return np.concatenate([v, v], axis=1)

    gcols = np.stack([tile2(inputs["moba_q_norm"]), tile2(inputs["moba_k_norm"]),
                      tile2(inputs["diff_q_norm"]), tile2(inputs["diff_k_norm"])], axis=2)
    shared = {
        "w_mod": np.ascontiguousarray(np.asarray(inputs["w_mod"], f32)[:L]),
        "b_modT": colsT(np.asarray(inputs["b_mod"])[:L], 48),
        "b_mod": np.ascontiguousarray(np.asarray(inputs["b_mod"], f32)[:L, None, :]),
        "nmixT": colsT(np.asarray(inputs["norm_mix"])[:L], 8),
        "nffnT": colsT(np.asarray(inputs["norm_ffn"])[:L], 8),
        "w_in": np.ascontiguousarray(np.asarray(inputs["w_in"], f32)[:L]),
        "gcols": np.ascontiguousarray(gcols[:L]),
        "mon": np.ascontiguousarray(np.asarray(inputs["moba_out_norm"], f32)[:L, None, :]),
        "subln": np.ascontiguousarray(np.asarray(inputs["diff_subln"], f32)[:L, None, :]),
        "dlam": np.ascontiguousarray(np.asarray(inputs["diff_lambda"], f32)[:L].reshape(L, 1, 256)),
        "w_out": np.ascontiguousarray(np.asarray(inputs["w_out"], f32)[:L]),
        "w_gate": np.ascontiguousarray(np.asarray(inputs["w_gate"], f32)[:L]),
        "w_up": np.ascontiguousarray(np.asarray(inputs["w_up"], f32)[:L]),
        "w_down": np.ascontiguousarray(np.asarray(inputs["w_down"], f32)[:L]),
        "cst": cst, "pmat": pm, "bones": bones,
    }
    maps = []
    for b in range(cores):
        m = dict(shared)
        m["x"] = np.ascontiguousarray(x[b, :S])
        m["cT"] = np.ascontiguousarray(c[b].reshape(8, 128).T)
        m["pos"] = np.ascontiguousarray(pos[b:b + 1, :S])
        maps.append(m)
    return maps


def kernel(**inputs):
    S, L = 2048, 2
    nc = build(S, L)
    maps = make_in_maps(inputs, S, L, 8)
    res = run_bass_kernel_spmd(nc, maps, core_ids=list(range(8)))
    return np.stack([np.asarray(r["y"], np.float32) for r in res.results], axis=0)
```

```python
import math
from contextlib import ExitStack
import numpy as np
import concourse.bass as bass
import concourse.mybir as mybir
from concourse.bass_utils import run_bass_kernel_spmd

F32 = mybir.dt.float32
BF16 = mybir.dt.bfloat16
I32 = mybir.dt.int32
ALU = mybir.AluOpType
AF = mybir.ActivationFunctionType
AX = mybir.AxisListType

D = 1024
DFF = 2816
NFF = DFF // 128
EPS = 1e-6
BIG = 30000.0
NDMA = {"sp": 20, "pool": 12}
COMPUTE = ("pe", "act", "dve", "pool")


class Op:
    __slots__ = ("eng", "fn", "reads", "writes", "dma", "waits", "ev")

    def __init__(self, eng, fn, reads, writes, dma):
        self.eng, self.fn, self.reads, self.writes, self.dma = eng, fn, reads, writes, dma
        self.waits = []
        self.ev = None


class Prog:
    def __init__(self):
        self.ops = []
        self.marks = {}

    def add(self, eng, name, args, kwargs, reads=(), writes=(), dma=False):
        def fn(e, name=name, args=args, kwargs=kwargs):
            return getattr(e, name)(*args, **kwargs)
        self.ops.append(Op(eng, fn, tuple(reads), tuple(writes), dma))

    def pe(self, name, *args, r=(), w=(), **kw):
        self.add("pe", name, args, kw, r, w)

    def act(self, name, *args, r=(), w=(), **kw):
        self.add("act", name, args, kw, r, w)

    def dve(self, name, *args, r=(), w=(), **kw):
        self.add("dve", name, args, kw, r, w)

    def pool(self, name, *args, r=(), w=(), **kw):
        self.add("pool", name, args, kw, r, w)

    def dma(self, q, out, in_, r=(), w=()):
        self.add(q, "dma_start", (), dict(out=out, in_=in_), r, w, dma=True)

    def barrier(self, mark=None):
        self.ops.append("BARRIER")
        if mark:
            self.marks[mark] = len(self.ops)

    def analyze(self):
        cnt = {e: 0 for e in COMPUTE}
        dcnt = {q: 0 for q in NDMA}
        last_w, readers = {}, {}
        seen = {e: {} for e in ("pe", "act", "dve", "pool", "sp")}
        all_events = {}

        def need(op, ev):
            if ev is None:
                return
            key, val = ev
            if key == "pe" and op.eng == "pe" and not op.dma:
                return
            s = seen[op.eng]
            if s.get(key, 0) >= val:
                return
            s[key] = val
            op.waits.append((key, val))

        out = []
        for op in self.ops:
            if op == "BARRIER":
                b = Op("all", None, (), (), False)
                b.waits = dict(all_events)
                for e in seen:
                    for k, v in all_events.items():
                        if seen[e].get(k, 0) < v:
                            seen[e][k] = v
                out.append(b)
                continue
            if op.dma:
                q = op.eng
                i = dcnt[q]
                dcnt[q] += 1
                key = ("dma", q, i % NDMA[q])
                val = 16 * (i // NDMA[q] + 1)
                if val > 16:
                    need(op, (key, val - 16))
                op.ev = (key, val)
            else:
                cnt[op.eng] += 1
                op.ev = (op.eng, cnt[op.eng])
            for r in op.reads:
                need(op, last_w.get(r))
            for w in op.writes:
                need(op, last_w.get(w))
                for ev in readers.get(w, ()):
                    need(op, ev)
            for r in op.reads:
                readers.setdefault(r, []).append(op.ev)
            for w in op.writes:
                last_w[w] = op.ev
                readers[w] = []
            all_events[op.ev[0]] = max(all_events.get(op.ev[0], 0), op.ev[1])
            out.append(op)
        self.sched = out
        self.final_events = dict(all_events)

    def bodies(self, sems):
        self.analyze()
        sched, final_events = self.sched, self.final_events

        def body_for(ename):
            def body(eng):
                seen = {}
                for op in sched:
                    if op.eng == "all":
                        for k, v in op.waits.items():
                            if k != ename and seen.get(k, 0) < v:
                                eng.wait_ge(sems[k], v)
                                seen[k] = v
                        continue
                    if op.eng != ename:
                        continue
                    for k, v in op.waits:
                        eng.wait_ge(sems[k], v)
                        seen[k] = max(seen.get(k, 0), v)
                    inst = op.fn(eng)
                    inst.then_inc(sems[op.ev[0]], 16 if op.dma else 1)
                if ename == "sp":
                    for k, v in final_events.items():
                        if seen.get(k, 0) < v:
                            eng.wait_ge(sems[k], v)
            return body

        return {e: body_for(e) for e in ("pe", "act", "dve", "pool", "sp")}


def all_sem_keys():
    keys = list(COMPUTE)
    for q, n in NDMA.items():
        keys += [("dma", q, s) for s in range(n)]
    return keys


def host_consts():
    f = np.arange(128)
    d = f % 64
    inv = (500000.0 ** (-np.arange(0, 16, 2, dtype=np.float32) / 16.0)).astype(np.float32)
    cst = np.zeros((128, 8), np.float32)
    cst[:, 0] = np.where(d < 16, inv[d % 8], 0.0)
    cst[:, 1] = np.where(d < 8, -1.0, np.where(d < 16, 1.0, 0.0))
    cst[:, 2] = EPS
    cst[:, 3] = 1.0
    pm = np.zeros((128, 128), np.float32)
    for m in range(128):
        dm = m % 64
        if dm < 8:
            pm[m + 8, m] = 1.0
        elif dm < 16:
            pm[m - 8, m] = 1.0
    bones = np.zeros((128, 128), np.float32)
    bones[:64, :64] = 1.0 / 64
    bones[64:, 64:] = 1.0 / 64
    return cst, pm, bones


def build(S=2048, L=2, dbg=(), stop=None):
    NT, NG, NB = S // 128, S // 512, S // 256
    VW = 8 * 65 + 4 * 129
    nc = bass.Bass("TRN2", target_bir_lowering=False)

    def din(name, shape, dt=F32):
        return nc.dram_tensor(name, list(shape), dt, kind="ExternalInput").ap()

    x_in = din("x", [S, D])
    cT_in = din("cT", [128, 8])
    pos_in = din("pos", [1, S], I32)
    w_mod = din("w_mod", [L, D, 6 * D])
    b_modT = din("b_modT", [L, 128, 48])
    b_mod = din("b_mod", [L, 1, 6 * D])
    nmixT = din("nmixT", [L, 128, 8])
    nffnT = din("nffnT", [L, 128, 8])
    w_in = din("w_in", [L, D, 3072])
    gcols_in = din("gcols", [L, 128, 4])
    mon_in = din("mon", [L, 1, 64])
    subln_in = din("subln", [L, 1, 128])
    dlam_in = din("dlam", [L, 1, 256])
    w_out = din("w_out", [L, D, D])
    w_gate = din("w_gate", [L, D, DFF])
    w_up = din("w_up", [L, D, DFF])
    w_down = din("w_down", [L, DFF, D])
    cst_in = din("cst", [128, 8])
    pm_in = din("pmat", [128, 128])
    bones_in = din("bones", [128, 128])
    y_out = nc.dram_tensor("y", [S, D], F32, kind="ExternalOutput").ap()
    xmid = [nc.dram_tensor(f"xmid{l}", [S, D], F32, kind="Internal").ap() for l in range(L)]
    xlay = [nc.dram_tensor(f"xlay{l}", [S, D], F32, kind="Internal").ap() for l in range(L - 1)]
    dbg_out = {}
    for name, shape, dt in dbg:
        dbg_out[name] = nc.dram_tensor(name, list(shape), dt, kind="ExternalOutput").ap()

    es = ExitStack()
    with es:
        def sb(name, shape, dt):
            return es.enter_context(nc.sbuf_tensor(name, list(shape), dt))

        def psum(name, shape, dt):
            return es.enter_context(nc.psum_tensor(name, list(shape), dt))

        R1 = sb("R1", [128, max(8 * S, NFF * 512)], BF16)
        R2N = max(16 * S + NT * VW, NFF * S, 8 * 6144 if S >= 512 else 0)
        R2 = sb("R2", [128, R2N], BF16)
        ctab = sb("ctab", [128, S], F32)
        stab = sb("stab", [128, S], F32)
        gate_a = sb("gate_a", [128, D], F32)
        gate_f = sb("gate_f", [128, D], F32)
        scrF = sb("scrF", [128, 3072], F32)
        scrB = sb("scrB", [128, 8192], BF16)
        WS = sb("WS", [128, 8192], BF16)
        identb = sb("identb", [128, 128], BF16)
        trimask = sb("trimask", [128, 128], BF16)
        pmb = sb("pmb", [128, 128], BF16)
        bonesb = sb("bonesb", [128, 128], BF16)
        cst = sb("cst_sb", [128, 8], F32)
        small = sb("small", [128, 512], F32)
        cond_rep = sb("cond_rep", [128, 8, 128], BF16)
        condb = sb("condb", [128, 8], BF16)
        kmT = sb("kmT", [128, 4, 8], BF16)
        pb = [psum(f"pb{i}", [128, 512], F32) for i in range(6)]
        pt = [psum(f"pt{i}", [128, 1024], BF16) for i in range(2)]
        sems = {}
        for k in all_sem_keys():
            nm = "s_" + ("_".join(map(str, k)) if isinstance(k, tuple) else k)
            sems[k] = es.enter_context(nc.semaphore(nm))

        P = Prog()
        hT = R1[:, 0:8 * S].rearrange("p (c s) -> p c s", c=8)
        qkT = R2[:, 0:16 * S].rearrange("p (c s) -> p c s", c=16)
        Vall = R2[:, 16 * S:16 * S + NT * VW].rearrange("p (t w) -> p t w", t=NT)
        Vm = Vall[:, :, 0:520].rearrange("p t (h e) -> p t h e", h=8)
        Vd = Vall[:, :, 520:VW].rearrange("p t (h e) -> p t h e", h=4)
        actT = R2[:, 0:NFF * S].rearrange("p (j s) -> p j s", j=NFF)
        wm = R2[:, 0:8 * 6144].rearrange("p (k n) -> p k n", k=8)
        modT = small[:, 0:48]
        sc1, bi1, sc2, bi2 = small[:, 48:56], small[:, 56:64], small[:, 64:72], small[:, 72:80]
        gcols = small[:, 80:84]
        lamcol = small[:, 84:85]
        tmpc = small[:, 85:96]
        nmx = small[:, 96:104]
        nfx = small[:, 104:112]
        bmt = small[:, 112:160]
        mon_bc = small[:, 160:224]
        subln_bc = small[:, 224:352]
        cond = small[:, 352:360]
        ctf = small[:, 360:368]
        lamt = small[:, 368:400]
        kmf = small[:, 400:432].rearrange("p (c n) -> p c n", c=4)
        eps_col = cst[:, 2:3]

        P.dma("sp", cst[:], cst_in, w=["cst"])
        P.dma("sp", ctf, cT_in, w=["ctf"])
        P.dma("pool", pmb[:], pm_in, w=["pmb"])
        P.dma("pool", bonesb[:], bones_in, w=["bonesb"])
        idf = scrF[:, 0:128]
        P.pool("memset", idf, 1.0, w=["idf"])
        P.pool("affine_select", out=idf, in_=idf, pattern=[[-1, 128]], compare_op=ALU.is_equal, fill=0.0,
                                         base=0, channel_multiplier=1, r=["idf"], w=["idf"])
        P.dve("tensor_copy", out=identb[:], in_=idf, r=["idf"], w=["identb"])
        trf = scrF[:, 128:256]
        P.pool("memset", trf, 1.0, w=["trf"])
        P.pool("affine_select", out=trf, in_=trf, pattern=[[1, 128]], compare_op=ALU.is_ge, fill=0.0,
                                         base=0, channel_multiplier=-1, r=["trf"], w=["trf"])
        P.dve("tensor_copy", out=trimask[:], in_=trf, r=["trf"], w=["trimask"])
        P.act("activation", out=cond, in_=ctf, func=AF.Silu, r=["ctf"], w=["cond"])
        P.dve("tensor_copy", out=condb[:], in_=cond, r=["cond"], w=["condb"])
        P.dve("tensor_copy", out=cond_rep[:], in_=cond.unsqueeze(2).broadcast_to([128, 8, 128]),
              r=["cond"], w=["cond_rep"])
        R1f = R1[:, 0:8 * S].bitcast(F32)
        tmpa = R1f[:, 0:S]
        tmpk = R1f[:, S:2 * S]
        tmpki = R1[:, 0:8 * S].bitcast(I32)[:, 2 * S:3 * S]
        tmpkf = R1f[:, 3 * S:4 * S]
        P.dma("sp", ctab[:].bitcast(I32), pos_in.partition_broadcast(128), w=["ctab"])
        P.dve("tensor_copy", out=stab[:], in_=ctab[:].bitcast(I32), r=["ctab"], w=["stab"])
        P.dve("tensor_scalar", out=stab[:], in0=stab[:], scalar1=cst[:, 0:1], scalar2=None, op0=ALU.mult,
              r=["stab", "cst"], w=["stab"])

        def range_reduce_sin(dst, shift, scale_ap):
            P.dve("tensor_scalar", out=tmpa, in0=stab[:], scalar1=float(shift), scalar2=None, op0=ALU.add,
                  r=["stab"], w=["tmpa"])
            P.dve("tensor_scalar", out=tmpk, in0=tmpa, scalar1=float(1.0 / (2 * math.pi)), scalar2=None,
                                            op0=ALU.mult, r=["tmpa"], w=["tmpk"])
            P.dve("tensor_copy", out=tmpki, in_=tmpk, r=["tmpk"], w=["tmpki"])
            P.dve("tensor_copy", out=tmpkf, in_=tmpki, r=["tmpki"], w=["tmpkf"])
            P.dve("scalar_tensor_tensor", out=tmpa, in0=tmpkf, scalar=float(-2 * math.pi), in1=tmpa,
                                                   op0=ALU.mult, op1=ALU.add, r=["tmpkf", "tmpa"], w=["tmpa"])
            P.dve("tensor_scalar", out=tmpk, in0=tmpa, scalar1=float(math.pi), scalar2=float(-2 * math.pi),
                                            op0=ALU.is_gt, op1=ALU.mult, r=["tmpa"], w=["tmpk"])
            P.dve("tensor_tensor", out=tmpa, in0=tmpa, in1=tmpk, op=ALU.add, r=["tmpa", "tmpk"], w=["tmpa"])
            P.dve("tensor_scalar", out=tmpk, in0=tmpa, scalar1=float(-math.pi), scalar2=float(2 * math.pi),
                                            op0=ALU.is_lt, op1=ALU.mult, r=["tmpa"], w=["tmpk"])
            P.dve("tensor_tensor", out=tmpa, in0=tmpa, in1=tmpk, op=ALU.add, r=["tmpa", "tmpk"], w=["tmpa"])
            if scale_ap is None:
                P.act("activation", out=dst, in_=tmpa, func=AF.Sin, r=["tmpa"], w=[dst.name if False else "tab"])
            else:
                P.act("activation", out=dst, in_=tmpa, func=AF.Sin, scale=scale_ap, r=["tmpa", "cst"], w=["tab"])

        range_reduce_sin(ctab[:], math.pi / 2, None)
        P.barrier("setup1")
        range_reduce_sin(stab[:], 0.0, cst[:, 1:2])
        P.barrier("setup2")

        def load_w(dst, src, key):
            P.dma("pool", dst, src, w=[key])

        def rstd_from(ss, n, out_col, tag):
            l1 = tmpc[:, 10:11]
            P.act("activation", out=l1, in_=ss, func=AF.Ln, scale=1.0 / n, bias=eps_col, r=[tag, "cst"], w=["l1"])
            P.act("activation", out=out_col, in_=l1, func=AF.Exp, scale=-0.5, r=["l1"], w=[tag + "r"])

        def ln_phase(xsrc, sc, bi, key_sc):
            junk = scrB[:, 2048:3072]
            xnbs = [scrB[:, 0:1024], scrB[:, 1024:2048]]
            xts = [scrF[:, 0:1024], scrF[:, 1024:2048]]
            banks = [[pb[i][:].bitcast(BF16) for i in range(4)],
                     [pb[4][:].bitcast(BF16), pb[5][:].bitcast(BF16), pt[0][:], pt[1][:]]]
            bkeys = [["pb0", "pb1", "pb2", "pb3"], ["pb4", "pb5", "pt0", "pt1"]]

            def load(t):
                P.dma("sp", xts[t % 2], xsrc[t * 128:(t + 1) * 128, :], w=[("xt", t % 2)])
            load(0)
            for gI in range(NT // 4):
                bs, ks = banks[gI % 2], bkeys[gI % 2]
                for ti in range(4):
                    t = 4 * gI + ti
                    par = t % 2
                    if t + 1 < NT:
                        load(t + 1)
                    xt, xnb = xts[par], xnbs[par]
                    ss = tmpc[:, 5 + par:6 + par]
                    rs = tmpc[:, 7 + par:8 + par]
                    P.act("activation", out=junk, in_=xt, func=AF.Square, accum_out=ss, r=[("xt", par)], w=["junk", ("lss", par)])
                    l1 = tmpc[:, 10:11]
                    P.act("activation", out=l1, in_=ss, func=AF.Ln, scale=1.0 / D, bias=eps_col, r=[("lss", par), "cst"], w=["l1"])
                    P.act("activation", out=rs, in_=l1, func=AF.Exp, scale=-0.5, r=["l1"], w=[("lrs", par)])
                    P.dve("tensor_scalar", out=xnb, in0=xt, scalar1=rs, scalar2=None, op0=ALU.mult,
                          r=[("xt", par), ("lrs", par)], w=[("xnb", par)])
                    for c in range(8):
                        o0 = (c % 2) * 512 + ti * 128
                        P.pe("transpose", bs[c // 2][:, o0:o0 + 128], xnb[:, c * 128:(c + 1) * 128], identb[:],
                             r=[("xnb", par), "identb"], w=[ks[c // 2]])
                for c in range(8):
                    b_ = c // 2
                    src = bs[b_][:, (c % 2) * 512:(c % 2) * 512 + 512]
                    dst = hT[:, c, gI * 512:(gI + 1) * 512]
                    wk = [("h", c, 4 * gI + i) for i in range(4)]
                    if b_ % 2 == 0:
                        P.act("activation", out=dst, in_=src, func=AF.Identity, scale=sc[:, c:c + 1], bias=bi[:, c:c + 1],
                              r=[ks[b_], key_sc], w=wk)
                    else:
                        P.dve("tensor_scalar", out=dst, in0=src, scalar1=sc[:, c:c + 1], scalar2=bi[:, c:c + 1],
                              op0=ALU.mult, op1=ALU.add, r=[ks[b_], key_sc], w=wk)

        def wpiece(w_ap, l, c0, ncol, nk=8):
            return w_ap[l].rearrange("(k p) n -> p k n", p=128)[:, :, c0:c0 + ncol]

        def dump(name, src, keys):
            if name in dbg_out:
                P.dma("sp", dbg_out[name], src, r=keys)

        for l in range(L):
            xsrc = x_in if l == 0 else xlay[l - 1]
            xdst = y_out if l == L - 1 else xlay[l]
            lam_init = 0.8 - 0.6 * math.exp(-0.3 * l)
            for kc in range(8):
                load_w(wm[:, kc, :], w_mod[l, kc * 128:(kc + 1) * 128, :], ("wm", kc))
            P.dma("sp", bmt, b_modT[l], w=["bmt"])
            P.dma("sp", nmx, nmixT[l], w=["nmx"])
            P.dma("sp", nfx, nffnT[l], w=["nfx"])
            P.dma("sp", gcols, gcols_in[l], w=["gcols"])
            P.dma("sp", mon_bc, mon_in[l].partition_broadcast(128), w=["mon_bc"])
            P.dma("sp", subln_bc, subln_in[l].partition_broadcast(128), w=["subln_bc"])
            P.dma("sp", scrF[:, 0:256], dlam_in[l].partition_broadcast(128), w=["dl"])
            for j in range(48):
                for kc in range(8):
                    P.pe("matmul", pb[0][:, j:j + 1], lhsT=wm[:, kc, j * 128:(j + 1) * 128],
                                                        rhs=condb[:, kc:kc + 1], start=(kc == 0), stop=(kc == 7),
                         r=[("wm", kc), "condb"], w=["pb0"])
            P.dve("tensor_tensor", out=modT, in0=pb[0][:, 0:48], in1=bmt, op=ALU.add, r=["pb0", "bmt"], w=["modT"])
            P.dve("scalar_tensor_tensor", out=sc1, in0=modT[:, 8:16], scalar=1.0, in1=nmx, op0=ALU.add, op1=ALU.mult,
                  r=["modT", "nmx"], w=["sc1"])
            P.dve("tensor_copy", out=bi1, in_=modT[:, 0:8], r=["modT"], w=["sc1"])
            P.dve("scalar_tensor_tensor", out=sc2, in0=modT[:, 32:40], scalar=1.0, in1=nfx, op0=ALU.add, op1=ALU.mult,
                  r=["modT", "nfx"], w=["sc2"])
            P.dve("tensor_copy", out=bi2, in_=modT[:, 24:32], r=["modT"], w=["sc2"])
            for gi, (gt, jj) in enumerate(((gate_a, 2), (gate_f, 5))):
                for half in range(2):
                    c0 = jj * D + half * 512
                    pbi = 1 + half
                    bb = scrF[:, 512 + half * 512:1024 + half * 512]
                    P.dma("sp", bb, b_mod[l][:, c0:c0 + 512].partition_broadcast(128), w=[("bb", half)])
                    for kc in range(8):
                        P.pe("matmul", pb[pbi][:], lhsT=cond_rep[:, kc, :],
                                                                      rhs=wm[:, kc, c0:c0 + 512], start=(kc == 0),
                                                                      stop=(kc == 7),
                             r=[("wm", kc), "cond_rep"], w=[f"pb{pbi}"])
                    P.dve("tensor_tensor",
                        out=gt[:, half * 512:(half + 1) * 512], in0=pb[pbi][:], in1=bb, op=ALU.add,
                        r=[f"pb{pbi}", ("bb", half)], w=[("gate", gi, half)])
            dl = scrF[:, 0:256]
            pr = scrF[:, 256:384]
            P.dve("tensor_tensor", out=pr[:, 0:64], in0=dl[:, 0:64], in1=dl[:, 64:128], op=ALU.mult, r=["dl"], w=["pr"])
            P.dve("tensor_tensor", out=pr[:, 64:128], in0=dl[:, 128:192], in1=dl[:, 192:256], op=ALU.mult,
                  r=["dl"], w=["pr"])
            P.dve("tensor_reduce", out=lamt[:, 0:2], in_=pr.rearrange("p (a b) -> p a b", a=2), axis=AX.X,
                                            op=ALU.add, r=["pr"], w=["lamt"])
            P.act("activation", out=lamt[:, 2:4], in_=lamt[:, 0:2], func=AF.Exp, r=["lamt"], w=["lamt"])
            P.dve("tensor_tensor", out=lamt[:, 4:5], in0=lamt[:, 2:3], in1=lamt[:, 3:4], op=ALU.subtract,
                  r=["lamt"], w=["lamt"])
            P.dve("tensor_scalar", out=lamcol, in0=lamt[:, 4:5], scalar1=float(lam_init), scalar2=None, op0=ALU.add,
                  r=["lamt"], w=["lamcol"])
            P.dve("tensor_scalar", out=subln_bc, in0=subln_bc, scalar1=float(1.0 - lam_init), scalar2=None,
                                            op0=ALU.mult, r=["subln_bc"], w=["subln_bc"])
            P.barrier("M")
            ln_phase(xsrc, sc1, bi1, "sc1")
            P.barrier("A")
            dump(f"d_hT{l}", R1[:], [])
            P.pool("memset", Vm[:, :, :, 64:65], 1.0, w=["Vones"])
            P.pool("memset", Vd[:, :, :, 128:129], 1.0, w=["Vones"])
            pieces = [(0, 0, 0), (512, 4, 1), (1536, 8, 2), (2048, 12, 3)]
            allcols = [0, 512, 1536, 2048, 1024, 2560]
            wAs = [WS[:, sl * 4096:(sl + 1) * 4096].rearrange("p (k n) -> p k n", k=8) for sl in range(2)]

            def load_piece(idx):
                if idx < len(allcols):
                    load_w(wAs[idx % 2], wpiece(w_in, l, allcols[idx], 512), ("wA", idx % 2))
            load_piece(0)
            btiles = [(pi, cb, gi, cc, tg) for pi, (col0, cb, gi) in enumerate(pieces) for cc in range(4) for tg in range(NG)]

            def b_mm(n):
                pi, cb, gi, cc, tg = btiles[n]
                if cc == 0 and tg == 0:
                    load_piece(pi + 1)
                par = n % 2
                wA = wAs[pi % 2]
                tok = slice(tg * 512, (tg + 1) * 512)
                for kc in range(8):
                    P.pe("matmul", pb[par][:], lhsT=wA[:, kc, cc * 128:(cc + 1) * 128], rhs=hT[:, kc, tok], start=(kc == 0),
                         stop=(kc == 7), r=[("wA", pi % 2)] + [("h", kc, 4 * tg + i) for i in range(4)], w=[f"pb{par}"])

            def b_aux(n):
                pi, cb, gi, cc, tg = btiles[n]
                par = n % 2
                pq, pss, psw = pb[par], pb[2 + par], pb[4 + par]
                kq, ks, kw = f"pb{par}", f"pb{2 + par}", f"pb{4 + par}"
                tok = slice(tg * 512, (tg + 1) * 512)
                sqb = scrB[:, par * 512:(par + 1) * 512]
                qgb = scrB[:, 1024 + par * 512:1024 + (par + 1) * 512]
                lnr = scrF[:, par * 1536:par * 1536 + 512]
                rstd = lnr
                qgf = scrF[:, par * 1536 + 512:par * 1536 + 1024]
                t1 = qgf
                t2 = scrF[:, par * 1536 + 1024:par * 1536 + 1536]
                gcol = gcols[:, gi:gi + 1]
                P.act("activation", out=sqb, in_=pq[:], func=AF.Square, r=[kq], w=[("sqb", par)])
                P.act("activation", out=qgf, in_=pq[:], func=AF.Copy, scale=gcol, r=[kq, "gcols"], w=[("qgf", par)])
                P.dve("tensor_copy", out=qgb, in_=qgf, r=[("qgf", par)], w=[("qgb", par)])
                P.pe("matmul", pss[:], lhsT=bonesb[:], rhs=sqb, start=True, stop=True, r=["bonesb", ("sqb", par)], w=[ks])
                P.pe("matmul", psw[:], lhsT=pmb[:], rhs=qgb, start=True, stop=True, r=["pmb", ("qgb", par)], w=[kw])
                P.act("activation", out=lnr, in_=pss[:], func=AF.Ln, bias=eps_col, r=[ks, "cst"], w=[("lnr", par)])
                P.act("activation", out=rstd, in_=lnr, func=AF.Exp, scale=-0.5, r=[("lnr", par)], w=[("lnr", par)])
                P.dve("tensor_tensor", out=t1, in0=qgf, in1=ctab[:, tok], op=ALU.mult, r=[("qgf", par), "tab"], w=[("qgf", par)])
                P.dve("tensor_tensor", out=t2, in0=psw[:], in1=stab[:, tok], op=ALU.mult, r=[kw, "tab"], w=[("t2", par)])
                P.pool("tensor_tensor", out=t1, in0=t1, in1=t2, op=ALU.add, r=[("qgf", par), ("t2", par)], w=[("qgf", par)])
                P.dve("tensor_tensor", out=qkT[:, cb + cc, tok], in0=t1, in1=rstd, op=ALU.mult,
                      r=[("qgf", par), ("lnr", par)], w=[("qk", cb + cc, tg)])

            b_mm(0)
            for n in range(len(btiles)):
                if n + 1 < len(btiles):
                    b_mm(n + 1)
                b_aux(n)
            for vi, (col0, isd) in enumerate(((1024, False), (2560, True))):
                wA = wAs[vi % 2]
                wkey = ("wA", vi % 2)
                load_piece(4 + vi + 1)
                for t in range(NT):
                    par = t % 2
                    pv = pb[par]
                    for kc in range(8):
                        P.pe("matmul",
                            pv[:], lhsT=hT[:, kc, t * 128:(t + 1) * 128], rhs=wA[:, kc, :], start=(kc == 0), stop=(kc == 7),
                            r=[wkey, ("h", kc, t)], w=[f"pb{par}"])
                    if isd:
                        dst = Vd[:, t, :, 0:128]
                        src = pv[:].rearrange("p (h e) -> p h e", h=4)
                    else:
                        dst = Vm[:, t, :, 0:64]
                        src = pv[:].rearrange("p (h e) -> p h e", h=8)
                    if t % 2 == 0:
                        P.act("activation", out=dst, in_=src, func=AF.Copy,
                              r=[f"pb{par}"], w=[("V", t, isd)])
                    else:
                        P.dve("tensor_copy", out=dst, in_=src, r=[f"pb{par}"], w=[("V", t, isd)])
            for c in range(4):
                P.dve("tensor_reduce", out=kmf[:, c, 0:NB], in_=qkT[:, 4 + c, :].rearrange("p (b t) -> p b t", t=256),
                                                     axis=AX.X, op=ALU.add,
                      r=[("qk", 4 + c, tg) for tg in range(NG)], w=["kmf"])
            P.dve("tensor_scalar", out=kmT[:, :, 0:NB], in0=kmf[:, :, 0:NB], scalar1=1.0 / 256, scalar2=None,
                                            op0=ALU.mult, r=["kmf"], w=["kmT"])
            P.barrier("B")
            dump(f"d_qk{l}", R2[:, 0:16 * S], [])
            dump(f"d_V{l}", R2[:, 16 * S:16 * S + NT * VW], [])
            oT = hT
            otok = scrB[:, 0:4096].rearrange("p (j f) -> p j f", j=4)
            pTb = [scrB[:, 4096 + i * 512:4096 + (i + 1) * 512] for i in range(4)]
            biasT = scrB[:, 6144:6656]
            biasq2 = scrB[:, 6656:6784].rearrange("p (d h n) -> p d h n", d=2, h=8)
            biasq = biasq2[:, 0]
            ocs = scrF[:, 0:1024].rearrange("p (j c e) -> p j c e", j=4, c=2)
            gs = scrF[:, 1280:1344].rearrange("p (h n) -> p h n", h=8)
            cnt = scrF[:, 1344:1408].rearrange("p (h n) -> p h n", h=8)
            cmpb = scrF[:, 1408:1920]
            ss4, ls4, rs4, rd4 = small[:, 432:436], small[:, 436:440], small[:, 440:444], small[:, 444:448]
            pt_i = 0
            st_i = 0
            stb = [pb[0], pb[1], pt[0][:].bitcast(F32)]
            stk = ["pb0", "pb1", "pt0"]
            for g in range(NG):
                masked_g = (2 * g + 1) >= 4
                import os
                atdbg = int(os.environ.get("ATDBG", "0"))
                if masked_g and atdbg in (2,):
                    P.dve("memset", biasT, 0.0, w=["biasT"])
                if masked_g and atdbg not in (2, 3):
                    for j in range(4):
                        qt = 4 * g + j
                        qb = qt // 2
                        for h in range(8):
                            base = (h % 2) * 64
                            P.pe("matmul", pb[h % 2][:, h * 8:h * 8 + NB],
                                 lhsT=qkT[base:base + 64, h // 2, qt * 128:(qt + 1) * 128],
                                 rhs=kmT[base:base + 64, h // 2, 0:NB], start=True, stop=True,
                                 r=[("qk", h // 2, g), "kmT"], w=[f"pb{h % 2}"])
                        for par2 in range(2):
                            P.dve("tensor_copy", out=gs[:, par2::2, :],
                                  in_=pb[par2][:, 0:64].rearrange("p (h n) -> p h n", h=8)[:, par2::2, :],
                                  r=[f"pb{par2}"], w=["gs"])
                        cmpv = cmpb[:, 0:8 * qb * qb].rearrange("p (h n m) -> p h n m", h=8, n=qb)
                        in0 = gs[:, :, 0:qb].unsqueeze(2).broadcast_to([128, 8, qb, qb])
                        in1 = gs[:, :, 0:qb].unsqueeze(3).broadcast_to([128, 8, qb, qb])
                        P.dve("tensor_tensor", out=cmpv, in0=in0, in1=in1, op=ALU.is_gt,
                              r=["gs"], w=["cmpb"])
                        P.dve("tensor_reduce", out=cnt[:, :, 0:qb], in_=cmpv, axis=AX.X, op=ALU.add,
                              r=["cmpb"], w=["cnt"])
                        P.dve("memset", scrB[:, 6656:6784], 0.0, w=["biasq"])
                        for d2 in range(2):
                            P.dve("tensor_scalar", out=biasq2[:, d2, :, 0:qb], in0=cnt[:, :, 0:qb], scalar1=2.5,
                                  scalar2=-BIG, op0=ALU.is_gt, op1=ALU.mult, r=["cnt"], w=["biasq"])
                        P.pe("transpose", pt[1][:, 0:128], scrB[:, 6656:6784], identb[:],
                             r=["biasq", "identb"], w=["pt1"])
                        P.dve("tensor_copy", out=biasT[:, j * 128:(j + 1) * 128], in_=pt[1][:, 0:128],
                              r=["pt1"], w=["biasT"])
                nkt = 4 * g + 4
                items = [(hp, kt) for hp in range(16) for kt in range(nkt)]
                info = {}

                def pass_cfg(hp):
                    isd = hp >= 8
                    hh = hp - 8 if isd else hp
                    base = (hh % 2) * 64
                    qc = (8 if isd else 0) + hh // 2
                    kc_ = (12 if isd else 4) + hh // 2
                    if isd:
                        vh, comp, dv = hh // 2, hh % 2, 128
                    else:
                        vh, comp, dv = hh, 0, 64
                    return isd, hh, base, qc, kc_, vh, comp, dv

                def emit_qk(idx):
                    nonlocal st_i, pt_i
                    hp, kt = items[idx]
                    isd, hh, base, qc, kc_, vh, comp, dv = pass_cfg(hp)
                    i = kt - 4 * g
                    c0 = 128 * max(i, 0)
                    ncol = 512 - c0
                    sp_ = st_i % 3
                    st_i += 1
                    ps_ = stb[sp_]
                    pk = stk[sp_]
                    mask_here = (not isd) and masked_g and (kt // 2) < (2 * g + 1) and atdbg not in (1, 3)
                    P.pe("matmul", ps_[:, 0:ncol], lhsT=qkT[base:base + 64, kc_, kt * 128:(kt + 1) * 128],
                         rhs=qkT[base:base + 64, qc, g * 512 + c0:(g + 1) * 512], start=True, stop=not mask_here,
                         r=[("qk", kc_, kt // 4), ("qk", qc, g)], w=[pk])
                    if mask_here:
                        n = kt // 2
                        col = base + 8 * hh + n
                        P.pe("matmul", ps_[:, 0:ncol],
                             lhsT=identb[base:base + 64, col:col + 1].broadcast_to([64, 128]),
                             rhs=biasT[base:base + 64, c0:512], start=False, stop=True,
                             r=["identb", "biasT"], w=[pk])
                    info[idx] = (ps_, pk, pt_i % 4, i, c0, ncol)
                    pt_i += 1

                def finalize(hp):
                    isd, hh, base, qc, kc_, vh, comp, dv = pass_cfg(hp)
                    for j in range(4):
                        acc = pb[2 + j]
                        ak = f"pb{2 + j}"
                        rd = rd4[:, j:j + 1]
                        P.dve("reciprocal", out=rd, in_=acc[:, dv:dv + 1], r=[ak], w=[("rd", j)])
                        if isd and comp == 1:
                            P.dve("tensor_tensor", out=rd, in0=rd, in1=lamcol, op=ALU.mult, r=[("rd", j), "lamcol"], w=[("rd", j)])
                        P.dve("tensor_scalar", out=ocs[:, j, comp, 0:dv], in0=acc[:, 0:dv], scalar1=rd, scalar2=None,
                              op0=ALU.mult, r=[ak, ("rd", j)], w=[("ocs", comp)])
                    if isd and comp == 0:
                        return
                    o0 = ocs[:, :, 0, 0:dv]
                    o1 = ocs[:, :, 1, 0:dv]
                    if isd:
                        P.dve("tensor_tensor", out=o0, in0=o0, in1=o1, op=ALU.subtract, r=[("ocs", 0), ("ocs", 1)], w=[("ocs", 0)])
                    P.dve("tensor_tensor", out=o1, in0=o0, in1=o0, op=ALU.mult, r=[("ocs", 0)], w=[("ocs", 1)])
                    P.dve("tensor_reduce", out=ss4, in_=o1, axis=AX.X, op=ALU.add, r=[("ocs", 1)], w=["ss4"])
                    P.act("activation", out=ls4, in_=ss4, func=AF.Ln, scale=1.0 / dv, bias=eps_col, r=["ss4", "cst"], w=["ls4"])
                    P.act("activation", out=rs4, in_=ls4, func=AF.Exp, scale=-0.5, r=["ls4"], w=["rs4"])
                    P.dve("tensor_tensor", out=o1, in0=o0, in1=rs4.unsqueeze(2).broadcast_to([128, 4, dv]), op=ALU.mult,
                          r=[("ocs", 0), "rs4"], w=[("ocs", 1)])
                    if isd:
                        dst = otok[:, :, 512 + vh * 128:512 + (vh + 1) * 128]
                        gbc = subln_bc.unsqueeze(1).broadcast_to([128, 4, 128])
                        gk = "subln_bc"
                    else:
                        dst = otok[:, :, hh * 64:(hh + 1) * 64]
                        gbc = mon_bc.unsqueeze(1).broadcast_to([128, 4, 64])
                        gk = "mon_bc"
                    P.dve("tensor_tensor", out=dst, in0=o1, in1=gbc, op=ALU.mult, r=[("ocs", 1), gk], w=["otok"])

                def emit_rest(idx):
                    hp, kt = items[idx]
                    isd, hh, base, qc, kc_, vh, comp, dv = pass_cfg(hp)
                    ps_, pk, pti, i, c0, ncol = info[idx]
                    pT = pTb[pti]
                    ptk = ("pT", pti)
                    P.act("activation", out=pT[:, 0:ncol], in_=ps_[:, 0:ncol], func=AF.Exp, scale=0.125, r=[pk], w=[ptk])
                    if i >= 0:
                        P.dve("tensor_tensor", out=pT[:, 0:128], in0=pT[:, 0:128], in1=trimask[:], op=ALU.mult,
                              r=[ptk, "trimask"], w=[ptk])
                    for j in range(max(i, 0), 4):
                        off = (j - max(i, 0)) * 128
                        vr = Vd[:, kt, vh, :] if isd else Vm[:, kt, vh, :]
                        P.pe("matmul", pb[2 + j][:, 0:dv + 1], lhsT=pT[:, off:off + 128], rhs=vr, start=(kt == 0),
                             stop=(kt == 4 * g + j), r=[ptk, ("V", kt, isd)], w=[f"pb{2 + j}"])
                    if kt == nkt - 1:
                        finalize(hp)

                emit_qk(0)
                emit_qk(1)
                for idx in range(len(items)):
                    if idx + 2 < len(items):
                        emit_qk(idx + 2)
                    emit_rest(idx)
                for j in range(4):
                    t = 4 * g + j
                    for c in range(8):
                        P.pe("transpose", pt[1][:, c * 128:(c + 1) * 128], otok[:, j, c * 128:(c + 1) * 128],
                                                             identb[:], r=["otok", "identb"], w=["pt1"])
                    P.act("activation", out=oT[:, :, t * 128:(t + 1) * 128],
                                                      in_=pt[1][:].rearrange("p (c s) -> p c s", c=8), func=AF.Copy,
                          r=["pt1"], w=[("o", t)])
            P.barrier("C")
            dump(f"d_oT{l}", R1[:], [])
            wO = [WS[:, hf * 4096:(hf + 1) * 4096].rearrange("p (k n) -> p k n", k=8) for hf in range(2)]
            for hf in range(2):
                load_w(wO[hf], wpiece(w_out, l, hf * 512, 512), ("wA", hf))
            ditems = [(hf, t) for hf in range(2) for t in range(NT)]

            def d_load(n):
                hf, t = ditems[n]
                P.dma("sp", scrF[:, (n % 4) * 512:(n % 4 + 1) * 512], xsrc[t * 128:(t + 1) * 128, hf * 512:(hf + 1) * 512],
                      w=[("xh", n % 4)])
            d_load(0)
            d_load(1)
            for n, (hf, t) in enumerate(ditems):
                if n + 2 < len(ditems):
                    d_load(n + 2)
                par = n % 2
                xh = scrF[:, (n % 4) * 512:(n % 4 + 1) * 512]
                yh = scrF[:, 2048 + par * 512:2048 + (par + 1) * 512]
                py = pb[par]
                for kc in range(8):
                    P.pe("matmul", py[:], lhsT=oT[:, kc, t * 128:(t + 1) * 128], rhs=wO[hf][:, kc, :], start=(kc == 0),
                         stop=(kc == 7), r=[("wA", hf), ("o", t)], w=[f"pb{par}"])
                P.dve("tensor_tensor", out=yh, in0=py[:], in1=gate_a[:, hf * 512:(hf + 1) * 512], op=ALU.mult,
                      r=[f"pb{par}", ("gate", 0, hf)], w=[("yh", par)])
                P.pool("tensor_tensor", out=xh, in0=xh, in1=yh, op=ALU.add, r=[("xh", n % 4), ("yh", par)], w=[("xh", n % 4)])
                P.dma("sp", xmid[l][t * 128:(t + 1) * 128, hf * 512:(hf + 1) * 512], xh, r=[("xh", n % 4)], w=[("xm", t, hf)])
            P.barrier("D")
            ln_phase(xmid[l], sc2, bi2, "sc2")
            P.barrier("E")
            npiece = DFF // 256
            def wgu(slot):
                return (WS[:, slot * 4096:slot * 4096 + 2048].rearrange("p (k n) -> p k n", k=8),
                        WS[:, slot * 4096 + 2048:slot * 4096 + 4096].rearrange("p (k n) -> p k n", k=8))

            def load_gu(pi):
                if pi < npiece:
                    a, b = wgu(pi % 2)
                    load_w(a, wpiece(w_gate, l, pi * 256, 256), ("wg", pi % 2))
                    load_w(b, wpiece(w_up, l, pi * 256, 256), ("wu", pi % 2))
            load_gu(0)
            for pi in range(npiece):
                slot = pi % 2
                wg, wu = wgu(slot)
                load_gu(pi + 1)
                for cc in range(2):
                    jf = 2 * pi + cc
                    for tg in range(NG):
                        par = (jf * NG + tg) % 2
                        pg, pu = pb[par], pb[2 + par]
                        tok = slice(tg * 512, (tg + 1) * 512)
                        for kc in range(8):
                            P.pe("matmul",
                                pg[:], lhsT=wg[:, kc, cc * 128:(cc + 1) * 128], rhs=hT[:, kc, tok], start=(kc == 0), stop=(kc == 7),
                                r=[("wg", slot)] + [("h", kc, 4 * tg + i) for i in range(4)], w=[f"pb{par}"])
                        for kc in range(8):
                            P.pe("matmul",
                                pu[:], lhsT=wu[:, kc, cc * 128:(cc + 1) * 128], rhs=hT[:, kc, tok], start=(kc == 0), stop=(kc == 7),
                                r=[("wu", slot)] + [("h", kc, 4 * tg + i) for i in range(4)], w=[f"pb{2 + par}"])
                        sg = scrB[:, par * 512:(par + 1) * 512]
                        P.act("activation", out=sg, in_=pg[:], func=AF.Silu, r=[f"pb{par}"], w=[("sg", par)])
                        P.dve("tensor_tensor", out=actT[:, jf, tok], in0=pu[:], in1=sg, op=ALU.mult,
                              r=[f"pb{2 + par}", ("sg", par)], w=[("a", jf, tg)])
            P.barrier("F")
            wD = R1[:, 0:NFF * 512].rearrange("p (j n) -> p j n", j=NFF)
            gitems = [(hf, t) for hf in range(2) for t in range(NT)]

            def g_load(n):
                hf, t = gitems[n]
                P.dma("sp", scrF[:, (n % 4) * 512:(n % 4 + 1) * 512], xmid[l][t * 128:(t + 1) * 128, hf * 512:(hf + 1) * 512],
                      r=[("xm", t, hf)], w=[("xh", n % 4)])
            g_load(0)
            g_load(1)
            for n, (hf, t) in enumerate(gitems):
                if t == 0:
                    load_w(wD, w_down[l].rearrange("(j p) n -> p j n", p=128)[:, :, hf * 512:(hf + 1) * 512], "wD")
                if n + 2 < len(gitems):
                    g_load(n + 2)
                par = n % 2
                xh = scrF[:, (n % 4) * 512:(n % 4 + 1) * 512]
                yh = scrF[:, 2048 + par * 512:2048 + (par + 1) * 512]
                py = pb[par]
                for jf in range(NFF):
                    P.pe("matmul", py[:], lhsT=actT[:, jf, t * 128:(t + 1) * 128], rhs=wD[:, jf, :], start=(jf == 0),
                         stop=(jf == NFF - 1), r=["wD", ("a", jf, t // 4)], w=[f"pb{par}"])
                P.dve("tensor_tensor", out=yh, in0=py[:], in1=gate_f[:, hf * 512:(hf + 1) * 512], op=ALU.mult,
                      r=[f"pb{par}", ("gate", 1, hf)], w=[("yh", par)])
                P.pool("tensor_tensor", out=xh, in0=xh, in1=yh, op=ALU.add, r=[("xh", n % 4), ("yh", par)], w=[("xh", n % 4)])
                P.dma("sp", xdst[t * 128:(t + 1) * 128, hf * 512:(hf + 1) * 512], xh, r=[("xh", n % 4)], w=[("xo", t, hf)])
            P.barrier("G")

        if stop:
            P.ops = P.ops[:P.marks[stop]]
        with nc.Block() as block:
            bodies = P.bodies(sems)
            block.sync(bodies["sp"])
            block.tensor(bodies["pe"])
            block.scalar(bodies["act"])
            block.vector(bodies["dve"])
            block.gpsimd(bodies["pool"])
    return nc


def make_in_maps(inputs, S=2048, L=2, cores=8):
    f32 = np.float32
    x = np.asarray(inputs["x"], f32)
    c = np.asarray(inputs["c"], f32)
    pos = np.asarray(inputs["positions"], np.int32)
    cst, pm, bones = host_consts()

    def colsT(v, n):
        v = np.asarray(v, f32)
        return np.ascontiguousarray(v.reshape(v.shape[0], n, 128).transpose(0, 2, 1))

    def tile2(v):
        v = np.asarray(v, f32)
        return np.concatenate([v, v], axis=1)

    gcols = np.stack([tile2(inputs["moba_q_norm"]), tile2(inputs["moba_k_norm"]),
                      tile2(inputs["diff_q_norm"]), tile2(inputs["diff_k_norm"])], axis=2)
    shared = {
        "w_mod": np.ascontiguousarray(np.asarray(inputs["w_mod"], f32)[:L]),
        "b_modT": colsT(np.asarray(inputs["b_mod"])[:L], 48),
        "b_mod": np.ascontiguousarray(np.asarray(inputs["b_mod"], f32)[:L, None, :]),
        "nmixT": colsT(np.asarray(inputs["norm_mix"])[:L], 8),
        "nffnT": colsT(np.asarray(inputs["norm_ffn"])[:L], 8),
        "w_in": np.ascontiguousarray(np.asarray(inputs["w_in"], f32)[:L]),
        "gcols": np.ascontiguousarray(gcols[:L]),
        "mon": np.ascontiguousarray(np.asarray(inputs["moba_out_norm"], f32)[:L, None, :]),
        "subln": np.ascontiguousarray(np.asarray(inputs["diff_subln"], f32)[:L, None, :]),
        "dlam": np.ascontiguousarray(np.asarray(inputs["diff_lambda"], f32)[:L].reshape(L, 1, 256)),
        "w_out": np.ascontiguousarray(np.asarray(inputs["w_out"], f32)[:L]),
        "w_gate": np.ascontiguousarray(np.asarray(inputs["w_gate"], f32)[:L]),
        "w_up": np.ascontiguousarray(np.asarray(inputs["w_up"], f32)[:L]),
        "w_down": np.ascontiguousarray(np.asarray(inputs["w_down"], f32)[:L]),
        "cst": cst, "pmat": pm, "bones": bones,
    }
    maps = []
    for b in range(cores):
        m = dict(shared)
        m["x"] = np.ascontiguousarray(x[b, :S])
        m["cT"] = np.ascontiguousarray(c[b].reshape(8, 128).T)
        m["pos"] = np.ascontiguousarray(pos[b:b + 1, :S])
        maps.append(m)
    return maps


def kernel(**inputs):
    S, L = 2048, 2
    nc = build(S, L)
    maps = make_in_maps(inputs, S, L, 8)
    res = run_bass_kernel_spmd(nc, maps, core_ids=list(range(8)))
    return np.stack([np.asarray(r["y"], np.float32) for r in res.results], axis=0)
```

```python
import math
from contextlib import ExitStack
import numpy as np
import concourse.bass as bass
import concourse.mybir as mybir
from concourse.bass_utils import run_bass_kernel_spmd

F32 = mybir.dt.float32
BF16 = mybir.dt.bfloat16
I32 = mybir.dt.int32
ALU = mybir.AluOpType
AF = mybir.ActivationFunctionType
AX = mybir.AxisListType

D = 1024
DFF = 2816
NFF = DFF // 128
EPS = 1e-6
BIG = 30000.0
NDMA = {"sp": 20, "pool": 12}
COMPUTE = ("pe", "act", "dve", "pool")


class Op:
    __slots__ = ("eng", "fn", "reads", "writes", "dma", "waits", "ev")

    def __init__(self, eng, fn, reads, writes, dma):
        self.eng, self.fn, self.reads, self.writes, self.dma = eng, fn, reads, writes, dma
        self.waits = []
        self.ev = None


class Prog:
    def __init__(self):
        self.ops = []
        self.marks = {}

    def add(self, eng, name, args, kwargs, reads=(), writes=(), dma=False):
        def fn(e, name=name, args=args, kwargs=kwargs):
            return getattr(e, name)(*args, **kwargs)
        self.ops.append(Op(eng, fn, tuple(reads), tuple(writes), dma))

    def pe(self, name, *args, r=(), w=(), **kw):
        self.add("pe", name, args, kw, r, w)

    def act(self, name, *args, r=(), w=(), **kw):
        self.add("act", name, args, kw, r, w)

    def dve(self, name, *args, r=(), w=(), **kw):
        self.add("dve", name, args, kw, r, w)

    def pool(self, name, *args, r=(), w=(), **kw):
        self.add("pool", name, args, kw, r, w)

    def dma(self, q, out, in_, r=(), w=()):
        self.add(q, "dma_start", (), dict(out=out, in_=in_), r, w, dma=True)

    def barrier(self, mark=None):
        self.ops.append("BARRIER")
        if mark:
            self.marks[mark] = len(self.ops)

    def analyze(self):
        cnt = {e: 0 for e in COMPUTE}
        dcnt = {q: 0 for q in NDMA}
        last_w, readers = {}, {}
        seen = {e: {} for e in ("pe", "act", "dve", "pool", "sp")}
        all_events = {}

        def need(op, ev):
            if ev is None:
                return
            key, val = ev
            if key == "pe" and op.eng == "pe" and not op.dma:
                return
            s = seen[op.eng]
            if s.get(key, 0) >= val:
                return
            s[key] = val
            op.waits.append((key, val))

        out = []
        for op in self.ops:
            if op == "BARRIER":
                b = Op("all", None, (), (), False)
                b.waits = dict(all_events)
                for e in seen:
                    for k, v in all_events.items():
                        if seen[e].get(k, 0) < v:
                            seen[e][k] = v
                out.append(b)
                continue
            if op.dma:
                q = op.eng
                i = dcnt[q]
                dcnt[q] += 1
                key = ("dma", q, i % NDMA[q])
                val = 16 * (i // NDMA[q] + 1)
                if val > 16:
                    need(op, (key, val - 16))
                op.ev = (key, val)
            else:
                cnt[op.eng] += 1
                op.ev = (op.eng, cnt[op.eng])
            for r in op.reads:
                need(op, last_w.get(r))
            for w in op.writes:
                need(op, last_w.get(w))
                for ev in readers.get(w, ()):
                    need(op, ev)
            for r in op.reads:
                readers.setdefault(r, []).append(op.ev)
            for w in op.writes:
                last_w[w] = op.ev
                readers[w] = []
            all_events[op.ev[0]] = max(all_events.get(op.ev[0], 0), op.ev[1])
            out.append(op)
        self.sched = out
        self.final_events = dict(all_events)

    def bodies(self, sems):
        self.analyze()
        sched, final_events = self.sched, self.final_events

        def body_for(ename):
            def body(eng):
                seen = {}
                for op in sched:
                    if op.eng == "all":
                        for k, v in op.waits.items():
                            if k != ename and seen.get(k, 0) < v:
                                eng.wait_ge(sems[k], v)
                                seen[k] = v
                        continue
                    if op.eng != ename:
                        continue
                    for k, v in op.waits:
                        eng.wait_ge(sems[k], v)
                        seen[k] = max(seen.get(k, 0), v)
                    inst = op.fn(eng)
                    inst.then_inc(sems[op.ev[0]], 16 if op.dma else 1)
                if ename == "sp":
                    for k, v in final_events.items():
                        if seen.get(k, 0) < v:
                            eng.wait_ge(sems[k], v)
            return body

        return {e: body_for(e) for e in ("pe", "act", "dve", "pool", "sp")}


def all_sem_keys():
    keys = list(COMPUTE)
    for q, n in NDMA.items():
        keys += [("dma", q, s) for s in range(n)]
    return keys


def host_consts():
    f = np.arange(128)
    d = f % 64
    inv = (500000.0 ** (-np.arange(0, 16, 2, dtype=np.float32) / 16.0)).astype(np.float32)
    cst = np.zeros((128, 8), np.float32)
    cst[:, 0] = np.where(d < 16, inv[d % 8], 0.0)
    cst[:, 1] = np.where(d < 8, -1.0, np.where(d < 16, 1.0, 0.0))
    cst[:, 2] = EPS
    cst[:, 3] = 1.0
    pm = np.zeros((128, 128), np.float32)
    for m in range(128):
        dm = m % 64
        if dm < 8:
            pm[m + 8, m] = 1.0
        elif dm < 16:
            pm[m - 8, m] = 1.0
    bones = np.zeros((128, 128), np.float32)
    bones[:64, :64] = 1.0 / 64
    bones[64:, 64:] = 1.0 / 64
    return cst, pm, bones


def build(S=2048, L=2, dbg=(), stop=None):
    NT, NG, NB = S // 128, S // 512, S // 256
    VW = 8 * 65 + 4 * 129
    nc = bass.Bass("TRN2", target_bir_lowering=False)

    def din(name, shape, dt=F32):
        return nc.dram_tensor(name, list(shape), dt, kind="ExternalInput").ap()

    x_in = din("x", [S, D])
    cT_in = din("cT", [128, 8])
    pos_in = din("pos", [1, S], I32)
    w_mod = din("w_mod", [L, D, 6 * D])
    b_modT = din("b_modT", [L, 128, 48])
    b_mod = din("b_mod", [L, 1, 6 * D])
    nmixT = din("nmixT", [L, 128, 8])
    nffnT = din("nffnT", [L, 128, 8])
    w_in = din("w_in", [L, D, 3072])
    gcols_in = din("gcols", [L, 128, 4])
    mon_in = din("mon", [L, 1, 64])
    subln_in = din("subln", [L, 1, 128])
    dlam_in = din("dlam", [L, 1, 256])
    w_out = din("w_out", [L, D, D])
    w_gate = din("w_gate", [L, D, DFF])
    w_up = din("w_up", [L, D, DFF])
    w_down = din("w_down", [L, DFF, D])
    cst_in = din("cst", [128, 8])
    pm_in = din("pmat", [128, 128])
    bones_in = din("bones", [128, 128])
    y_out = nc.dram_tensor("y", [S, D], F32, kind="ExternalOutput").ap()
    xmid = [nc.dram_tensor(f"xmid{l}", [S, D], F32, kind="Internal").ap() for l in range(L)]
    xlay = [nc.dram_tensor(f"xlay{l}", [S, D], F32, kind="Internal").ap() for l in range(L - 1)]
    dbg_out = {}
    for name, shape, dt in dbg:
        dbg_out[name] = nc.dram_tensor(name, list(shape), dt, kind="ExternalOutput").ap()

    es = ExitStack()
    with es:
        def sb(name, shape, dt):
            return es.enter_context(nc.sbuf_tensor(name, list(shape), dt))

        def psum(name, shape, dt):
            return es.enter_context(nc.psum_tensor(name, list(shape), dt))

        R1 = sb("R1", [128, max(8 * S, (NFF + 6) * 512)], BF16)
        R2N = max(16 * S + NT * VW, NFF * S, 8 * 6144 if S >= 512 else 0)
        R2 = sb("R2", [128, R2N], BF16)
        ctab = sb("ctab", [128, S], F32)
        stab = sb("stab", [128, S], F32)
        gate_a = sb("gate_a", [128, D], F32)
        gate_f = sb("gate_f", [128, D], F32)
        scrF = sb("scrF", [128, 3072], F32)
        scrB = sb("scrB", [128, 8192], BF16)
        WS = sb("WS", [128, 8192], BF16)
        identb = sb("identb", [128, 128], BF16)
        trimask = sb("trimask", [128, 128], BF16)
        pmb = sb("pmb", [128, 128], BF16)
        bonesb = sb("bonesb", [128, 128], BF16)
        cst = sb("cst_sb", [128, 8], F32)
        small = sb("small", [128, 512], F32)
        cond_rep = sb("cond_rep", [128, 8, 128], BF16)
        condb = sb("condb", [128, 8], BF16)
        kmT = sb("kmT", [128, 4, 8], BF16)
        pb = [psum(f"pb{i}", [128, 512], F32) for i in range(6)]
        pt = [psum(f"pt{i}", [128, 1024], BF16) for i in range(2)]
        sems = {}
        for k in all_sem_keys():
            nm = "s_" + ("_".join(map(str, k)) if isinstance(k, tuple) else k)
            sems[k] = es.enter_context(nc.semaphore(nm))

        P = Prog()
        hT = R1[:, 0:8 * S].rearrange("p (c s) -> p c s", c=8)
        qkT = R2[:, 0:16 * S].rearrange("p (c s) -> p c s", c=16)
        Vall = R2[:, 16 * S:16 * S + NT * VW].rearrange("p (t w) -> p t w", t=NT)
        Vm = Vall[:, :, 0:520].rearrange("p t (h e) -> p t h e", h=8)
        Vd = Vall[:, :, 520:VW].rearrange("p t (h e) -> p t h e", h=4)
        actT = R2[:, 0:NFF * S].rearrange("p (j s) -> p j s", j=NFF)
        wm = R2[:, 0:8 * 6144].rearrange("p (k n) -> p k n", k=8)
        modT = small[:, 0:48]
        sc1, bi1, sc2, bi2 = small[:, 48:56], small[:, 56:64], small[:, 64:72], small[:, 72:80]
        gcols = small[:, 80:84]
        lamcol = small[:, 84:85]
        tmpc = small[:, 85:96]
        nmx = small[:, 96:104]
        nfx = small[:, 104:112]
        bmt = small[:, 112:160]
        mon_bc = small[:, 160:224]
        subln_bc = small[:, 224:352]
        cond = small[:, 352:360]
        ctf = small[:, 360:368]
        lamt = small[:, 368:400]
        kmf = small[:, 400:432].rearrange("p (c n) -> p c n", c=4)
        eps_col = cst[:, 2:3]

        for kc in range(8):
            P.dma("pool", wm[:, kc, :], w_mod[0, kc * 128:(kc + 1) * 128, :], w=[("wm", kc)])
        P.dma("sp", cst[:], cst_in, w=["cst"])
        P.dma("sp", ctf, cT_in, w=["ctf"])
        P.dma("pool", pmb[:], pm_in, w=["pmb"])
        P.dma("pool", bonesb[:], bones_in, w=["bonesb"])
        idf = scrF[:, 0:128]
        P.pool("memset", idf, 1.0, w=["idf"])
        P.pool("affine_select", out=idf, in_=idf, pattern=[[-1, 128]], compare_op=ALU.is_equal, fill=0.0,
                                         base=0, channel_multiplier=1, r=["idf"], w=["idf"])
        P.dve("tensor_copy", out=identb[:], in_=idf, r=["idf"], w=["identb"])
        trf = scrF[:, 128:256]
        P.pool("memset", trf, 1.0, w=["trf"])
        P.pool("affine_select", out=trf, in_=trf, pattern=[[1, 128]], compare_op=ALU.is_ge, fill=0.0,
                                         base=0, channel_multiplier=-1, r=["trf"], w=["trf"])
        P.dve("tensor_copy", out=trimask[:], in_=trf, r=["trf"], w=["trimask"])
        P.act("activation", out=cond, in_=ctf, func=AF.Silu, r=["ctf"], w=["cond"])
        P.dve("tensor_copy", out=condb[:], in_=cond, r=["cond"], w=["condb"])
        P.dve("tensor_copy", out=cond_rep[:], in_=cond.unsqueeze(2).broadcast_to([128, 8, 128]),
              r=["cond"], w=["cond_rep"])
        R1f = R1[:, 0:8 * S].bitcast(F32)
        tmpa = R1f[:, 0:S]
        tmpk = R1f[:, S:2 * S]
        tmpki = R1[:, 0:8 * S].bitcast(I32)[:, 2 * S:3 * S]
        tmpkf = R1f[:, 3 * S:4 * S]
        P.dma("sp", ctab[:].bitcast(I32), pos_in.partition_broadcast(128), w=["ctab"])
        P.dve("tensor_copy", out=stab[:], in_=ctab[:].bitcast(I32), r=["ctab"], w=["stab"])
        P.dve("tensor_scalar", out=stab[:], in0=stab[:], scalar1=cst[:, 0:1], scalar2=None, op0=ALU.mult,
              r=["stab", "cst"], w=["stab"])

        def range_reduce_sin(dst, shift, scale_ap):
            P.dve("tensor_scalar", out=tmpa, in0=stab[:], scalar1=float(shift), scalar2=None, op0=ALU.add,
                  r=["stab"], w=["tmpa"])
            P.dve("tensor_scalar", out=tmpk, in0=tmpa, scalar1=float(1.0 / (2 * math.pi)), scalar2=None,
                                            op0=ALU.mult, r=["tmpa"], w=["tmpk"])
            P.dve("tensor_copy", out=tmpki, in_=tmpk, r=["tmpk"], w=["tmpki"])
            P.dve("tensor_copy", out=tmpkf, in_=tmpki, r=["tmpki"], w=["tmpkf"])
            P.dve("scalar_tensor_tensor", out=tmpa, in0=tmpkf, scalar=float(-2 * math.pi), in1=tmpa,
                                                   op0=ALU.mult, op1=ALU.add, r=["tmpkf", "tmpa"], w=["tmpa"])
            P.dve("tensor_scalar", out=tmpk, in0=tmpa, scalar1=float(math.pi), scalar2=float(-2 * math.pi),
                                            op0=ALU.is_gt, op1=ALU.mult, r=["tmpa"], w=["tmpk"])
            P.dve("tensor_tensor", out=tmpa, in0=tmpa, in1=tmpk, op=ALU.add, r=["tmpa", "tmpk"], w=["tmpa"])
            P.dve("tensor_scalar", out=tmpk, in0=tmpa, scalar1=float(-math.pi), scalar2=float(2 * math.pi),
                                            op0=ALU.is_lt, op1=ALU.mult, r=["tmpa"], w=["tmpk"])
            P.dve("tensor_tensor", out=tmpa, in0=tmpa, in1=tmpk, op=ALU.add, r=["tmpa", "tmpk"], w=["tmpa"])
            if scale_ap is None:
                P.act("activation", out=dst, in_=tmpa, func=AF.Sin, r=["tmpa"], w=[dst.name if False else "tab"])
            else:
                P.act("activation", out=dst, in_=tmpa, func=AF.Sin, scale=scale_ap, r=["tmpa", "cst"], w=["tab"])

        range_reduce_sin(ctab[:], math.pi / 2, None)
        P.barrier("setup1")
        range_reduce_sin(stab[:], 0.0, cst[:, 1:2])
        P.barrier("setup2")

        def load_w(dst, src, key):
            P.dma("pool", dst, src, w=[key])

        def rstd_from(ss, n, out_col, tag):
            l1 = tmpc[:, 10:11]
            P.act("activation", out=l1, in_=ss, func=AF.Ln, scale=1.0 / n, bias=eps_col, r=[tag, "cst"], w=["l1"])
            P.act("activation", out=out_col, in_=l1, func=AF.Exp, scale=-0.5, r=["l1"], w=[tag + "r"])

        def ln_phase(xsrc, sc, bi, key_sc):
            junk = scrB[:, 2048:3072]
            xnbs = [scrB[:, 0:1024], scrB[:, 1024:2048]]
            xts = [scrF[:, 0:1024], scrF[:, 1024:2048]]
            banks = [[pb[i][:].bitcast(BF16) for i in range(4)],
                     [pb[4][:].bitcast(BF16), pb[5][:].bitcast(BF16), pt[0][:], pt[1][:]]]
            bkeys = [["pb0", "pb1", "pb2", "pb3"], ["pb4", "pb5", "pt0", "pt1"]]

            def load(t):
                P.dma("sp", xts[t % 2], xsrc[t * 128:(t + 1) * 128, :], w=[("xt", t % 2)])
            load(0)
            for gI in range(NT // 4):
                bs, ks = banks[gI % 2], bkeys[gI % 2]
                for ti in range(4):
                    t = 4 * gI + ti
                    par = t % 2
                    if t + 1 < NT:
                        load(t + 1)
                    xt, xnb = xts[par], xnbs[par]
                    ss = tmpc[:, 5 + par:6 + par]
                    rs = tmpc[:, 7 + par:8 + par]
                    P.act("activation", out=junk, in_=xt, func=AF.Square, accum_out=ss, r=[("xt", par)], w=["junk", ("lss", par)])
                    l1 = tmpc[:, 10:11]
                    P.act("activation", out=l1, in_=ss, func=AF.Ln, scale=1.0 / D, bias=eps_col, r=[("lss", par), "cst"], w=["l1"])
                    P.act("activation", out=rs, in_=l1, func=AF.Exp, scale=-0.5, r=["l1"], w=[("lrs", par)])
                    P.dve("tensor_scalar", out=xnb, in0=xt, scalar1=rs, scalar2=None, op0=ALU.mult,
                          r=[("xt", par), ("lrs", par)], w=[("xnb", par)])
                    for c in range(8):
                        o0 = (c % 2) * 512 + ti * 128
                        P.pe("transpose", bs[c // 2][:, o0:o0 + 128], xnb[:, c * 128:(c + 1) * 128], identb[:],
                             r=[("xnb", par), "identb"], w=[ks[c // 2]])
                for c in range(8):
                    b_ = c // 2
                    src = bs[b_][:, (c % 2) * 512:(c % 2) * 512 + 512]
                    dst = hT[:, c, gI * 512:(gI + 1) * 512]
                    wk = [("h", c, 4 * gI + i) for i in range(4)]
                    if b_ % 2 == 0:
                        P.act("activation", out=dst, in_=src, func=AF.Identity, scale=sc[:, c:c + 1], bias=bi[:, c:c + 1],
                              r=[ks[b_], key_sc], w=wk)
                    else:
                        P.dve("tensor_scalar", out=dst, in0=src, scalar1=sc[:, c:c + 1], scalar2=bi[:, c:c + 1],
                              op0=ALU.mult, op1=ALU.add, r=[ks[b_], key_sc], w=wk)

        def wpiece(w_ap, l, c0, ncol, nk=8):
            return w_ap[l].rearrange("(k p) n -> p k n", p=128)[:, :, c0:c0 + ncol]

        def dump(name, src, keys):
            if name in dbg_out:
                P.dma("sp", dbg_out[name], src, r=keys)

        for l in range(L):
            xsrc = x_in if l == 0 else xlay[l - 1]
            xdst = y_out if l == L - 1 else xlay[l]
            lam_init = 0.8 - 0.6 * math.exp(-0.3 * l)
            if l > 0:
                for kc in range(8):
                    load_w(wm[:, kc, :], w_mod[l, kc * 128:(kc + 1) * 128, :], ("wm", kc))
            P.dma("sp", bmt, b_modT[l], w=["bmt"])
            P.dma("sp", nmx, nmixT[l], w=["nmx"])
            P.dma("sp", nfx, nffnT[l], w=["nfx"])
            P.dma("sp", gcols, gcols_in[l], w=["gcols"])
            P.dma("sp", mon_bc, mon_in[l].partition_broadcast(128), w=["mon_bc"])
            P.dma("sp", subln_bc, subln_in[l].partition_broadcast(128), w=["subln_bc"])
            P.dma("sp", scrF[:, 0:256], dlam_in[l].partition_broadcast(128), w=["dl"])
            for j in range(48):
                for kc in range(8):
                    P.pe("matmul", pb[0][:, j:j + 1], lhsT=wm[:, kc, j * 128:(j + 1) * 128],
                                                        rhs=condb[:, kc:kc + 1], start=(kc == 0), stop=(kc == 7),
                         r=[("wm", kc), "condb"], w=["pb0"])
            P.dve("tensor_tensor", out=modT, in0=pb[0][:, 0:48], in1=bmt, op=ALU.add, r=["pb0", "bmt"], w=["modT"])
            P.dve("scalar_tensor_tensor", out=sc1, in0=modT[:, 8:16], scalar=1.0, in1=nmx, op0=ALU.add, op1=ALU.mult,
                  r=["modT", "nmx"], w=["sc1"])
            P.dve("tensor_copy", out=bi1, in_=modT[:, 0:8], r=["modT"], w=["sc1"])
            P.dve("scalar_tensor_tensor", out=sc2, in0=modT[:, 32:40], scalar=1.0, in1=nfx, op0=ALU.add, op1=ALU.mult,
                  r=["modT", "nfx"], w=["sc2"])
            P.dve("tensor_copy", out=bi2, in_=modT[:, 24:32], r=["modT"], w=["sc2"])
            for gi, (gt, jj) in enumerate(((gate_a, 2), (gate_f, 5))):
                for half in range(2):
                    c0 = jj * D + half * 512
                    pbi = 1 + half
                    bb = scrF[:, 512 + half * 512:1024 + half * 512]
                    P.dma("sp", bb, b_mod[l][:, c0:c0 + 512].partition_broadcast(128), w=[("bb", half)])
                    for kc in range(8):
                        P.pe("matmul", pb[pbi][:], lhsT=cond_rep[:, kc, :],
                                                                      rhs=wm[:, kc, c0:c0 + 512], start=(kc == 0),
                                                                      stop=(kc == 7),
                             r=[("wm", kc), "cond_rep"], w=[f"pb{pbi}"])
                    P.dve("tensor_tensor",
                        out=gt[:, half * 512:(half + 1) * 512], in0=pb[pbi][:], in1=bb, op=ALU.add,
                        r=[f"pb{pbi}", ("bb", half)], w=[("gate", gi, half)])
            dl = scrF[:, 0:256]
            pr = scrF[:, 256:384]
            P.dve("tensor_tensor", out=pr[:, 0:64], in0=dl[:, 0:64], in1=dl[:, 64:128], op=ALU.mult, r=["dl"], w=["pr"])
            P.dve("tensor_tensor", out=pr[:, 64:128], in0=dl[:, 128:192], in1=dl[:, 192:256], op=ALU.mult,
                  r=["dl"], w=["pr"])
            P.dve("tensor_reduce", out=lamt[:, 0:2], in_=pr.rearrange("p (a b) -> p a b", a=2), axis=AX.X,
                                            op=ALU.add, r=["pr"], w=["lamt"])
            P.act("activation", out=lamt[:, 2:4], in_=lamt[:, 0:2], func=AF.Exp, r=["lamt"], w=["lamt"])
            P.dve("tensor_tensor", out=lamt[:, 4:5], in0=lamt[:, 2:3], in1=lamt[:, 3:4], op=ALU.subtract,
                  r=["lamt"], w=["lamt"])
            P.dve("tensor_scalar", out=lamcol, in0=lamt[:, 4:5], scalar1=float(lam_init), scalar2=None, op0=ALU.add,
                  r=["lamt"], w=["lamcol"])
            P.dve("tensor_scalar", out=subln_bc, in0=subln_bc, scalar1=float(1.0 - lam_init), scalar2=None,
                                            op0=ALU.mult, r=["subln_bc"], w=["subln_bc"])
            P.barrier("M")
            pieces = [(0, 0, 0), (512, 4, 1), (1536, 8, 2), (2048, 12, 3)]
            allcols = [0, 512, 1536, 2048, 1024, 2560]
            wAs = [WS[:, sl * 4096:(sl + 1) * 4096].rearrange("p (k n) -> p k n", k=8) for sl in range(2)]

            def load_piece(idx):
                if idx < len(allcols):
                    load_w(wAs[idx % 2], wpiece(w_in, l, allcols[idx], 512), ("wA", idx % 2))
            load_piece(0)
            ln_phase(xsrc, sc1, bi1, "sc1")
            P.barrier("A")
            dump(f"d_hT{l}", R1[:], [])
            P.pool("memset", Vm[:, :, :, 64:65], 1.0, w=["Vones"])
            P.pool("memset", Vd[:, :, :, 128:129], 1.0, w=["Vones"])
            btiles = [(pi, cb, gi, cc, tg) for pi, (col0, cb, gi) in enumerate(pieces) for cc in range(4) for tg in range(NG)]

            def b_mm(n):
                pi, cb, gi, cc, tg = btiles[n]
                if cc == 0 and tg == 0:
                    load_piece(pi + 1)
                par = n % 2
                wA = wAs[pi % 2]
                tok = slice(tg * 512, (tg + 1) * 512)
                for kc in range(8):
                    P.pe("matmul", pb[par][:], lhsT=wA[:, kc, cc * 128:(cc + 1) * 128], rhs=hT[:, kc, tok], start=(kc == 0),
                         stop=(kc == 7), r=[("wA", pi % 2)] + [("h", kc, 4 * tg + i) for i in range(4)], w=[f"pb{par}"])

            def b_bufs(n):
                pi, cb, gi, cc, tg = btiles[n]
                par = n % 2
                d = dict(par=par, pq=pb[par], pss=pb[2 + par], psw=pb[4 + par], kq=f"pb{par}", ks=f"pb{2 + par}",
                         kw=f"pb{4 + par}", tok=slice(tg * 512, (tg + 1) * 512),
                         sqb=scrB[:, par * 512:(par + 1) * 512], qgb=scrB[:, 1024 + par * 512:1024 + (par + 1) * 512],
                         lnr=scrF[:, par * 1536:par * 1536 + 512], qgf=scrF[:, par * 1536 + 512:par * 1536 + 1024],
                         t2=scrF[:, par * 1536 + 1024:par * 1536 + 1536], gcol=gcols[:, gi:gi + 1], dst=(cb + cc, tg))
                return d

            def b_s2(n):
                d = b_bufs(n)
                par = d["par"]
                P.act("activation", out=d["sqb"], in_=d["pq"][:], func=AF.Square, r=[d["kq"]], w=[("sqb", par)])
                P.act("activation", out=d["qgf"], in_=d["pq"][:], func=AF.Copy, scale=d["gcol"], r=[d["kq"], "gcols"], w=[("qgf", par)])
                P.dve("tensor_copy", out=d["qgb"], in_=d["qgf"], r=[("qgf", par)], w=[("qgb", par)])
                P.pe("matmul", d["pss"][:], lhsT=bonesb[:], rhs=d["sqb"], start=True, stop=True, r=["bonesb", ("sqb", par)], w=[d["ks"]])
                P.pe("matmul", d["psw"][:], lhsT=pmb[:], rhs=d["qgb"], start=True, stop=True, r=["pmb", ("qgb", par)], w=[d["kw"]])

            def b_s3(n):
                d = b_bufs(n)
                par, tok, lnr, qgf, t2 = d["par"], d["tok"], d["lnr"], d["qgf"], d["t2"]
                P.act("activation", out=lnr, in_=d["pss"][:], func=AF.Ln, bias=eps_col, r=[d["ks"], "cst"], w=[("lnr", par)])
                P.act("activation", out=lnr, in_=lnr, func=AF.Exp, scale=-0.5, r=[("lnr", par)], w=[("lnr", par)])
                P.dve("tensor_tensor", out=qgf, in0=qgf, in1=ctab[:, tok], op=ALU.mult, r=[("qgf", par), "tab"], w=[("qgf", par)])
                P.dve("tensor_tensor", out=t2, in0=d["psw"][:], in1=stab[:, tok], op=ALU.mult, r=[d["kw"], "tab"], w=[("t2", par)])
                P.pool("tensor_tensor", out=qgf, in0=qgf, in1=t2, op=ALU.add, r=[("qgf", par), ("t2", par)], w=[("qgf", par)])
                P.dve("tensor_tensor", out=qkT[:, d["dst"][0], tok], in0=qgf, in1=lnr, op=ALU.mult,
                      r=[("qgf", par), ("lnr", par)], w=[("qk", d["dst"][0], d["dst"][1])])

            nb_ = len(btiles)
            b_mm(0)
            if nb_ > 1:
                b_mm(1)
            b_s2(0)
            for n in range(nb_):
                if n + 2 < nb_:
                    b_mm(n + 2)
                if n + 1 < nb_:
                    b_s2(n + 1)
                b_s3(n)
            for vi, (col0, isd) in enumerate(((1024, False), (2560, True))):
                wA = wAs[vi % 2]
                wkey = ("wA", vi % 2)
                load_piece(4 + vi + 1)
                for t in range(NT):
                    par = t % 2
                    pv = pb[par]
                    for kc in range(8):
                        P.pe("matmul",
                            pv[:], lhsT=hT[:, kc, t * 128:(t + 1) * 128], rhs=wA[:, kc, :], start=(kc == 0), stop=(kc == 7),
                            r=[wkey, ("h", kc, t)], w=[f"pb{par}"])
                    if isd:
                        dst = Vd[:, t, :, 0:128]
                        src = pv[:].rearrange("p (h e) -> p h e", h=4)
                    else:
                        dst = Vm[:, t, :, 0:64]
                        src = pv[:].rearrange("p (h e) -> p h e", h=8)
                    if t % 2 == 0:
                        P.act("activation", out=dst, in_=src, func=AF.Copy,
                              r=[f"pb{par}"], w=[("V", t, isd)])
                    else:
                        P.dve("tensor_copy", out=dst, in_=src, r=[f"pb{par}"], w=[("V", t, isd)])
            for c in range(4):
                P.dve("tensor_reduce", out=kmf[:, c, 0:NB], in_=qkT[:, 4 + c, :].rearrange("p (b t) -> p b t", t=256),
                                                     axis=AX.X, op=ALU.add,
                      r=[("qk", 4 + c, tg) for tg in range(NG)], w=["kmf"])
            P.dve("tensor_scalar", out=kmT[:, :, 0:NB], in0=kmf[:, :, 0:NB], scalar1=1.0 / 256, scalar2=None,
                                            op0=ALU.mult, r=["kmf"], w=["kmT"])
            P.barrier("B")
            dump(f"d_qk{l}", R2[:, 0:16 * S], [])
            dump(f"d_V{l}", R2[:, 16 * S:16 * S + NT * VW], [])
            oT = hT
            wO = [WS[:, hf * 4096:(hf + 1) * 4096].rearrange("p (k n) -> p k n", k=8) for hf in range(2)]
            for hf in range(2):
                load_w(wO[hf], wpiece(w_out, l, hf * 512, 512), ("wA", hf))
            otok = scrB[:, 0:4096].rearrange("p (j f) -> p j f", j=4)
            pTb = [scrB[:, 4096 + i * 512:4096 + (i + 1) * 512] for i in range(4)]
            biasT = scrB[:, 6144:6656]
            biasq2 = scrB[:, 6656:6784].rearrange("p (d h n) -> p d h n", d=2, h=8)
            biasq = biasq2[:, 0]
            ocs = scrF[:, 0:1024].rearrange("p (j c e) -> p j c e", j=4, c=2)
            gs = scrF[:, 1280:1344].rearrange("p (h n) -> p h n", h=8)
            cnt = scrF[:, 1344:1408].rearrange("p (h n) -> p h n", h=8)
            cmpb = scrF[:, 1408:1920]
            ss4, ls4, rs4, rd4 = small[:, 432:436], small[:, 436:440], small[:, 440:444], small[:, 444:448]
            pt_i = 0
            st_i = 0
            stb = [pb[0], pb[1], pt[0][:].bitcast(F32)]
            stk = ["pb0", "pb1", "pt0"]
            for g in range(NG):
                masked_g = (2 * g + 1) >= 4
                import os
                atdbg = int(os.environ.get("ATDBG", "0"))
                if masked_g and atdbg in (2,):
                    P.dve("memset", biasT, 0.0, w=["biasT"])
                if masked_g and atdbg not in (2, 3):
                    for j in range(4):
                        qt = 4 * g + j
                        qb = qt // 2
                        for h in range(8):
                            base = (h % 2) * 64
                            P.pe("matmul", pb[h % 2][:, h * 8:h * 8 + NB],
                                 lhsT=qkT[base:base + 64, h // 2, qt * 128:(qt + 1) * 128],
                                 rhs=kmT[base:base + 64, h // 2, 0:NB], start=True, stop=True,
                                 r=[("qk", h // 2, g), "kmT"], w=[f"pb{h % 2}"])
                        for par2 in range(2):
                            P.dve("tensor_copy", out=gs[:, par2::2, :],
                                  in_=pb[par2][:, 0:64].rearrange("p (h n) -> p h n", h=8)[:, par2::2, :],
                                  r=[f"pb{par2}"], w=["gs"])
                        cmpv = cmpb[:, 0:8 * qb * qb].rearrange("p (h n m) -> p h n m", h=8, n=qb)
                        in0 = gs[:, :, 0:qb].unsqueeze(2).broadcast_to([128, 8, qb, qb])
                        in1 = gs[:, :, 0:qb].unsqueeze(3).broadcast_to([128, 8, qb, qb])
                        P.dve("tensor_tensor", out=cmpv, in0=in0, in1=in1, op=ALU.is_gt,
                              r=["gs"], w=["cmpb"])
                        P.dve("tensor_reduce", out=cnt[:, :, 0:qb], in_=cmpv, axis=AX.X, op=ALU.add,
                              r=["cmpb"], w=["cnt"])
                        P.dve("memset", scrB[:, 6656:6784], 0.0, w=["biasq"])
                        for d2 in range(2):
                            P.dve("tensor_scalar", out=biasq2[:, d2, :, 0:qb], in0=cnt[:, :, 0:qb], scalar1=2.5,
                                  scalar2=-BIG, op0=ALU.is_gt, op1=ALU.mult, r=["cnt"], w=["biasq"])
                        P.pe("transpose", pt[1][:, 0:128], scrB[:, 6656:6784], identb[:],
                             r=["biasq", "identb"], w=["pt1"])
                        P.dve("tensor_copy", out=biasT[:, j * 128:(j + 1) * 128], in_=pt[1][:, 0:128],
                              r=["pt1"], w=["biasT"])
                nkt = 4 * g + 4
                items = [(hp, kt) for hp in range(16) for kt in range(nkt)]
                info = {}

                def pass_cfg(hp):
                    isd = hp >= 8
                    hh = hp - 8 if isd else hp
                    base = (hh % 2) * 64
                    qc = (8 if isd else 0) + hh // 2
                    kc_ = (12 if isd else 4) + hh // 2
                    if isd:
                        vh, comp, dv = hh // 2, hh % 2, 128
                    else:
                        vh, comp, dv = hh, 0, 64
                    return isd, hh, base, qc, kc_, vh, comp, dv

                def emit_qk(idx):
                    nonlocal st_i, pt_i
                    hp, kt = items[idx]
                    isd, hh, base, qc, kc_, vh, comp, dv = pass_cfg(hp)
                    i = kt - 4 * g
                    c0 = 128 * max(i, 0)
                    ncol = 512 - c0
                    sp_ = st_i % 3
                    st_i += 1
                    ps_ = stb[sp_]
                    pk = stk[sp_]
                    mask_here = (not isd) and masked_g and (kt // 2) < (2 * g + 1) and atdbg not in (1, 3)
                    P.pe("matmul", ps_[:, 0:ncol], lhsT=qkT[base:base + 64, kc_, kt * 128:(kt + 1) * 128],
                         rhs=qkT[base:base + 64, qc, g * 512 + c0:(g + 1) * 512], start=True, stop=not mask_here,
                         r=[("qk", kc_, kt // 4), ("qk", qc, g)], w=[pk])
                    if mask_here:
                        n = kt // 2
                        col = base + 8 * hh + n
                        P.pe("matmul", ps_[:, 0:ncol],
                             lhsT=identb[base:base + 64, col:col + 1].broadcast_to([64, 128]),
                             rhs=biasT[base:base + 64, c0:512], start=False, stop=True,
                             r=["identb", "biasT"], w=[pk])
                    info[idx] = (ps_, pk, pt_i % 4, i, c0, ncol)
                    pt_i += 1

                def finalize(hp):
                    isd, hh, base, qc, kc_, vh, comp, dv = pass_cfg(hp)
                    for j in range(4):
                        acc = pb[2 + j]
                        ak = f"pb{2 + j}"
                        rd = rd4[:, j:j + 1]
                        P.dve("reciprocal", out=rd, in_=acc[:, dv:dv + 1], r=[ak], w=[("rd", j)])
                        if isd and comp == 1:
                            P.dve("tensor_tensor", out=rd, in0=rd, in1=lamcol, op=ALU.mult, r=[("rd", j), "lamcol"], w=[("rd", j)])
                        P.dve("tensor_scalar", out=ocs[:, j, comp, 0:dv], in0=acc[:, 0:dv], scalar1=rd, scalar2=None,
                              op0=ALU.mult, r=[ak, ("rd", j)], w=[("ocs", comp)])
                    if isd and comp == 0:
                        return
                    o0 = ocs[:, :, 0, 0:dv]
                    o1 = ocs[:, :, 1, 0:dv]
                    if isd:
                        P.dve("tensor_tensor", out=o0, in0=o0, in1=o1, op=ALU.subtract, r=[("ocs", 0), ("ocs", 1)], w=[("ocs", 0)])
                    P.dve("tensor_tensor", out=o1, in0=o0, in1=o0, op=ALU.mult, r=[("ocs", 0)], w=[("ocs", 1)])
                    P.dve("tensor_reduce", out=ss4, in_=o1, axis=AX.X, op=ALU.add, r=[("ocs", 1)], w=["ss4"])
                    P.act("activation", out=ls4, in_=ss4, func=AF.Ln, scale=1.0 / dv, bias=eps_col, r=["ss4", "cst"], w=["ls4"])
                    P.act("activation", out=rs4, in_=ls4, func=AF.Exp, scale=-0.5, r=["ls4"], w=["rs4"])
                    P.dve("tensor_tensor", out=o1, in0=o0, in1=rs4.unsqueeze(2).broadcast_to([128, 4, dv]), op=ALU.mult,
                          r=[("ocs", 0), "rs4"], w=[("ocs", 1)])
                    if isd:
                        dst = otok[:, :, 512 + vh * 128:512 + (vh + 1) * 128]
                        gbc = subln_bc.unsqueeze(1).broadcast_to([128, 4, 128])
                        gk = "subln_bc"
                    else:
                        dst = otok[:, :, hh * 64:(hh + 1) * 64]
                        gbc = mon_bc.unsqueeze(1).broadcast_to([128, 4, 64])
                        gk = "mon_bc"
                    P.dve("tensor_tensor", out=dst, in0=o1, in1=gbc, op=ALU.mult, r=[("ocs", 1), gk], w=["otok"])

                def emit_rest(idx):
                    hp, kt = items[idx]
                    isd, hh, base, qc, kc_, vh, comp, dv = pass_cfg(hp)
                    ps_, pk, pti, i, c0, ncol = info[idx]
                    pT = pTb[pti]
                    ptk = ("pT", pti)
                    P.act("activation", out=pT[:, 0:ncol], in_=ps_[:, 0:ncol], func=AF.Exp, scale=0.125, r=[pk], w=[ptk])
                    if i >= 0:
                        P.dve("tensor_tensor", out=pT[:, 0:128], in0=pT[:, 0:128], in1=trimask[:], op=ALU.mult,
                              r=[ptk, "trimask"], w=[ptk])
                    for j in range(max(i, 0), 4):
                        off = (j - max(i, 0)) * 128
                        vr = Vd[:, kt, vh, :] if isd else Vm[:, kt, vh, :]
                        P.pe("matmul", pb[2 + j][:, 0:dv + 1], lhsT=pT[:, off:off + 128], rhs=vr, start=(kt == 0),
                             stop=(kt == 4 * g + j), r=[ptk, ("V", kt, isd)], w=[f"pb{2 + j}"])
                    if kt == nkt - 1:
                        finalize(hp)

                emit_qk(0)
                emit_qk(1)
                for idx in range(len(items)):
                    if idx + 2 < len(items):
                        emit_qk(idx + 2)
                    emit_rest(idx)
                for j in range(4):
                    t = 4 * g + j
                    for c in range(8):
                        P.pe("transpose", pt[1][:, c * 128:(c + 1) * 128], otok[:, j, c * 128:(c + 1) * 128],
                                                             identb[:], r=["otok", "identb"], w=["pt1"])
                    P.act("activation", out=oT[:, :, t * 128:(t + 1) * 128],
                                                      in_=pt[1][:].rearrange("p (c s) -> p c s", c=8), func=AF.Copy,
                          r=["pt1"], w=[("o", t)])
            P.barrier("C")
            dump(f"d_oT{l}", R1[:], [])
            ditems = [(hf, t) for hf in range(2) for t in range(NT)]

            def d_load(n):
                hf, t = ditems[n]
                P.dma("sp", scrF[:, (n % 4) * 512:(n % 4 + 1) * 512], xsrc[t * 128:(t + 1) * 128, hf * 512:(hf + 1) * 512],
                      w=[("xh", n % 4)])
            d_load(0)
            d_load(1)
            for n, (hf, t) in enumerate(ditems):
                if n + 2 < len(ditems):
                    d_load(n + 2)
                par = n % 2
                xh = scrF[:, (n % 4) * 512:(n % 4 + 1) * 512]
                yh = scrF[:, 2048 + par * 512:2048 + (par + 1) * 512]
                py = pb[par]
                for kc in range(8):
                    P.pe("matmul", py[:], lhsT=oT[:, kc, t * 128:(t + 1) * 128], rhs=wO[hf][:, kc, :], start=(kc == 0),
                         stop=(kc == 7), r=[("wA", hf), ("o", t)], w=[f"pb{par}"])
                P.dve("tensor_tensor", out=yh, in0=py[:], in1=gate_a[:, hf * 512:(hf + 1) * 512], op=ALU.mult,
                      r=[f"pb{par}", ("gate", 0, hf)], w=[("yh", par)])
                P.pool("tensor_tensor", out=xh, in0=xh, in1=yh, op=ALU.add, r=[("xh", n % 4), ("yh", par)], w=[("xh", n % 4)])
                P.dma("sp", xmid[l][t * 128:(t + 1) * 128, hf * 512:(hf + 1) * 512], xh, r=[("xh", n % 4)], w=[("xm", t, hf)])
            P.barrier("D")
            npiece = DFF // 256
            def wgu(slot):
                return (WS[:, slot * 4096:slot * 4096 + 2048].rearrange("p (k n) -> p k n", k=8),
                        WS[:, slot * 4096 + 2048:slot * 4096 + 4096].rearrange("p (k n) -> p k n", k=8))

            def load_gu(pi):
                if pi < npiece:
                    a, b = wgu(pi % 2)
                    load_w(a, wpiece(w_gate, l, pi * 256, 256), ("wg", pi % 2))
                    load_w(b, wpiece(w_up, l, pi * 256, 256), ("wu", pi % 2))
            load_gu(0)
            ln_phase(xmid[l], sc2, bi2, "sc2")
            P.barrier("E")
            for pi in range(npiece):
                slot = pi % 2
                wg, wu = wgu(slot)
                load_gu(pi + 1)
                for cc in range(2):
                    jf = 2 * pi + cc
                    for tg in range(NG):
                        par = (jf * NG + tg) % 2
                        pg, pu = pb[par], pb[2 + par]
                        tok = slice(tg * 512, (tg + 1) * 512)
                        for kc in range(8):
                            P.pe("matmul",
                                pg[:], lhsT=wg[:, kc, cc * 128:(cc + 1) * 128], rhs=hT[:, kc, tok], start=(kc == 0), stop=(kc == 7),
                                r=[("wg", slot)] + [("h", kc, 4 * tg + i) for i in range(4)], w=[f"pb{par}"])
                        for kc in range(8):
                            P.pe("matmul",
                                pu[:], lhsT=wu[:, kc, cc * 128:(cc + 1) * 128], rhs=hT[:, kc, tok], start=(kc == 0), stop=(kc == 7),
                                r=[("wu", slot)] + [("h", kc, 4 * tg + i) for i in range(4)], w=[f"pb{2 + par}"])
                        sg = scrB[:, par * 512:(par + 1) * 512]
                        P.act("activation", out=sg, in_=pg[:], func=AF.Silu, r=[f"pb{par}"], w=[("sg", par)])
                        P.dve("tensor_tensor", out=actT[:, jf, tok], in0=pu[:], in1=sg, op=ALU.mult,
                              r=[f"pb{2 + par}", ("sg", par)], w=[("a", jf, tg)])
            P.barrier("F")
            wD0 = R1[:, 0:NFF * 512].rearrange("p (j n) -> p j n", j=NFF)
            wD1a = WS[:].rearrange("p (j n) -> p j n", j=16)
            wD1b = R1[:, NFF * 512:NFF * 512 + 6 * 512].rearrange("p (j n) -> p j n", j=6)
            wdv = w_down[l].rearrange("(j p) n -> p j n", p=128)
            load_w(wD0, wdv[:, :, 0:512], ("wD", 0))
            load_w(wD1a, wdv[:, 0:16, 512:1024], ("wD", 1))
            load_w(wD1b, wdv[:, 16:22, 512:1024], ("wD", 2))

            def wd_slice(hf, jf):
                if hf == 0:
                    return wD0[:, jf, :], ("wD", 0)
                if jf < 16:
                    return wD1a[:, jf, :], ("wD", 1)
                return wD1b[:, jf - 16, :], ("wD", 2)
            gitems = [(hf, t) for hf in range(2) for t in range(NT)]

            def g_load(n):
                hf, t = gitems[n]
                P.dma("sp", scrF[:, (n % 4) * 512:(n % 4 + 1) * 512], xmid[l][t * 128:(t + 1) * 128, hf * 512:(hf + 1) * 512],
                      r=[("xm", t, hf)], w=[("xh", n % 4)])
            g_load(0)
            g_load(1)
            for n, (hf, t) in enumerate(gitems):
                if n + 2 < len(gitems):
                    g_load(n + 2)
                par = n % 2
                xh = scrF[:, (n % 4) * 512:(n % 4 + 1) * 512]
                yh = scrF[:, 2048 + par * 512:2048 + (par + 1) * 512]
                py = pb[par]
                for jf in range(NFF):
                    wsl, wkey = wd_slice(hf, jf)
                    P.pe("matmul", py[:], lhsT=actT[:, jf, t * 128:(t + 1) * 128], rhs=wsl, start=(jf == 0),
                         stop=(jf == NFF - 1), r=[wkey, ("a", jf, t // 4)], w=[f"pb{par}"])
                P.dve("tensor_tensor", out=yh, in0=py[:], in1=gate_f[:, hf * 512:(hf + 1) * 512], op=ALU.mult,
                      r=[f"pb{par}", ("gate", 1, hf)], w=[("yh", par)])
                P.pool("tensor_tensor", out=xh, in0=xh, in1=yh, op=ALU.add, r=[("xh", n % 4), ("yh", par)], w=[("xh", n % 4)])
                P.dma("sp", xdst[t * 128:(t + 1) * 128, hf * 512:(hf + 1) * 512], xh, r=[("xh", n % 4)], w=[("xo", t, hf)])
            P.barrier("G")

        if stop:
            P.ops = P.ops[:P.marks[stop]]
        with nc.Block() as block:
            bodies = P.bodies(sems)
            block.sync(bodies["sp"])
            block.tensor(bodies["pe"])
            block.scalar(bodies["act"])
            block.vector(bodies["dve"])
            block.gpsimd(bodies["pool"])
    return nc


def make_in_maps(inputs, S=2048, L=2, cores=8):
    f32 = np.float32
    x = np.asarray(inputs["x"], f32)
    c = np.asarray(inputs["c"], f32)
    pos = np.asarray(inputs["positions"], np.int32)
    cst, pm, bones = host_consts()

    def colsT(v, n):
        v = np.asarray(v, f32)
        return np.ascontiguousarray(v.reshape(v.shape[0], n, 128).transpose(0, 2, 1))

    def tile2(v):
        v = np.asarray(v, f32)
        return np.concatenate([v, v], axis=1)

    gcols = np.stack([tile2(inputs["moba_q_norm"]), tile2(inputs["moba_k_norm"]),
                      tile2(inputs["diff_q_norm"]), tile2(inputs["diff_k_norm"])], axis=2)
    shared = {
        "w_mod": np.ascontiguousarray(np.asarray(inputs["w_mod"], f32)[:L]),
        "b_modT": colsT(np.asarray(inputs["b_mod"])[:L], 48),
        "b_mod": np.ascontiguousarray(np.asarray(inputs["b_mod"], f32)[:L, None, :]),
        "nmixT": colsT(np.asarray(inputs["norm_mix"])[:L], 8),
        "nffnT": colsT(np.asarray(inputs["norm_ffn"])[:L], 8),
        "w_in": np.ascontiguousarray(np.asarray(inputs["w_in"], f32)[:L]),
        "gcols": np.ascontiguousarray(gcols[:L]),
        "mon": np.ascontiguousarray(np.asarray(inputs["moba_out_norm"], f32)[:L, None, :]),
        "subln": np.ascontiguousarray(np.asarray(inputs["diff_subln"], f32)[:L, None, :]),
        "dlam": np.ascontiguousarray(np.asarray(inputs["diff_lambda"], f32)[:L].reshape(L, 1, 256)),
        "w_out": np.ascontiguousarray(np.asarray(inputs["w_out"], f32)[:L]),
        "w_gate": np.ascontiguousarray(np.asarray(inputs["w_gate"], f32)[:L]),
        "w_up": np.ascontiguousarray(np.asarray(inputs["w_up"], f32)[:L]),
        "w_down": np.ascontiguousarray(np.asarray(inputs["w_down"], f32)[:L]),
        "cst": cst, "pmat": pm, "bones": bones,
    }
    maps = []
    for b in range(cores):
        m = dict(shared)
        m["x"] = np.ascontiguousarray(x[b, :S])
        m["cT"] = np.ascontiguousarray(c[b].reshape(8, 128).T)
        m["pos"] = np.ascontiguousarray(pos[b:b + 1, :S])
        maps.append(m)
    return maps


def kernel(**inputs):
    S, L = 2048, 2
    nc = build(S, L)
    maps = make_in_maps(inputs, S, L, 8)
    res = run_bass_kernel_spmd(nc, maps, core_ids=list(range(8)))
    return np.stack([np.asarray(r["y"], np.float32) for r in res.results], axis=0)
```

```python
import math
from contextlib import ExitStack
import numpy as np
import concourse.bass as bass
import concourse.mybir as mybir
from concourse.bass_utils import run_bass_kernel_spmd

F32 = mybir.dt.float32
BF16 = mybir.dt.bfloat16
I32 = mybir.dt.int32
ALU = mybir.AluOpType
AF = mybir.ActivationFunctionType
AX = mybir.AxisListType

D = 1024
DFF = 2816
NFF = DFF // 128
EPS = 1e-6
BIG = 30000.0
NDMA = {"sp": 20, "pool": 12}
COMPUTE = ("pe", "act", "dve", "pool")


class Op:
    __slots__ = ("eng", "fn", "reads", "writes", "dma", "waits", "ev")

    def __init__(self, eng, fn, reads, writes, dma):
        self.eng, self.fn, self.reads, self.writes, self.dma = eng, fn, reads, writes, dma
        self.waits = []
        self.ev = None


class Prog:
    def __init__(self):
        self.ops = []
        self.marks = {}

    def add(self, eng, name, args, kwargs, reads=(), writes=(), dma=False):
        def fn(e, name=name, args=args, kwargs=kwargs):
            return getattr(e, name)(*args, **kwargs)
        self.ops.append(Op(eng, fn, tuple(reads), tuple(writes), dma))

    def pe(self, name, *args, r=(), w=(), **kw):
        self.add("pe", name, args, kw, r, w)

    def act(self, name, *args, r=(), w=(), **kw):
        self.add("act", name, args, kw, r, w)

    def dve(self, name, *args, r=(), w=(), **kw):
        self.add("dve", name, args, kw, r, w)

    def pool(self, name, *args, r=(), w=(), **kw):
        self.add("pool", name, args, kw, r, w)

    def dma(self, q, out, in_, r=(), w=()):
        self.add(q, "dma_start", (), dict(out=out, in_=in_), r, w, dma=True)

    def barrier(self, mark=None):
        self.ops.append("BARRIER")
        if mark:
            self.marks[mark] = len(self.ops)

    def analyze(self):
        cnt = {e: 0 for e in COMPUTE}
        dcnt = {q: 0 for q in NDMA}
        last_w, readers = {}, {}
        seen = {e: {} for e in ("pe", "act", "dve", "pool", "sp")}
        all_events = {}

        def need(op, ev):
            if ev is None:
                return
            key, val = ev
            if key == "pe" and op.eng == "pe" and not op.dma:
                return
            s = seen[op.eng]
            if s.get(key, 0) >= val:
                return
            s[key] = val
            op.waits.append((key, val))

        out = []
        for op in self.ops:
            if op == "BARRIER":
                b = Op("all", None, (), (), False)
                b.waits = dict(all_events)
                for e in seen:
                    for k, v in all_events.items():
                        if seen[e].get(k, 0) < v:
                            seen[e][k] = v
                out.append(b)
                continue
            if op.dma:
                q = op.eng
                i = dcnt[q]
                dcnt[q] += 1
                key = ("dma", q, i % NDMA[q])
                val = 16 * (i // NDMA[q] + 1)
                if val > 16:
                    need(op, (key, val - 16))
                op.ev = (key, val)
            else:
                cnt[op.eng] += 1
                op.ev = (op.eng, cnt[op.eng])
            for r in op.reads:
                need(op, last_w.get(r))
            for w in op.writes:
                need(op, last_w.get(w))
                for ev in readers.get(w, ()):
                    need(op, ev)
            for r in op.reads:
                readers.setdefault(r, []).append(op.ev)
            for w in op.writes:
                last_w[w] = op.ev
                readers[w] = []
            all_events[op.ev[0]] = max(all_events.get(op.ev[0], 0), op.ev[1])
            out.append(op)
        self.sched = out
        self.final_events = dict(all_events)

    def bodies(self, sems):
        self.analyze()
        sched, final_events = self.sched, self.final_events

        def body_for(ename):
            def body(eng):
                seen = {}
                for op in sched:
                    if op.eng == "all":
                        for k, v in op.waits.items():
                            if k != ename and seen.get(k, 0) < v:
                                eng.wait_ge(sems[k], v)
                                seen[k] = v
                        continue
                    if op.eng != ename:
                        continue
                    for k, v in op.waits:
                        eng.wait_ge(sems[k], v)
                        seen[k] = max(seen.get(k, 0), v)
                    inst = op.fn(eng)
                    inst.then_inc(sems[op.ev[0]], 16 if op.dma else 1)
                if ename == "sp":
                    for k, v in final_events.items():
                        if seen.get(k, 0) < v:
                            eng.wait_ge(sems[k], v)
            return body

        return {e: body_for(e) for e in ("pe", "act", "dve", "pool", "sp")}


def all_sem_keys():
    keys = list(COMPUTE)
    for q, n in NDMA.items():
        keys += [("dma", q, s) for s in range(n)]
    return keys


def host_consts():
    f = np.arange(128)
    d = f % 64
    inv = (500000.0 ** (-np.arange(0, 16, 2, dtype=np.float32) / 16.0)).astype(np.float32)
    cst = np.zeros((128, 8), np.float32)
    cst[:, 0] = np.where(d < 16, inv[d % 8], 0.0)
    cst[:, 1] = np.where(d < 8, -1.0, np.where(d < 16, 1.0, 0.0))
    cst[:, 2] = EPS
    cst[:, 3] = 1.0
    pm = np.zeros((128, 128), np.float32)
    for m in range(128):
        dm = m % 64
        if dm < 8:
            pm[m + 8, m] = 1.0
        elif dm < 16:
            pm[m - 8, m] = 1.0
    bones = np.zeros((128, 128), np.float32)
    bones[:64, :64] = 1.0 / 64
    bones[64:, 64:] = 1.0 / 64
    return cst, pm, bones


def build(S=2048, L=2, dbg=(), stop=None):
    NT, NG, NB = S // 128, S // 512, S // 256
    VW = 8 * 65 + 4 * 129
    nc = bass.Bass("TRN2", target_bir_lowering=False)

    def din(name, shape, dt=F32):
        return nc.dram_tensor(name, list(shape), dt, kind="ExternalInput").ap()

    x_in = din("x", [S, D])
    cT_in = din("cT", [128, 8])
    pos_in = din("pos", [1, S], I32)
    w_mod = din("w_mod", [L, D, 6 * D])
    b_modT = din("b_modT", [L, 128, 48])
    b_mod = din("b_mod", [L, 1, 6 * D])
    nmixT = din("nmixT", [L, 128, 8])
    nffnT = din("nffnT", [L, 128, 8])
    w_in = din("w_in", [L, D, 3072])
    gcols_in = din("gcols", [L, 128, 4])
    mon_in = din("mon", [L, 1, 64])
    subln_in = din("subln", [L, 1, 128])
    dlam_in = din("dlam", [L, 1, 256])
    w_out = din("w_out", [L, D, D])
    w_gate = din("w_gate", [L, D, DFF])
    w_up = din("w_up", [L, D, DFF])
    w_down = din("w_down", [L, DFF, D])
    cst_in = din("cst", [128, 8])
    pm_in = din("pmat", [128, 128])
    bones_in = din("bones", [128, 128])
    y_out = nc.dram_tensor("y", [S, D], F32, kind="ExternalOutput").ap()
    xmid = [nc.dram_tensor(f"xmid{l}", [S, D], F32, kind="Internal").ap() for l in range(L)]
    xlay = [nc.dram_tensor(f"xlay{l}", [S, D], F32, kind="Internal").ap() for l in range(L - 1)]
    dbg_out = {}
    for name, shape, dt in dbg:
        dbg_out[name] = nc.dram_tensor(name, list(shape), dt, kind="ExternalOutput").ap()

    es = ExitStack()
    with es:
        def sb(name, shape, dt):
            return es.enter_context(nc.sbuf_tensor(name, list(shape), dt))

        def psum(name, shape, dt):
            return es.enter_context(nc.psum_tensor(name, list(shape), dt))

        R1 = sb("R1", [128, max(8 * S, (NFF + 6) * 512)], BF16)
        R2N = max(16 * S + NT * VW, NFF * S, 8 * 6144 if S >= 512 else 0)
        R2 = sb("R2", [128, R2N], BF16)
        ctab = sb("ctab", [128, S], F32)
        stab = sb("stab", [128, S], F32)
        gate_a = sb("gate_a", [128, D], F32)
        gate_f = sb("gate_f", [128, D], F32)
        scrF = sb("scrF", [128, 3072], F32)
        scrB = sb("scrB", [128, 8192], BF16)
        WS = sb("WS", [128, 8192], BF16)
        identb = sb("identb", [128, 128], BF16)
        trimask = sb("trimask", [128, 128], BF16)
        pmb = sb("pmb", [128, 128], BF16)
        bonesb = sb("bonesb", [128, 128], BF16)
        cst = sb("cst_sb", [128, 8], F32)
        small = sb("small", [128, 512], F32)
        cond_rep = sb("cond_rep", [128, 8, 128], BF16)
        condb = sb("condb", [128, 8], BF16)
        kmT = sb("kmT", [128, 4, 8], BF16)
        pb = [psum(f"pb{i}", [128, 512], F32) for i in range(6)]
        pt = [psum(f"pt{i}", [128, 1024], BF16) for i in range(2)]
        sems = {}
        for k in all_sem_keys():
            nm = "s_" + ("_".join(map(str, k)) if isinstance(k, tuple) else k)
            sems[k] = es.enter_context(nc.semaphore(nm))

        P = Prog()
        hT = R1[:, 0:8 * S].rearrange("p (c s) -> p c s", c=8)
        qkT = R2[:, 0:16 * S].rearrange("p (c s) -> p c s", c=16)
        Vall = R2[:, 16 * S:16 * S + NT * VW].rearrange("p (t w) -> p t w", t=NT)
        Vm = Vall[:, :, 0:520].rearrange("p t (h e) -> p t h e", h=8)
        Vd = Vall[:, :, 520:VW].rearrange("p t (h e) -> p t h e", h=4)
        actT = R2[:, 0:NFF * S].rearrange("p (j s) -> p j s", j=NFF)
        wm = R2[:, 0:8 * 6144].rearrange("p (k n) -> p k n", k=8)
        modT = small[:, 0:48]
        sc1, bi1, sc2, bi2 = small[:, 48:56], small[:, 56:64], small[:, 64:72], small[:, 72:80]
        gcols = small[:, 80:84]
        lamcol = small[:, 84:85]
        tmpc = small[:, 85:96]
        nmx = small[:, 96:104]
        nfx = small[:, 104:112]
        bmt = small[:, 112:160]
        mon_bc = small[:, 160:224]
        subln_bc = small[:, 224:352]
        cond = small[:, 352:360]
        ctf = small[:, 360:368]
        lamt = small[:, 368:400]
        kmf = small[:, 400:432].rearrange("p (c n) -> p c n", c=4)
        eps_col = cst[:, 2:3]

        def load_wm(l_):
            for pc in (0, 1, 3, 4, 2, 5):
                P.dma("pool", wm[:, :, pc * D:(pc + 1) * D],
                      w_mod[l_].rearrange("(k p) n -> p k n", p=128)[:, :, pc * D:(pc + 1) * D], w=[("wm", pc)])
        load_wm(0)
        P.dma("sp", cst[:], cst_in, w=["cst"])
        P.dma("sp", ctf, cT_in, w=["ctf"])
        P.dma("pool", pmb[:], pm_in, w=["pmb"])
        P.dma("pool", bonesb[:], bones_in, w=["bonesb"])
        idf = scrF[:, 0:128]
        P.pool("memset", idf, 1.0, w=["idf"])
        P.pool("affine_select", out=idf, in_=idf, pattern=[[-1, 128]], compare_op=ALU.is_equal, fill=0.0,
                                         base=0, channel_multiplier=1, r=["idf"], w=["idf"])
        P.dve("tensor_copy", out=identb[:], in_=idf, r=["idf"], w=["identb"])
        trf = scrF[:, 128:256]
        P.pool("memset", trf, 1.0, w=["trf"])
        P.pool("affine_select", out=trf, in_=trf, pattern=[[1, 128]], compare_op=ALU.is_ge, fill=0.0,
                                         base=0, channel_multiplier=-1, r=["trf"], w=["trf"])
        P.dve("tensor_copy", out=trimask[:], in_=trf, r=["trf"], w=["trimask"])
        P.act("activation", out=cond, in_=ctf, func=AF.Silu, r=["ctf"], w=["cond"])
        P.dve("tensor_copy", out=condb[:], in_=cond, r=["cond"], w=["condb"])
        P.dve("tensor_copy", out=cond_rep[:], in_=cond.unsqueeze(2).broadcast_to([128, 8, 128]),
              r=["cond"], w=["cond_rep"])
        R1f = R1[:, 0:8 * S].bitcast(F32)
        tmpa = R1f[:, 0:S]
        tmpk = R1f[:, S:2 * S]
        tmpki = R1[:, 0:8 * S].bitcast(I32)[:, 2 * S:3 * S]
        tmpkf = R1f[:, 3 * S:4 * S]
        P.dma("sp", ctab[:].bitcast(I32), pos_in.partition_broadcast(128), w=["ctab"])
        P.dve("tensor_copy", out=stab[:], in_=ctab[:].bitcast(I32), r=["ctab"], w=["stab"])
        P.dve("tensor_scalar", out=stab[:], in0=stab[:], scalar1=cst[:, 0:1], scalar2=None, op0=ALU.mult,
              r=["stab", "cst"], w=["stab"])

        def range_reduce_sin(dst, shift, scale_ap):
            P.dve("tensor_scalar", out=tmpa, in0=stab[:], scalar1=float(shift), scalar2=None, op0=ALU.add,
                  r=["stab"], w=["tmpa"])
            P.dve("tensor_scalar", out=tmpk, in0=tmpa, scalar1=float(1.0 / (2 * math.pi)), scalar2=None,
                                            op0=ALU.mult, r=["tmpa"], w=["tmpk"])
            P.dve("tensor_copy", out=tmpki, in_=tmpk, r=["tmpk"], w=["tmpki"])
            P.dve("tensor_copy", out=tmpkf, in_=tmpki, r=["tmpki"], w=["tmpkf"])
            P.dve("scalar_tensor_tensor", out=tmpa, in0=tmpkf, scalar=float(-2 * math.pi), in1=tmpa,
                                                   op0=ALU.mult, op1=ALU.add, r=["tmpkf", "tmpa"], w=["tmpa"])
            P.dve("tensor_scalar", out=tmpk, in0=tmpa, scalar1=float(math.pi), scalar2=float(-2 * math.pi),
                                            op0=ALU.is_gt, op1=ALU.mult, r=["tmpa"], w=["tmpk"])
            P.dve("tensor_tensor", out=tmpa, in0=tmpa, in1=tmpk, op=ALU.add, r=["tmpa", "tmpk"], w=["tmpa"])
            P.dve("tensor_scalar", out=tmpk, in0=tmpa, scalar1=float(-math.pi), scalar2=float(2 * math.pi),
                                            op0=ALU.is_lt, op1=ALU.mult, r=["tmpa"], w=["tmpk"])
            P.dve("tensor_tensor", out=tmpa, in0=tmpa, in1=tmpk, op=ALU.add, r=["tmpa", "tmpk"], w=["tmpa"])
            if scale_ap is None:
                P.act("activation", out=dst, in_=tmpa, func=AF.Sin, r=["tmpa"], w=[dst.name if False else "tab"])
            else:
                P.act("activation", out=dst, in_=tmpa, func=AF.Sin, scale=scale_ap, r=["tmpa", "cst"], w=["tab"])

        range_reduce_sin(ctab[:], math.pi / 2, None)
        P.barrier("setup1")
        range_reduce_sin(stab[:], 0.0, cst[:, 1:2])
        P.barrier("setup2")

        def load_w(dst, src, key):
            P.dma("pool", dst, src, w=[key])

        def rstd_from(ss, n, out_col, tag):
            l1 = tmpc[:, 10:11]
            P.act("activation", out=l1, in_=ss, func=AF.Ln, scale=1.0 / n, bias=eps_col, r=[tag, "cst"], w=["l1"])
            P.act("activation", out=out_col, in_=l1, func=AF.Exp, scale=-0.5, r=["l1"], w=[tag + "r"])

        def ln_phase(xsrc, sc, bi, key_sc):
            junk = scrB[:, 2048:3072]
            xnbs = [scrB[:, 0:1024], scrB[:, 1024:2048]]
            xts = [scrF[:, 0:1024], scrF[:, 1024:2048]]
            banks = [[pb[i][:].bitcast(BF16) for i in range(4)],
                     [pb[4][:].bitcast(BF16), pb[5][:].bitcast(BF16), pt[0][:], pt[1][:]]]
            bkeys = [["pb0", "pb1", "pb2", "pb3"], ["pb4", "pb5", "pt0", "pt1"]]

            def load(t):
                P.dma("sp", xts[t % 2], xsrc[t * 128:(t + 1) * 128, :], w=[("xt", t % 2)])
            load(0)
            for gI in range(NT // 4):
                bs, ks = banks[gI % 2], bkeys[gI % 2]
                for ti in range(4):
                    t = 4 * gI + ti
                    par = t % 2
                    if t + 1 < NT:
                        load(t + 1)
                    xt, xnb = xts[par], xnbs[par]
                    ss = tmpc[:, 5 + par:6 + par]
                    rs = tmpc[:, 7 + par:8 + par]
                    P.act("activation", out=junk, in_=xt, func=AF.Square, accum_out=ss, r=[("xt", par)], w=["junk", ("lss", par)])
                    l1 = tmpc[:, 10:11]
                    P.act("activation", out=l1, in_=ss, func=AF.Ln, scale=1.0 / D, bias=eps_col, r=[("lss", par), "cst"], w=["l1"])
                    P.act("activation", out=rs, in_=l1, func=AF.Exp, scale=-0.5, r=["l1"], w=[("lrs", par)])
                    P.dve("tensor_scalar", out=xnb, in0=xt, scalar1=rs, scalar2=None, op0=ALU.mult,
                          r=[("xt", par), ("lrs", par)], w=[("xnb", par)])
                    for c in range(8):
                        o0 = (c % 2) * 512 + ti * 128
                        P.pe("transpose", bs[c // 2][:, o0:o0 + 128], xnb[:, c * 128:(c + 1) * 128], identb[:],
                             r=[("xnb", par), "identb"], w=[ks[c // 2]])
                for c in range(8):
                    b_ = c // 2
                    src = bs[b_][:, (c % 2) * 512:(c % 2) * 512 + 512]
                    dst = hT[:, c, gI * 512:(gI + 1) * 512]
                    wk = [("h", c, 4 * gI + i) for i in range(4)]
                    if b_ % 2 == 0:
                        P.act("activation", out=dst, in_=src, func=AF.Identity, scale=sc[:, c:c + 1], bias=bi[:, c:c + 1],
                              r=[ks[b_], key_sc], w=wk)
                    else:
                        P.dve("tensor_scalar", out=dst, in0=src, scalar1=sc[:, c:c + 1], scalar2=bi[:, c:c + 1],
                              op0=ALU.mult, op1=ALU.add, r=[ks[b_], key_sc], w=wk)

        def wpiece(w_ap, l, c0, ncol, nk=8):
            return w_ap[l].rearrange("(k p) n -> p k n", p=128)[:, :, c0:c0 + ncol]

        def dump(name, src, keys):
            if name in dbg_out:
                P.dma("sp", dbg_out[name], src, r=keys)

        for l in range(L):
            xsrc = x_in if l == 0 else xlay[l - 1]
            xdst = y_out if l == L - 1 else xlay[l]
            lam_init = 0.8 - 0.6 * math.exp(-0.3 * l)
            if l > 0:
                load_wm(l)
            P.dma("sp", bmt, b_modT[l], w=["bmt"])
            P.dma("sp", nmx, nmixT[l], w=["nmx"])
            P.dma("sp", nfx, nffnT[l], w=["nfx"])
            P.dma("sp", gcols, gcols_in[l], w=["gcols"])
            P.dma("sp", mon_bc, mon_in[l].partition_broadcast(128), w=["mon_bc"])
            P.dma("sp", subln_bc, subln_in[l].partition_broadcast(128), w=["subln_bc"])
            P.dma("sp", scrF[:, 0:256], dlam_in[l].partition_broadcast(128), w=["dl"])
            for j in list(range(0, 16)) + list(range(24, 40)):
                for kc in range(8):
                    P.pe("matmul", pb[0][:, j:j + 1], lhsT=wm[:, kc, j * 128:(j + 1) * 128],
                                                        rhs=condb[:, kc:kc + 1], start=(kc == 0), stop=(kc == 7),
                         r=[("wm", j // 8), "condb"], w=["pb0"])
            for j0 in (0, 24):
                P.dve("tensor_tensor", out=modT[:, j0:j0 + 16], in0=pb[0][:, j0:j0 + 16], in1=bmt[:, j0:j0 + 16],
                      op=ALU.add, r=["pb0", "bmt"], w=["modT"])
            P.dve("scalar_tensor_tensor", out=sc1, in0=modT[:, 8:16], scalar=1.0, in1=nmx, op0=ALU.add, op1=ALU.mult,
                  r=["modT", "nmx"], w=["sc1"])
            P.dve("tensor_copy", out=bi1, in_=modT[:, 0:8], r=["modT"], w=["sc1"])
            P.dve("scalar_tensor_tensor", out=sc2, in0=modT[:, 32:40], scalar=1.0, in1=nfx, op0=ALU.add, op1=ALU.mult,
                  r=["modT", "nfx"], w=["sc2"])
            P.dve("tensor_copy", out=bi2, in_=modT[:, 24:32], r=["modT"], w=["sc2"])
            for gi, (gt, jj) in enumerate(((gate_a, 2), (gate_f, 5))):
                for half in range(2):
                    c0 = jj * D + half * 512
                    pbi = 1 + half
                    bb = scrF[:, 512 + half * 512:1024 + half * 512]
                    P.dma("sp", bb, b_mod[l][:, c0:c0 + 512].partition_broadcast(128), w=[("bb", half)])
                    for kc in range(8):
                        P.pe("matmul", pb[pbi][:], lhsT=cond_rep[:, kc, :],
                                                                      rhs=wm[:, kc, c0:c0 + 512], start=(kc == 0),
                                                                      stop=(kc == 7),
                             r=[("wm", jj), "cond_rep"], w=[f"pb{pbi}"])
                    P.dve("tensor_tensor",
                        out=gt[:, half * 512:(half + 1) * 512], in0=pb[pbi][:], in1=bb, op=ALU.add,
                        r=[f"pb{pbi}", ("bb", half)], w=[("gate", gi, half)])
            dl = scrF[:, 0:256]
            pr = scrF[:, 256:384]
            P.dve("tensor_tensor", out=pr[:, 0:64], in0=dl[:, 0:64], in1=dl[:, 64:128], op=ALU.mult, r=["dl"], w=["pr"])
            P.dve("tensor_tensor", out=pr[:, 64:128], in0=dl[:, 128:192], in1=dl[:, 192:256], op=ALU.mult,
                  r=["dl"], w=["pr"])
            P.dve("tensor_reduce", out=lamt[:, 0:2], in_=pr.rearrange("p (a b) -> p a b", a=2), axis=AX.X,
                                            op=ALU.add, r=["pr"], w=["lamt"])
            P.act("activation", out=lamt[:, 2:4], in_=lamt[:, 0:2], func=AF.Exp, r=["lamt"], w=["lamt"])
            P.dve("tensor_tensor", out=lamt[:, 4:5], in0=lamt[:, 2:3], in1=lamt[:, 3:4], op=ALU.subtract,
                  r=["lamt"], w=["lamt"])
            P.dve("tensor_scalar", out=lamcol, in0=lamt[:, 4:5], scalar1=float(lam_init), scalar2=None, op0=ALU.add,
                  r=["lamt"], w=["lamcol"])
            P.dve("tensor_scalar", out=subln_bc, in0=subln_bc, scalar1=float(1.0 - lam_init), scalar2=None,
                                            op0=ALU.mult, r=["subln_bc"], w=["subln_bc"])
            P.barrier("M")
            pieces = [(0, 0, 0), (512, 4, 1), (1536, 8, 2), (2048, 12, 3)]
            allcols = [0, 512, 1536, 2048, 1024, 2560]
            wAs = [WS[:, sl * 4096:(sl + 1) * 4096].rearrange("p (k n) -> p k n", k=8) for sl in range(2)]

            def load_piece(idx):
                if idx < len(allcols):
                    load_w(wAs[idx % 2], wpiece(w_in, l, allcols[idx], 512), ("wA", idx % 2))
            load_piece(0)
            ln_phase(xsrc, sc1, bi1, "sc1")
            P.barrier("A")
            dump(f"d_hT{l}", R1[:], [])
            P.pool("memset", Vm[:, :, :, 64:65], 1.0, w=["Vones"])
            P.pool("memset", Vd[:, :, :, 128:129], 1.0, w=["Vones"])
            btiles = [(pi, cb, gi, cc, tg) for pi, (col0, cb, gi) in enumerate(pieces) for cc in range(4) for tg in range(NG)]

            def b_mm(n):
                pi, cb, gi, cc, tg = btiles[n]
                if cc == 0 and tg == 0:
                    load_piece(pi + 1)
                par = n % 2
                wA = wAs[pi % 2]
                tok = slice(tg * 512, (tg + 1) * 512)
                for kc in range(8):
                    P.pe("matmul", pb[par][:], lhsT=wA[:, kc, cc * 128:(cc + 1) * 128], rhs=hT[:, kc, tok], start=(kc == 0),
                         stop=(kc == 7), r=[("wA", pi % 2)] + [("h", kc, 4 * tg + i) for i in range(4)], w=[f"pb{par}"])

            def b_bufs(n):
                pi, cb, gi, cc, tg = btiles[n]
                par = n % 2
                d = dict(par=par, pq=pb[par], pss=pb[2 + par], psw=pb[4 + par], kq=f"pb{par}", ks=f"pb{2 + par}",
                         kw=f"pb{4 + par}", tok=slice(tg * 512, (tg + 1) * 512),
                         sqb=scrB[:, par * 512:(par + 1) * 512], qgb=scrB[:, 1024 + par * 512:1024 + (par + 1) * 512],
                         lnr=scrF[:, par * 1536:par * 1536 + 512], qgf=scrF[:, par * 1536 + 512:par * 1536 + 1024],
                         t2=scrF[:, par * 1536 + 1024:par * 1536 + 1536], gcol=gcols[:, gi:gi + 1], dst=(cb + cc, tg))
                return d

            def b_s2(n):
                d = b_bufs(n)
                par = d["par"]
                P.act("activation", out=d["sqb"], in_=d["pq"][:], func=AF.Square, r=[d["kq"]], w=[("sqb", par)])
                P.act("activation", out=d["qgf"], in_=d["pq"][:], func=AF.Copy, scale=d["gcol"], r=[d["kq"], "gcols"], w=[("qgf", par)])
                P.dve("tensor_copy", out=d["qgb"], in_=d["qgf"], r=[("qgf", par)], w=[("qgb", par)])
                P.pe("matmul", d["pss"][:], lhsT=bonesb[:], rhs=d["sqb"], start=True, stop=True, r=["bonesb", ("sqb", par)], w=[d["ks"]])
                P.pe("matmul", d["psw"][:], lhsT=pmb[:], rhs=d["qgb"], start=True, stop=True, r=["pmb", ("qgb", par)], w=[d["kw"]])

            def b_s3(n):
                d = b_bufs(n)
                par, tok, lnr, qgf, t2 = d["par"], d["tok"], d["lnr"], d["qgf"], d["t2"]
                P.act("activation", out=lnr, in_=d["pss"][:], func=AF.Ln, bias=eps_col, r=[d["ks"], "cst"], w=[("lnr", par)])
                P.act("activation", out=lnr, in_=lnr, func=AF.Exp, scale=-0.5, r=[("lnr", par)], w=[("lnr", par)])
                P.dve("tensor_tensor", out=qgf, in0=qgf, in1=ctab[:, tok], op=ALU.mult, r=[("qgf", par), "tab"], w=[("qgf", par)])
                P.dve("tensor_tensor", out=t2, in0=d["psw"][:], in1=stab[:, tok], op=ALU.mult, r=[d["kw"], "tab"], w=[("t2", par)])
                P.pool("tensor_tensor", out=qgf, in0=qgf, in1=t2, op=ALU.add, r=[("qgf", par), ("t2", par)], w=[("qgf", par)])
                P.dve("tensor_tensor", out=qkT[:, d["dst"][0], tok], in0=qgf, in1=lnr, op=ALU.mult,
                      r=[("qgf", par), ("lnr", par)], w=[("qk", d["dst"][0], d["dst"][1])])

            nb_ = len(btiles)
            b_mm(0)
            if nb_ > 1:
                b_mm(1)
            b_s2(0)
            for n in range(nb_):
                if n + 2 < nb_:
                    b_mm(n + 2)
                if n + 1 < nb_:
                    b_s2(n + 1)
                b_s3(n)
            for vi, (col0, isd) in enumerate(((1024, False), (2560, True))):
                wA = wAs[vi % 2]
                wkey = ("wA", vi % 2)
                load_piece(4 + vi + 1)
                for t in range(NT):
                    par = t % 2
                    pv = pb[par]
                    for kc in range(8):
                        P.pe("matmul",
                            pv[:], lhsT=hT[:, kc, t * 128:(t + 1) * 128], rhs=wA[:, kc, :], start=(kc == 0), stop=(kc == 7),
                            r=[wkey, ("h", kc, t)], w=[f"pb{par}"])
                    if isd:
                        dst = Vd[:, t, :, 0:128]
                        src = pv[:].rearrange("p (h e) -> p h e", h=4)
                    else:
                        dst = Vm[:, t, :, 0:64]
                        src = pv[:].rearrange("p (h e) -> p h e", h=8)
                    if t % 2 == 0:
                        P.act("activation", out=dst, in_=src, func=AF.Copy,
                              r=[f"pb{par}"], w=[("V", t, isd)])
                    else:
                        P.dve("tensor_copy", out=dst, in_=src, r=[f"pb{par}"], w=[("V", t, isd)])
            for c in range(4):
                P.dve("tensor_reduce", out=kmf[:, c, 0:NB], in_=qkT[:, 4 + c, :].rearrange("p (b t) -> p b t", t=256),
                                                     axis=AX.X, op=ALU.add,
                      r=[("qk", 4 + c, tg) for tg in range(NG)], w=["kmf"])
            P.dve("tensor_scalar", out=kmT[:, :, 0:NB], in0=kmf[:, :, 0:NB], scalar1=1.0 / 256, scalar2=None,
                                            op0=ALU.mult, r=["kmf"], w=["kmT"])
            P.barrier("B")
            dump(f"d_qk{l}", R2[:, 0:16 * S], [])
            dump(f"d_V{l}", R2[:, 16 * S:16 * S + NT * VW], [])
            oT = hT
            wO = [WS[:, hf * 4096:(hf + 1) * 4096].rearrange("p (k n) -> p k n", k=8) for hf in range(2)]
            for hf in range(2):
                load_w(wO[hf], wpiece(w_out, l, hf * 512, 512), ("wA", hf))
            otok = scrB[:, 0:4096].rearrange("p (j f) -> p j f", j=4)
            pTb = [scrB[:, 4096 + i * 512:4096 + (i + 1) * 512] for i in range(4)]
            biasT = scrB[:, 6144:6656]
            biasq2 = scrB[:, 6656:6784].rearrange("p (d h n) -> p d h n", d=2, h=8)
            biasq = biasq2[:, 0]
            ocs = scrF[:, 0:1024].rearrange("p (j c e) -> p j c e", j=4, c=2)
            gs = scrF[:, 1280:1344].rearrange("p (h n) -> p h n", h=8)
            cnt = scrF[:, 1344:1408].rearrange("p (h n) -> p h n", h=8)
            cmpb = scrF[:, 1408:1920]
            ss4, ls4, rs4, rd4 = small[:, 432:436], small[:, 436:440], small[:, 440:444], small[:, 444:448]
            pt_i = 0
            st_i = 0
            stb = [pb[0], pb[1], pt[0][:].bitcast(F32)]
            stk = ["pb0", "pb1", "pt0"]
            for g in range(NG):
                masked_g = (2 * g + 1) >= 4
                import os
                atdbg = int(os.environ.get("ATDBG", "0"))
                if masked_g and atdbg in (2,):
                    P.dve("memset", biasT, 0.0, w=["biasT"])
                if masked_g and atdbg not in (2, 3):
                    for j in range(4):
                        qt = 4 * g + j
                        qb = qt // 2
                        for h in range(8):
                            base = (h % 2) * 64
                            P.pe("matmul", pb[h % 2][:, h * 8:h * 8 + NB],
                                 lhsT=qkT[base:base + 64, h // 2, qt * 128:(qt + 1) * 128],
                                 rhs=kmT[base:base + 64, h // 2, 0:NB], start=True, stop=True,
                                 r=[("qk", h // 2, g), "kmT"], w=[f"pb{h % 2}"])
                        for par2 in range(2):
                            P.dve("tensor_copy", out=gs[:, par2::2, :],
                                  in_=pb[par2][:, 0:64].rearrange("p (h n) -> p h n", h=8)[:, par2::2, :],
                                  r=[f"pb{par2}"], w=["gs"])
                        cmpv = cmpb[:, 0:8 * qb * qb].rearrange("p (h n m) -> p h n m", h=8, n=qb)
                        in0 = gs[:, :, 0:qb].unsqueeze(2).broadcast_to([128, 8, qb, qb])
                        in1 = gs[:, :, 0:qb].unsqueeze(3).broadcast_to([128, 8, qb, qb])
                        P.dve("tensor_tensor", out=cmpv, in0=in0, in1=in1, op=ALU.is_gt,
                              r=["gs"], w=["cmpb"])
                        P.dve("tensor_reduce", out=cnt[:, :, 0:qb], in_=cmpv, axis=AX.X, op=ALU.add,
                              r=["cmpb"], w=["cnt"])
                        P.dve("memset", scrB[:, 6656:6784], 0.0, w=["biasq"])
                        for d2 in range(2):
                            P.dve("tensor_scalar", out=biasq2[:, d2, :, 0:qb], in0=cnt[:, :, 0:qb], scalar1=2.5,
                                  scalar2=-BIG, op0=ALU.is_gt, op1=ALU.mult, r=["cnt"], w=["biasq"])
                        P.pe("transpose", pt[1][:, 0:128], scrB[:, 6656:6784], identb[:],
                             r=["biasq", "identb"], w=["pt1"])
                        P.dve("tensor_copy", out=biasT[:, j * 128:(j + 1) * 128], in_=pt[1][:, 0:128],
                              r=["pt1"], w=["biasT"])
                nkt = 4 * g + 4
                items = [(hp, kt) for hp in range(16) for kt in range(nkt)]
                info = {}

                def pass_cfg(hp):
                    isd = hp >= 8
                    hh = hp - 8 if isd else hp
                    base = (hh % 2) * 64
                    qc = (8 if isd else 0) + hh // 2
                    kc_ = (12 if isd else 4) + hh // 2
                    if isd:
                        vh, comp, dv = hh // 2, hh % 2, 128
                    else:
                        vh, comp, dv = hh, 0, 64
                    return isd, hh, base, qc, kc_, vh, comp, dv

                def emit_qk(idx):
                    nonlocal st_i, pt_i
                    hp, kt = items[idx]
                    isd, hh, base, qc, kc_, vh, comp, dv = pass_cfg(hp)
                    i = kt - 4 * g
                    c0 = 128 * max(i, 0)
                    ncol = 512 - c0
                    sp_ = st_i % 3
                    st_i += 1
                    ps_ = stb[sp_]
                    pk = stk[sp_]
                    mask_here = (not isd) and masked_g and (kt // 2) < (2 * g + 1) and atdbg not in (1, 3)
                    P.pe("matmul", ps_[:, 0:ncol], lhsT=qkT[base:base + 64, kc_, kt * 128:(kt + 1) * 128],
                         rhs=qkT[base:base + 64, qc, g * 512 + c0:(g + 1) * 512], start=True, stop=not mask_here,
                         r=[("qk", kc_, kt // 4), ("qk", qc, g)], w=[pk])
                    if mask_here:
                        n = kt // 2
                        col = base + 8 * hh + n
                        P.pe("matmul", ps_[:, 0:ncol],
                             lhsT=identb[base:base + 64, col:col + 1].broadcast_to([64, 128]),
                             rhs=biasT[base:base + 64, c0:512], start=False, stop=True,
                             r=["identb", "biasT"], w=[pk])
                    info[idx] = (ps_, pk, pt_i % 4, i, c0, ncol)
                    pt_i += 1

                def finalize(hp):
                    isd, hh, base, qc, kc_, vh, comp, dv = pass_cfg(hp)
                    for j in range(4):
                        acc = pb[2 + j]
                        ak = f"pb{2 + j}"
                        rd = rd4[:, j:j + 1]
                        P.dve("reciprocal", out=rd, in_=acc[:, dv:dv + 1], r=[ak], w=[("rd", j)])
                        if isd and comp == 1:
                            P.dve("tensor_tensor", out=rd, in0=rd, in1=lamcol, op=ALU.mult, r=[("rd", j), "lamcol"], w=[("rd", j)])
                        P.dve("tensor_scalar", out=ocs[:, j, comp, 0:dv], in0=acc[:, 0:dv], scalar1=rd, scalar2=None,
                              op0=ALU.mult, r=[ak, ("rd", j)], w=[("ocs", comp)])
                    if isd and comp == 0:
                        return
                    o0 = ocs[:, :, 0, 0:dv]
                    o1 = ocs[:, :, 1, 0:dv]
                    if isd:
                        P.dve("tensor_tensor", out=o0, in0=o0, in1=o1, op=ALU.subtract, r=[("ocs", 0), ("ocs", 1)], w=[("ocs", 0)])
                    P.dve("tensor_tensor", out=o1, in0=o0, in1=o0, op=ALU.mult, r=[("ocs", 0)], w=[("ocs", 1)])
                    P.dve("tensor_reduce", out=ss4, in_=o1, axis=AX.X, op=ALU.add, r=[("ocs", 1)], w=["ss4"])
                    P.act("activation", out=ls4, in_=ss4, func=AF.Ln, scale=1.0 / dv, bias=eps_col, r=["ss4", "cst"], w=["ls4"])
                    P.act("activation", out=rs4, in_=ls4, func=AF.Exp, scale=-0.5, r=["ls4"], w=["rs4"])
                    P.dve("tensor_tensor", out=o1, in0=o0, in1=rs4.unsqueeze(2).broadcast_to([128, 4, dv]), op=ALU.mult,
                          r=[("ocs", 0), "rs4"], w=[("ocs", 1)])
                    if isd:
                        dst = otok[:, :, 512 + vh * 128:512 + (vh + 1) * 128]
                        gbc = subln_bc.unsqueeze(1).broadcast_to([128, 4, 128])
                        gk = "subln_bc"
                    else:
                        dst = otok[:, :, hh * 64:(hh + 1) * 64]
                        gbc = mon_bc.unsqueeze(1).broadcast_to([128, 4, 64])
                        gk = "mon_bc"
                    P.dve("tensor_tensor", out=dst, in0=o1, in1=gbc, op=ALU.mult, r=[("ocs", 1), gk], w=["otok"])

                def emit_rest(idx):
                    hp, kt = items[idx]
                    isd, hh, base, qc, kc_, vh, comp, dv = pass_cfg(hp)
                    ps_, pk, pti, i, c0, ncol = info[idx]
                    pT = pTb[pti]
                    ptk = ("pT", pti)
                    P.act("activation", out=pT[:, 0:ncol], in_=ps_[:, 0:ncol], func=AF.Exp, scale=0.125, r=[pk], w=[ptk])
                    if i >= 0:
                        P.dve("tensor_tensor", out=pT[:, 0:128], in0=pT[:, 0:128], in1=trimask[:], op=ALU.mult,
                              r=[ptk, "trimask"], w=[ptk])
                    for j in range(max(i, 0), 4):
                        off = (j - max(i, 0)) * 128
                        vr = Vd[:, kt, vh, :] if isd else Vm[:, kt, vh, :]
                        P.pe("matmul", pb[2 + j][:, 0:dv + 1], lhsT=pT[:, off:off + 128], rhs=vr, start=(kt == 0),
                             stop=(kt == 4 * g + j), r=[ptk, ("V", kt, isd)], w=[f"pb{2 + j}"])
                    if kt == nkt - 1:
                        finalize(hp)

                emit_qk(0)
                emit_qk(1)
                for idx in range(len(items)):
                    if idx + 2 < len(items):
                        emit_qk(idx + 2)
                    emit_rest(idx)
                for j in range(4):
                    t = 4 * g + j
                    for c in range(8):
                        P.pe("transpose", pt[1][:, c * 128:(c + 1) * 128], otok[:, j, c * 128:(c + 1) * 128],
                                                             identb[:], r=["otok", "identb"], w=["pt1"])
                    P.act("activation", out=oT[:, :, t * 128:(t + 1) * 128],
                                                      in_=pt[1][:].rearrange("p (c s) -> p c s", c=8), func=AF.Copy,
                          r=["pt1"], w=[("o", t)])
            P.barrier("C")
            dump(f"d_oT{l}", R1[:], [])
            ditems = [(hf, t) for hf in range(2) for t in range(NT)]

            def d_load(n):
                hf, t = ditems[n]
                P.dma("sp", scrF[:, (n % 4) * 512:(n % 4 + 1) * 512], xsrc[t * 128:(t + 1) * 128, hf * 512:(hf + 1) * 512],
                      w=[("xh", n % 4)])
            d_load(0)
            d_load(1)
            for n, (hf, t) in enumerate(ditems):
                if n + 2 < len(ditems):
                    d_load(n + 2)
                par = n % 2
                xh = scrF[:, (n % 4) * 512:(n % 4 + 1) * 512]
                yh = scrF[:, 2048 + par * 512:2048 + (par + 1) * 512]
                py = pb[par]
                for kc in range(8):
                    P.pe("matmul", py[:], lhsT=oT[:, kc, t * 128:(t + 1) * 128], rhs=wO[hf][:, kc, :], start=(kc == 0),
                         stop=(kc == 7), r=[("wA", hf), ("o", t)], w=[f"pb{par}"])
                P.dve("tensor_tensor", out=yh, in0=py[:], in1=gate_a[:, hf * 512:(hf + 1) * 512], op=ALU.mult,
                      r=[f"pb{par}", ("gate", 0, hf)], w=[("yh", par)])
                P.pool("tensor_tensor", out=xh, in0=xh, in1=yh, op=ALU.add, r=[("xh", n % 4), ("yh", par)], w=[("xh", n % 4)])
                P.dma("sp", xmid[l][t * 128:(t + 1) * 128, hf * 512:(hf + 1) * 512], xh, r=[("xh", n % 4)], w=[("xm", t, hf)])
            P.barrier("D")
            npiece = DFF // 256
            def wgu(slot):
                return (WS[:, slot * 4096:slot * 4096 + 2048].rearrange("p (k n) -> p k n", k=8),
                        WS[:, slot * 4096 + 2048:slot * 4096 + 4096].rearrange("p (k n) -> p k n", k=8))

            def load_gu(pi):
                if pi < npiece:
                    a, b = wgu(pi % 2)
                    load_w(a, wpiece(w_gate, l, pi * 256, 256), ("wg", pi % 2))
                    load_w(b, wpiece(w_up, l, pi * 256, 256), ("wu", pi % 2))
            load_gu(0)
            ln_phase(xmid[l], sc2, bi2, "sc2")
            P.barrier("E")
            wdv = w_down[l].rearrange("(j p) n -> p j n", p=128)
            wD0a = scrB[:, 1024:8192].rearrange("p (j n) -> p j n", j=14)
            wD0b = scrF[:, 0:2048].bitcast(BF16).rearrange("p (j n) -> p j n", j=8)
            for pi in range(npiece):
                slot = pi % 2
                wg, wu = wgu(slot)
                load_gu(pi + 1)
                if pi == 1:
                    load_w(wD0a, wdv[:, 0:14, 0:512], ("wD", 0))
                if pi == 3:
                    load_w(wD0b, wdv[:, 14:22, 0:512], ("wD", 3))
                for cc in range(2):
                    jf = 2 * pi + cc
                    for tg in range(NG):
                        par = (jf * NG + tg) % 2
                        pg, pu = pb[par], pb[2 + par]
                        tok = slice(tg * 512, (tg + 1) * 512)
                        for kc in range(8):
                            P.pe("matmul",
                                pg[:], lhsT=wg[:, kc, cc * 128:(cc + 1) * 128], rhs=hT[:, kc, tok], start=(kc == 0), stop=(kc == 7),
                                r=[("wg", slot)] + [("h", kc, 4 * tg + i) for i in range(4)], w=[f"pb{par}"])
                        for kc in range(8):
                            P.pe("matmul",
                                pu[:], lhsT=wu[:, kc, cc * 128:(cc + 1) * 128], rhs=hT[:, kc, tok], start=(kc == 0), stop=(kc == 7),
                                r=[("wu", slot)] + [("h", kc, 4 * tg + i) for i in range(4)], w=[f"pb{2 + par}"])
                        sg = scrB[:, par * 512:(par + 1) * 512]
                        P.act("activation", out=sg, in_=pg[:], func=AF.Silu, r=[f"pb{par}"], w=[("sg", par)])
                        P.dve("tensor_tensor", out=actT[:, jf, tok], in0=pu[:], in1=sg, op=ALU.mult,
                              r=[f"pb{2 + par}", ("sg", par)], w=[("a", jf, tg)])
            P.barrier("F")
            wD1a = WS[:].rearrange("p (j n) -> p j n", j=16)
            wD1b = R1[:, NFF * 512:NFF * 512 + 6 * 512].rearrange("p (j n) -> p j n", j=6)
            load_w(wD1a, wdv[:, 0:16, 512:1024], ("wD", 1))
            load_w(wD1b, wdv[:, 16:22, 512:1024], ("wD", 2))
            gF = R1[:, 0:6144].bitcast(F32)

            def wd_slice(hf, jf):
                if hf == 0:
                    if jf < 14:
                        return wD0a[:, jf, :], ("wD", 0)
                    return wD0b[:, jf - 14, :], ("wD", 3)
                if jf < 16:
                    return wD1a[:, jf, :], ("wD", 1)
                return wD1b[:, jf - 16, :], ("wD", 2)
            gitems = [(hf, t) for hf in range(2) for t in range(NT)]

            def g_load(n):
                hf, t = gitems[n]
                P.dma("sp", gF[:, (n % 4) * 512:(n % 4 + 1) * 512], xmid[l][t * 128:(t + 1) * 128, hf * 512:(hf + 1) * 512],
                      r=[("xm", t, hf)], w=[("xh", n % 4)])
            g_load(0)
            g_load(1)
            for n, (hf, t) in enumerate(gitems):
                if n + 2 < len(gitems):
                    g_load(n + 2)
                par = n % 2
                xh = gF[:, (n % 4) * 512:(n % 4 + 1) * 512]
                yh = gF[:, 2048 + par * 512:2048 + (par + 1) * 512]
                py = pb[par]
                for jf in range(NFF):
                    wsl, wkey = wd_slice(hf, jf)
                    P.pe("matmul", py[:], lhsT=actT[:, jf, t * 128:(t + 1) * 128], rhs=wsl, start=(jf == 0),
                         stop=(jf == NFF - 1), r=[wkey, ("a", jf, t // 4)], w=[f"pb{par}"])
                P.dve("tensor_tensor", out=yh, in0=py[:], in1=gate_f[:, hf * 512:(hf + 1) * 512], op=ALU.mult,
                      r=[f"pb{par}", ("gate", 1, hf)], w=[("yh", par)])
                P.pool("tensor_tensor", out=xh, in0=xh, in1=yh, op=ALU.add, r=[("xh", n % 4), ("yh", par)], w=[("xh", n % 4)])
                P.dma("sp", xdst[t * 128:(t + 1) * 128, hf * 512:(hf + 1) * 512], xh, r=[("xh", n % 4)], w=[("xo", t, hf)])
            P.barrier("G")

        if stop:
            P.ops = P.ops[:P.marks[stop]]
        with nc.Block() as block:
            bodies = P.bodies(sems)
            block.sync(bodies["sp"])
            block.tensor(bodies["pe"])
            block.scalar(bodies["act"])
            block.vector(bodies["dve"])
            block.gpsimd(bodies["pool"])
    return nc


def make_in_maps(inputs, S=2048, L=2, cores=8):
    f32 = np.float32
    x = np.asarray(inputs["x"], f32)
    c = np.asarray(inputs["c"], f32)
    pos = np.asarray(inputs["positions"], np.int32)
    cst, pm, bones = host_consts()

    def colsT(v, n):
        v = np.asarray(v, f32)
        return np.ascontiguousarray(v.reshape(v.shape[0], n, 128).transpose(0, 2, 1))

    def tile2(v):
        v = np.asarray(v, f32)
        return np.concatenate([v, v], axis=1)

    gcols = np.stack([tile2(inputs["moba_q_norm"]), tile2(inputs["moba_k_norm"]),
                      tile2(inputs["diff_q_norm"]), tile2(inputs["diff_k_norm"])], axis=2)
    shared = {
        "w_mod": np.ascontiguousarray(np.asarray(inputs["w_mod"], f32)[:L]),
        "b_modT": colsT(np.asarray(inputs["b_mod"])[:L], 48),
        "b_mod": np.ascontiguousarray(np.asarray(inputs["b_mod"], f32)[:L, None, :]),
        "nmixT": colsT(np.asarray(inputs["norm_mix"])[:L], 8),
        "nffnT": colsT(np.asarray(inputs["norm_ffn"])[:L], 8),
        "w_in": np.ascontiguousarray(np.asarray(inputs["w_in"], f32)[:L]),
        "gcols": np.ascontiguousarray(gcols[:L]),
        "mon": np.ascontiguousarray(np.asarray(inputs["moba_out_norm"], f32)[:L, None, :]),
        "subln": np.ascontiguousarray(np.asarray(inputs["diff_subln"], f32)[:L, None, :]),
        "dlam": np.ascontiguousarray(np.asarray(inputs["diff_lambda"], f32)[:L].reshape(L, 1, 256)),
        "w_out": np.ascontiguousarray(np.asarray(inputs["w_out"], f32)[:L]),
        "w_gate": np.ascontiguousarray(np.asarray(inputs["w_gate"], f32)[:L]),
        "w_up": np.ascontiguousarray(np.asarray(inputs["w_up"], f32)[:L]),
        "w_down": np.ascontiguousarray(np.asarray(inputs["w_down"], f32)[:L]),
        "cst": cst, "pmat": pm, "bones": bones,
    }
    maps = []
    for b in range(cores):
        m = dict(shared)
        m["x"] = np.ascontiguousarray(x[b, :S])
        m["cT"] = np.ascontiguousarray(c[b].reshape(8, 128).T)
        m["pos"] = np.ascontiguousarray(pos[b:b + 1, :S])
        maps.append(m)
    return maps


def kernel(**inputs):
    S, L = 2048, 2
    nc = build(S, L)
    maps = make_in_maps(inputs, S, L, 8)
    res = run_bass_kernel_spmd(nc, maps, core_ids=list(range(8)))
    return np.stack([np.asarray(r["y"], np.float32) for r in res.results], axis=0)
```

```python
import math
from contextlib import ExitStack
import numpy as np
import concourse.bass as bass
import concourse.mybir as mybir
from concourse.bass_utils import run_bass_kernel_spmd

F32 = mybir.dt.float32
BF16 = mybir.dt.bfloat16
I32 = mybir.dt.int32
ALU = mybir.AluOpType
AF = mybir.ActivationFunctionType
AX = mybir.AxisListType

D = 1024
DFF = 2816
NFF = DFF // 128
EPS = 1e-6
BIG = 30000.0
NDMA = {"sp": 20, "pool": 12}
COMPUTE = ("pe", "act", "dve", "pool")


class Op:
    __slots__ = ("eng", "fn", "reads", "writes", "dma", "waits", "ev")

    def __init__(self, eng, fn, reads, writes, dma):
        self.eng, self.fn, self.reads, self.writes, self.dma = eng, fn, reads, writes, dma
        self.waits = []
        self.ev = None


class Prog:
    def __init__(self):
        self.ops = []
        self.marks = {}

    def add(self, eng, name, args, kwargs, reads=(), writes=(), dma=False):
        def fn(e, name=name, args=args, kwargs=kwargs):
            return getattr(e, name)(*args, **kwargs)
        self.ops.append(Op(eng, fn, tuple(reads), tuple(writes), dma))

    def pe(self, name, *args, r=(), w=(), **kw):
        self.add("pe", name, args, kw, r, w)

    def act(self, name, *args, r=(), w=(), **kw):
        self.add("act", name, args, kw, r, w)

    def dve(self, name, *args, r=(), w=(), **kw):
        self.add("dve", name, args, kw, r, w)

    def pool(self, name, *args, r=(), w=(), **kw):
        self.add("pool", name, args, kw, r, w)

    def dma(self, q, out, in_, r=(), w=()):
        self.add(q, "dma_start", (), dict(out=out, in_=in_), r, w, dma=True)

    def barrier(self, mark=None):
        self.ops.append("BARRIER")
        if mark:
            self.marks[mark] = len(self.ops)

    def analyze(self):
        cnt = {e: 0 for e in COMPUTE}
        dcnt = {q: 0 for q in NDMA}
        last_w, readers = {}, {}
        seen = {e: {} for e in ("pe", "act", "dve", "pool", "sp")}
        all_events = {}

        def need(op, ev):
            if ev is None:
                return
            key, val = ev
            if key == "pe" and op.eng == "pe" and not op.dma:
                return
            s = seen[op.eng]
            if s.get(key, 0) >= val:
                return
            s[key] = val
            op.waits.append((key, val))

        out = []
        for op in self.ops:
            if op == "BARRIER":
                b = Op("all", None, (), (), False)
                b.waits = dict(all_events)
                for e in seen:
                    for k, v in all_events.items():
                        if seen[e].get(k, 0) < v:
                            seen[e][k] = v
                out.append(b)
                continue
            if op.dma:
                q = op.eng
                i = dcnt[q]
                dcnt[q] += 1
                key = ("dma", q, i % NDMA[q])
                val = 16 * (i // NDMA[q] + 1)
                if val > 16:
                    need(op, (key, val - 16))
                op.ev = (key, val)
            else:
                cnt[op.eng] += 1
                op.ev = (op.eng, cnt[op.eng])
            for r in op.reads:
                need(op, last_w.get(r))
            for w in op.writes:
                need(op, last_w.get(w))
                for ev in readers.get(w, ()):
                    need(op, ev)
            for r in op.reads:
                readers.setdefault(r, []).append(op.ev)
            for w in op.writes:
                last_w[w] = op.ev
                readers[w] = []
            all_events[op.ev[0]] = max(all_events.get(op.ev[0], 0), op.ev[1])
            out.append(op)
        self.sched = out
        self.final_events = dict(all_events)

    def bodies(self, sems):
        self.analyze()
        sched, final_events = self.sched, self.final_events

        def body_for(ename):
            def body(eng):
                seen = {}
                for op in sched:
                    if op.eng == "all":
                        for k, v in op.waits.items():
                            if k != ename and seen.get(k, 0) < v:
                                eng.wait_ge(sems[k], v)
                                seen[k] = v
                        continue
                    if op.eng != ename:
                        continue
                    for k, v in op.waits:
                        eng.wait_ge(sems[k], v)
                        seen[k] = max(seen.get(k, 0), v)
                    inst = op.fn(eng)
                    inst.then_inc(sems[op.ev[0]], 16 if op.dma else 1)
                if ename == "sp":
                    for k, v in final_events.items():
                        if seen.get(k, 0) < v:
                            eng.wait_ge(sems[k], v)
            return body

        return {e: body_for(e) for e in ("pe", "act", "dve", "pool", "sp")}


def all_sem_keys():
    keys = list(COMPUTE)
    for q, n in NDMA.items():
        keys += [("dma", q, s) for s in range(n)]
    return keys


def host_consts():
    f = np.arange(128)
    d = f % 64
    inv = (500000.0 ** (-np.arange(0, 16, 2, dtype=np.float32) / 16.0)).astype(np.float32)
    cst = np.zeros((128, 8), np.float32)
    cst[:, 0] = np.where(d < 16, inv[d % 8], 0.0)
    cst[:, 1] = np.where(d < 8, -1.0, np.where(d < 16, 1.0, 0.0))
    cst[:, 2] = EPS
    cst[:, 3] = 1.0
    pm = np.zeros((128, 128), np.float32)
    for m in range(128):
        dm = m % 64
        if dm < 8:
            pm[m + 8, m] = 1.0
        elif dm < 16:
            pm[m - 8, m] = 1.0
    bones = np.zeros((128, 128), np.float32)
    bones[:64, :64] = 1.0 / 64
    bones[64:, 64:] = 1.0 / 64
    return cst, pm, bones


def build(S=2048, L=2, dbg=(), stop=None):
    NT, NG, NB = S // 128, S // 512, S // 256
    VW = 8 * 65 + 4 * 129
    nc = bass.Bass("TRN2", target_bir_lowering=False)

    def din(name, shape, dt=F32):
        return nc.dram_tensor(name, list(shape), dt, kind="ExternalInput").ap()

    x_in = din("x", [S, D])
    cT_in = din("cT", [128, 8])
    pos_in = din("pos", [1, S], I32)
    w_mod = din("w_mod", [L, D, 6 * D])
    b_modT = din("b_modT", [L, 128, 48])
    b_mod = din("b_mod", [L, 1, 6 * D])
    nmixT = din("nmixT", [L, 128, 8])
    nffnT = din("nffnT", [L, 128, 8])
    w_in = din("w_in", [L, D, 3072])
    gcols_in = din("gcols", [L, 128, 4])
    mon_in = din("mon", [L, 1, 64])
    subln_in = din("subln", [L, 1, 128])
    dlam_in = din("dlam", [L, 1, 256])
    w_out = din("w_out", [L, D, D])
    w_gate = din("w_gate", [L, D, DFF])
    w_up = din("w_up", [L, D, DFF])
    w_down = din("w_down", [L, DFF, D])
    cst_in = din("cst", [128, 8])
    pm_in = din("pmat", [128, 128])
    bones_in = din("bones", [128, 128])
    y_out = nc.dram_tensor("y", [S, D], F32, kind="ExternalOutput").ap()
    xmid = [nc.dram_tensor(f"xmid{l}", [S, D], F32, kind="Internal").ap() for l in range(L)]
    xlay = [nc.dram_tensor(f"xlay{l}", [S, D], F32, kind="Internal").ap() for l in range(L - 1)]
    dbg_out = {}
    for name, shape, dt in dbg:
        dbg_out[name] = nc.dram_tensor(name, list(shape), dt, kind="ExternalOutput").ap()

    es = ExitStack()
    with es:
        def sb(name, shape, dt):
            return es.enter_context(nc.sbuf_tensor(name, list(shape), dt))

        def psum(name, shape, dt):
            return es.enter_context(nc.psum_tensor(name, list(shape), dt))

        R1 = sb("R1", [128, max(8 * S, (NFF + 6) * 512)], BF16)
        R2N = max(16 * S + NT * VW, NFF * S, 8 * 6144 if S >= 512 else 0)
        R2 = sb("R2", [128, R2N], BF16)
        ctab = sb("ctab", [128, S], F32)
        stab = sb("stab", [128, S], F32)
        gate_a = sb("gate_a", [128, D], F32)
        gate_f = sb("gate_f", [128, D], F32)
        scrF = sb("scrF", [128, 3072], F32)
        scrB = sb("scrB", [128, 8192], BF16)
        WS = sb("WS", [128, 8192], BF16)
        identb = sb("identb", [128, 128], BF16)
        trimask = sb("trimask", [128, 128], BF16)
        pmb = sb("pmb", [128, 128], BF16)
        bonesb = sb("bonesb", [128, 128], BF16)
        cst = sb("cst_sb", [128, 8], F32)
        small = sb("small", [128, 512], F32)
        cond_rep = sb("cond_rep", [128, 8, 128], BF16)
        condb = sb("condb", [128, 8], BF16)
        kmT = sb("kmT", [128, 4, 8], BF16)
        pb = [psum(f"pb{i}", [128, 512], F32) for i in range(6)]
        pt = [psum(f"pt{i}", [128, 1024], BF16) for i in range(2)]
        sems = {}
        for k in all_sem_keys():
            nm = "s_" + ("_".join(map(str, k)) if isinstance(k, tuple) else k)
            sems[k] = es.enter_context(nc.semaphore(nm))

        P = Prog()
        hT = R1[:, 0:8 * S].rearrange("p (c s) -> p c s", c=8)
        qkT = R2[:, 0:16 * S].rearrange("p (c s) -> p c s", c=16)
        Vall = R2[:, 16 * S:16 * S + NT * VW].rearrange("p (t w) -> p t w", t=NT)
        Vm = Vall[:, :, 0:520].rearrange("p t (h e) -> p t h e", h=8)
        Vd = Vall[:, :, 520:VW].rearrange("p t (h e) -> p t h e", h=4)
        actT = R2[:, 0:NFF * S].rearrange("p (j s) -> p j s", j=NFF)
        wm = R2[:, 0:8 * 6144].rearrange("p (k n) -> p k n", k=8)
        modT = small[:, 0:48]
        sc1, bi1, sc2, bi2 = small[:, 48:56], small[:, 56:64], small[:, 64:72], small[:, 72:80]
        gcols = small[:, 80:84]
        lamcol = small[:, 84:85]
        tmpc = small[:, 85:96]
        nmx = small[:, 96:104]
        nfx = small[:, 104:112]
        bmt = small[:, 112:160]
        mon_bc = small[:, 160:224]
        subln_bc = small[:, 224:352]
        cond = small[:, 352:360]
        ctf = small[:, 360:368]
        lamt = small[:, 368:400]
        kmf = small[:, 400:432].rearrange("p (c n) -> p c n", c=4)
        eps_col = cst[:, 2:3]

        def load_wm(l_):
            for pc in (0, 1, 3, 4, 2, 5):
                P.dma("pool", wm[:, :, pc * D:(pc + 1) * D],
                      w_mod[l_].rearrange("(k p) n -> p k n", p=128)[:, :, pc * D:(pc + 1) * D], w=[("wm", pc)])
        load_wm(0)
        P.dma("sp", cst[:], cst_in, w=["cst"])
        P.dma("sp", ctf, cT_in, w=["ctf"])
        P.dma("pool", pmb[:], pm_in, w=["pmb"])
        P.dma("pool", bonesb[:], bones_in, w=["bonesb"])
        idf = scrF[:, 0:128]
        P.pool("memset", idf, 1.0, w=["idf"])
        P.pool("affine_select", out=idf, in_=idf, pattern=[[-1, 128]], compare_op=ALU.is_equal, fill=0.0,
                                         base=0, channel_multiplier=1, r=["idf"], w=["idf"])
        P.dve("tensor_copy", out=identb[:], in_=idf, r=["idf"], w=["identb"])
        trf = scrF[:, 128:256]
        P.pool("memset", trf, 1.0, w=["trf"])
        P.pool("affine_select", out=trf, in_=trf, pattern=[[1, 128]], compare_op=ALU.is_ge, fill=0.0,
                                         base=0, channel_multiplier=-1, r=["trf"], w=["trf"])
        P.dve("tensor_copy", out=trimask[:], in_=trf, r=["trf"], w=["trimask"])
        P.act("activation", out=cond, in_=ctf, func=AF.Silu, r=["ctf"], w=["cond"])
        P.dve("tensor_copy", out=condb[:], in_=cond, r=["cond"], w=["condb"])
        P.dve("tensor_copy", out=cond_rep[:], in_=cond.unsqueeze(2).broadcast_to([128, 8, 128]),
              r=["cond"], w=["cond_rep"])
        R1f = R1[:, 0:8 * S].bitcast(F32)
        tmpa = R1f[:, 0:S]
        tmpk = R1f[:, S:2 * S]
        tmpki = R1[:, 0:8 * S].bitcast(I32)[:, 2 * S:3 * S]
        tmpkf = R1f[:, 3 * S:4 * S]
        P.dma("sp", ctab[:].bitcast(I32), pos_in.partition_broadcast(128), w=["ctab"])
        P.dve("tensor_copy", out=stab[:], in_=ctab[:].bitcast(I32), r=["ctab"], w=["stab"])
        P.dve("tensor_scalar", out=stab[:], in0=stab[:], scalar1=cst[:, 0:1], scalar2=None, op0=ALU.mult,
              r=["stab", "cst"], w=["stab"])

        def range_reduce_sin(dst, shift, scale_ap):
            P.dve("tensor_scalar", out=tmpa, in0=stab[:], scalar1=float(shift), scalar2=None, op0=ALU.add,
                  r=["stab"], w=["tmpa"])
            P.dve("tensor_scalar", out=tmpk, in0=tmpa, scalar1=float(1.0 / (2 * math.pi)), scalar2=None,
                                            op0=ALU.mult, r=["tmpa"], w=["tmpk"])
            P.dve("tensor_copy", out=tmpki, in_=tmpk, r=["tmpk"], w=["tmpki"])
            P.dve("tensor_copy", out=tmpkf, in_=tmpki, r=["tmpki"], w=["tmpkf"])
            P.dve("scalar_tensor_tensor", out=tmpa, in0=tmpkf, scalar=float(-2 * math.pi), in1=tmpa,
                                                   op0=ALU.mult, op1=ALU.add, r=["tmpkf", "tmpa"], w=["tmpa"])
            P.dve("tensor_scalar", out=tmpk, in0=tmpa, scalar1=float(math.pi), scalar2=float(-2 * math.pi),
                                            op0=ALU.is_gt, op1=ALU.mult, r=["tmpa"], w=["tmpk"])
            P.dve("tensor_tensor", out=tmpa, in0=tmpa, in1=tmpk, op=ALU.add, r=["tmpa", "tmpk"], w=["tmpa"])
            P.dve("tensor_scalar", out=tmpk, in0=tmpa, scalar1=float(-math.pi), scalar2=float(2 * math.pi),
                                            op0=ALU.is_lt, op1=ALU.mult, r=["tmpa"], w=["tmpk"])
            P.dve("tensor_tensor", out=tmpa, in0=tmpa, in1=tmpk, op=ALU.add, r=["tmpa", "tmpk"], w=["tmpa"])
            if scale_ap is None:
                P.act("activation", out=dst, in_=tmpa, func=AF.Sin, r=["tmpa"], w=[dst.name if False else "tab"])
            else:
                P.act("activation", out=dst, in_=tmpa, func=AF.Sin, scale=scale_ap, r=["tmpa", "cst"], w=["tab"])

        range_reduce_sin(ctab[:], math.pi / 2, None)
        P.barrier("setup1")
        range_reduce_sin(stab[:], 0.0, cst[:, 1:2])
        P.barrier("setup2")

        def load_w(dst, src, key):
            P.dma("pool", dst, src, w=[key])

        def rstd_from(ss, n, out_col, tag):
            l1 = tmpc[:, 10:11]
            P.act("activation", out=l1, in_=ss, func=AF.Ln, scale=1.0 / n, bias=eps_col, r=[tag, "cst"], w=["l1"])
            P.act("activation", out=out_col, in_=l1, func=AF.Exp, scale=-0.5, r=["l1"], w=[tag + "r"])

        def ln_phase(xsrc, sc, bi, key_sc):
            junk = scrB[:, 2048:3072]
            xnbs = [scrB[:, 0:1024], scrB[:, 1024:2048]]
            xts = [scrF[:, 0:1024], scrF[:, 1024:2048]]
            banks = [[pb[i][:].bitcast(BF16) for i in range(4)],
                     [pb[4][:].bitcast(BF16), pb[5][:].bitcast(BF16), pt[0][:], pt[1][:]]]
            bkeys = [["pb0", "pb1", "pb2", "pb3"], ["pb4", "pb5", "pt0", "pt1"]]

            def load(t):
                P.dma("sp", xts[t % 2], xsrc[t * 128:(t + 1) * 128, :], w=[("xt", t % 2)])
            load(0)
            for gI in range(NT // 4):
                bs, ks = banks[gI % 2], bkeys[gI % 2]
                for ti in range(4):
                    t = 4 * gI + ti
                    par = t % 2
                    if t + 1 < NT:
                        load(t + 1)
                    xt, xnb = xts[par], xnbs[par]
                    ss = tmpc[:, 5 + par:6 + par]
                    rs = tmpc[:, 7 + par:8 + par]
                    P.act("activation", out=junk, in_=xt, func=AF.Square, accum_out=ss, r=[("xt", par)], w=["junk", ("lss", par)])
                    l1 = tmpc[:, 10:11]
                    P.act("activation", out=l1, in_=ss, func=AF.Ln, scale=1.0 / D, bias=eps_col, r=[("lss", par), "cst"], w=["l1"])
                    P.act("activation", out=rs, in_=l1, func=AF.Exp, scale=-0.5, r=["l1"], w=[("lrs", par)])
                    P.dve("tensor_scalar", out=xnb, in0=xt, scalar1=rs, scalar2=None, op0=ALU.mult,
                          r=[("xt", par), ("lrs", par)], w=[("xnb", par)])
                    for c in range(8):
                        o0 = (c % 2) * 512 + ti * 128
                        P.pe("transpose", bs[c // 2][:, o0:o0 + 128], xnb[:, c * 128:(c + 1) * 128], identb[:],
                             r=[("xnb", par), "identb"], w=[ks[c // 2]])
                for c in range(8):
                    b_ = c // 2
                    src = bs[b_][:, (c % 2) * 512:(c % 2) * 512 + 512]
                    dst = hT[:, c, gI * 512:(gI + 1) * 512]
                    wk = [("h", c, 4 * gI + i) for i in range(4)]
                    if b_ % 2 == 0:
                        P.act("activation", out=dst, in_=src, func=AF.Identity, scale=sc[:, c:c + 1], bias=bi[:, c:c + 1],
                              r=[ks[b_], key_sc], w=wk)
                    else:
                        P.dve("tensor_scalar", out=dst, in0=src, scalar1=sc[:, c:c + 1], scalar2=bi[:, c:c + 1],
                              op0=ALU.mult, op1=ALU.add, r=[ks[b_], key_sc], w=wk)

        def wpiece(w_ap, l, c0, ncol, nk=8):
            return w_ap[l].rearrange("(k p) n -> p k n", p=128)[:, :, c0:c0 + ncol]

        def dump(name, src, keys):
            if name in dbg_out:
                P.dma("sp", dbg_out[name], src, r=keys)

        for l in range(L):
            xsrc = x_in if l == 0 else xlay[l - 1]
            xdst = y_out if l == L - 1 else xlay[l]
            lam_init = 0.8 - 0.6 * math.exp(-0.3 * l)
            if l > 0:
                load_wm(l)
            P.dma("sp", bmt, b_modT[l], w=["bmt"])
            P.dma("sp", nmx, nmixT[l], w=["nmx"])
            P.dma("sp", nfx, nffnT[l], w=["nfx"])
            P.dma("sp", gcols, gcols_in[l], w=["gcols"])
            P.dma("sp", mon_bc, mon_in[l].partition_broadcast(128), w=["mon_bc"])
            P.dma("sp", subln_bc, subln_in[l].partition_broadcast(128), w=["subln_bc"])
            P.dma("sp", scrF[:, 0:256], dlam_in[l].partition_broadcast(128), w=["dl"])
            for j in list(range(0, 16)) + list(range(24, 40)):
                for kc in range(8):
                    P.pe("matmul", pb[0][:, j:j + 1], lhsT=wm[:, kc, j * 128:(j + 1) * 128],
                                                        rhs=condb[:, kc:kc + 1], start=(kc == 0), stop=(kc == 7),
                         r=[("wm", j // 8), "condb"], w=["pb0"])
            for j0 in (0, 24):
                P.dve("tensor_tensor", out=modT[:, j0:j0 + 16], in0=pb[0][:, j0:j0 + 16], in1=bmt[:, j0:j0 + 16],
                      op=ALU.add, r=["pb0", "bmt"], w=["modT"])
            P.dve("scalar_tensor_tensor", out=sc1, in0=modT[:, 8:16], scalar=1.0, in1=nmx, op0=ALU.add, op1=ALU.mult,
                  r=["modT", "nmx"], w=["sc1"])
            P.dve("tensor_copy", out=bi1, in_=modT[:, 0:8], r=["modT"], w=["sc1"])
            P.dve("scalar_tensor_tensor", out=sc2, in0=modT[:, 32:40], scalar=1.0, in1=nfx, op0=ALU.add, op1=ALU.mult,
                  r=["modT", "nfx"], w=["sc2"])
            P.dve("tensor_copy", out=bi2, in_=modT[:, 24:32], r=["modT"], w=["sc2"])
            for gi, (gt, jj) in enumerate(((gate_a, 2), (gate_f, 5))):
                for half in range(2):
                    c0 = jj * D + half * 512
                    pbi = 1 + half
                    bb = scrF[:, 512 + half * 512:1024 + half * 512]
                    P.dma("sp", bb, b_mod[l][:, c0:c0 + 512].partition_broadcast(128), w=[("bb", half)])
                    for kc in range(8):
                        P.pe("matmul", pb[pbi][:], lhsT=cond_rep[:, kc, :],
                                                                      rhs=wm[:, kc, c0:c0 + 512], start=(kc == 0),
                                                                      stop=(kc == 7),
                             r=[("wm", jj), "cond_rep"], w=[f"pb{pbi}"])
                    P.dve("tensor_tensor",
                        out=gt[:, half * 512:(half + 1) * 512], in0=pb[pbi][:], in1=bb, op=ALU.add,
                        r=[f"pb{pbi}", ("bb", half)], w=[("gate", gi, half)])
            dl = scrF[:, 0:256]
            pr = scrF[:, 256:384]
            P.dve("tensor_tensor", out=pr[:, 0:64], in0=dl[:, 0:64], in1=dl[:, 64:128], op=ALU.mult, r=["dl"], w=["pr"])
            P.dve("tensor_tensor", out=pr[:, 64:128], in0=dl[:, 128:192], in1=dl[:, 192:256], op=ALU.mult,
                  r=["dl"], w=["pr"])
            P.dve("tensor_reduce", out=lamt[:, 0:2], in_=pr.rearrange("p (a b) -> p a b", a=2), axis=AX.X,
                                            op=ALU.add, r=["pr"], w=["lamt"])
            P.act("activation", out=lamt[:, 2:4], in_=lamt[:, 0:2], func=AF.Exp, r=["lamt"], w=["lamt"])
            P.dve("tensor_tensor", out=lamt[:, 4:5], in0=lamt[:, 2:3], in1=lamt[:, 3:4], op=ALU.subtract,
                  r=["lamt"], w=["lamt"])
            P.dve("tensor_scalar", out=lamcol, in0=lamt[:, 4:5], scalar1=float(lam_init), scalar2=None, op0=ALU.add,
                  r=["lamt"], w=["lamcol"])
            P.dve("tensor_scalar", out=subln_bc, in0=subln_bc, scalar1=float(1.0 - lam_init), scalar2=None,
                                            op0=ALU.mult, r=["subln_bc"], w=["subln_bc"])
            P.barrier("M")
            pieces = [(0, 0, 0), (512, 4, 1), (1536, 8, 2), (2048, 12, 3)]
            allcols = [0, 512, 1536, 2048, 1024, 2560]
            wAs = [WS[:, sl * 4096:(sl + 1) * 4096].rearrange("p (k n) -> p k n", k=8) for sl in range(2)]

            def load_piece(idx):
                if idx < len(allcols):
                    load_w(wAs[idx % 2], wpiece(w_in, l, allcols[idx], 512), ("wA", idx % 2))
            load_piece(0)
            ln_phase(xsrc, sc1, bi1, "sc1")
            P.barrier("A")
            dump(f"d_hT{l}", R1[:], [])
            P.pool("memset", Vm[:, :, :, 64:65], 1.0, w=["Vones"])
            P.pool("memset", Vd[:, :, :, 128:129], 1.0, w=["Vones"])
            btiles = [(pi, cb, gi, cc, tg) for pi, (col0, cb, gi) in enumerate(pieces) for cc in range(4) for tg in range(NG)]

            def b_mm(n):
                pi, cb, gi, cc, tg = btiles[n]
                if cc == 0 and tg == 0:
                    load_piece(pi + 1)
                par = n % 2
                wA = wAs[pi % 2]
                tok = slice(tg * 512, (tg + 1) * 512)
                for kc in range(8):
                    P.pe("matmul", pb[par][:], lhsT=wA[:, kc, cc * 128:(cc + 1) * 128], rhs=hT[:, kc, tok], start=(kc == 0),
                         stop=(kc == 7), r=[("wA", pi % 2)] + [("h", kc, 4 * tg + i) for i in range(4)], w=[f"pb{par}"])

            def b_bufs(n):
                pi, cb, gi, cc, tg = btiles[n]
                par = n % 2
                d = dict(par=par, pq=pb[par], pss=pb[2 + par], psw=pb[4 + par], kq=f"pb{par}", ks=f"pb{2 + par}",
                         kw=f"pb{4 + par}", tok=slice(tg * 512, (tg + 1) * 512),
                         sqb=scrB[:, par * 512:(par + 1) * 512], qgb=scrB[:, 1024 + par * 512:1024 + (par + 1) * 512],
                         lnr=scrF[:, par * 1536:par * 1536 + 512], qgf=scrF[:, par * 1536 + 512:par * 1536 + 1024],
                         t2=scrF[:, par * 1536 + 1024:par * 1536 + 1536], gcol=gcols[:, gi:gi + 1], dst=(cb + cc, tg))
                return d

            def b_s2(n):
                d = b_bufs(n)
                par = d["par"]
                P.act("activation", out=d["sqb"], in_=d["pq"][:], func=AF.Square, r=[d["kq"]], w=[("sqb", par)])
                P.act("activation", out=d["qgf"], in_=d["pq"][:], func=AF.Copy, scale=d["gcol"], r=[d["kq"], "gcols"], w=[("qgf", par)])
                P.dve("tensor_copy", out=d["qgb"], in_=d["qgf"], r=[("qgf", par)], w=[("qgb", par)])
                P.pe("matmul", d["pss"][:], lhsT=bonesb[:], rhs=d["sqb"], start=True, stop=True, r=["bonesb", ("sqb", par)], w=[d["ks"]])
                P.pe("matmul", d["psw"][:], lhsT=pmb[:], rhs=d["qgb"], start=True, stop=True, r=["pmb", ("qgb", par)], w=[d["kw"]])

            def b_s3(n):
                d = b_bufs(n)
                par, tok, lnr, qgf, t2 = d["par"], d["tok"], d["lnr"], d["qgf"], d["t2"]
                P.act("activation", out=lnr, in_=d["pss"][:], func=AF.Ln, bias=eps_col, r=[d["ks"], "cst"], w=[("lnr", par)])
                P.act("activation", out=lnr, in_=lnr, func=AF.Exp, scale=-0.5, r=[("lnr", par)], w=[("lnr", par)])
                P.dve("tensor_tensor", out=qgf, in0=qgf, in1=ctab[:, tok], op=ALU.mult, r=[("qgf", par), "tab"], w=[("qgf", par)])
                P.dve("tensor_tensor", out=t2, in0=d["psw"][:], in1=stab[:, tok], op=ALU.mult, r=[d["kw"], "tab"], w=[("t2", par)])
                P.pool("tensor_tensor", out=qgf, in0=qgf, in1=t2, op=ALU.add, r=[("qgf", par), ("t2", par)], w=[("qgf", par)])
                P.dve("tensor_tensor", out=qkT[:, d["dst"][0], tok], in0=qgf, in1=lnr, op=ALU.mult,
                      r=[("qgf", par), ("lnr", par)], w=[("qk", d["dst"][0], d["dst"][1])])

            nb_ = len(btiles)
            b_mm(0)
            if nb_ > 1:
                b_mm(1)
            b_s2(0)
            for n in range(nb_):
                if n + 2 < nb_:
                    b_mm(n + 2)
                if n + 1 < nb_:
                    b_s2(n + 1)
                b_s3(n)
            for vi, (col0, isd) in enumerate(((1024, False), (2560, True))):
                wA = wAs[vi % 2]
                wkey = ("wA", vi % 2)
                load_piece(4 + vi + 1)
                for t in range(NT):
                    par = t % 2
                    pv = pb[par]
                    for kc in range(8):
                        P.pe("matmul",
                            pv[:], lhsT=hT[:, kc, t * 128:(t + 1) * 128], rhs=wA[:, kc, :], start=(kc == 0), stop=(kc == 7),
                            r=[wkey, ("h", kc, t)], w=[f"pb{par}"])
                    if isd:
                        dst = Vd[:, t, :, 0:128]
                        src = pv[:].rearrange("p (h e) -> p h e", h=4)
                    else:
                        dst = Vm[:, t, :, 0:64]
                        src = pv[:].rearrange("p (h e) -> p h e", h=8)
                    if t % 2 == 0:
                        P.act("activation", out=dst, in_=src, func=AF.Copy,
                              r=[f"pb{par}"], w=[("V", t, isd)])
                    else:
                        P.dve("tensor_copy", out=dst, in_=src, r=[f"pb{par}"], w=[("V", t, isd)])
            for c in range(4):
                P.dve("tensor_reduce", out=kmf[:, c, 0:NB], in_=qkT[:, 4 + c, :].rearrange("p (b t) -> p b t", t=256),
                                                     axis=AX.X, op=ALU.add,
                      r=[("qk", 4 + c, tg) for tg in range(NG)], w=["kmf"])
            P.dve("tensor_scalar", out=kmT[:, :, 0:NB], in0=kmf[:, :, 0:NB], scalar1=1.0 / 256, scalar2=None,
                                            op0=ALU.mult, r=["kmf"], w=["kmT"])
            P.barrier("B")
            dump(f"d_qk{l}", R2[:, 0:16 * S], [])
            dump(f"d_V{l}", R2[:, 16 * S:16 * S + NT * VW], [])
            oT = hT
            wO = [WS[:, hf * 4096:(hf + 1) * 4096].rearrange("p (k n) -> p k n", k=8) for hf in range(2)]
            for hf in range(2):
                load_w(wO[hf], wpiece(w_out, l, hf * 512, 512), ("wA", hf))
            otok = scrB[:, 0:4096].rearrange("p (j f) -> p j f", j=4)
            pTb = [scrB[:, 4096 + i * 512:4096 + (i + 1) * 512] for i in range(4)]
            biasT = scrB[:, 6144:6656]
            biasq2 = scrB[:, 6656:6784].rearrange("p (d h n) -> p d h n", d=2, h=8)
            biasq = biasq2[:, 0]
            ocs = scrF[:, 0:1024].rearrange("p (j c e) -> p j c e", j=4, c=2)
            gs = scrF[:, 1280:1344].rearrange("p (h n) -> p h n", h=8)
            cnt = scrF[:, 1344:1408].rearrange("p (h n) -> p h n", h=8)
            cmpb = scrF[:, 1408:1920]
            ss4, ls4, rs4, rd4 = small[:, 432:436], small[:, 436:440], small[:, 440:444], small[:, 444:448]
            pt_i = 0
            st_i = 0
            stb = [pb[0], pb[1], pt[0][:].bitcast(F32)]
            stk = ["pb0", "pb1", "pt0"]
            for g in range(NG):
                masked_g = (2 * g + 1) >= 4
                import os
                atdbg = int(os.environ.get("ATDBG", "0"))
                if masked_g and atdbg in (2,):
                    P.dve("memset", biasT, 0.0, w=["biasT"])
                if masked_g and atdbg not in (2, 3):
                    for j in range(4):
                        qt = 4 * g + j
                        qb = qt // 2
                        for h in range(8):
                            base = (h % 2) * 64
                            P.pe("matmul", pb[h % 2][:, h * 8:h * 8 + NB],
                                 lhsT=qkT[base:base + 64, h // 2, qt * 128:(qt + 1) * 128],
                                 rhs=kmT[base:base + 64, h // 2, 0:NB], start=True, stop=True,
                                 r=[("qk", h // 2, g), "kmT"], w=[f"pb{h % 2}"])
                        for par2 in range(2):
                            P.dve("tensor_copy", out=gs[:, par2::2, :],
                                  in_=pb[par2][:, 0:64].rearrange("p (h n) -> p h n", h=8)[:, par2::2, :],
                                  r=[f"pb{par2}"], w=["gs"])
                        cmpv = cmpb[:, 0:8 * qb * qb].rearrange("p (h n m) -> p h n m", h=8, n=qb)
                        in0 = gs[:, :, 0:qb].unsqueeze(2).broadcast_to([128, 8, qb, qb])
                        in1 = gs[:, :, 0:qb].unsqueeze(3).broadcast_to([128, 8, qb, qb])
                        P.dve("tensor_tensor", out=cmpv, in0=in0, in1=in1, op=ALU.is_gt,
                              r=["gs"], w=["cmpb"])
                        P.dve("tensor_reduce", out=cnt[:, :, 0:qb], in_=cmpv, axis=AX.X, op=ALU.add,
                              r=["cmpb"], w=["cnt"])
                        P.dve("memset", scrB[:, 6656:6784], 0.0, w=["biasq"])
                        for d2 in range(2):
                            P.dve("tensor_scalar", out=biasq2[:, d2, :, 0:qb], in0=cnt[:, :, 0:qb], scalar1=2.5,
                                  scalar2=-BIG, op0=ALU.is_gt, op1=ALU.mult, r=["cnt"], w=["biasq"])
                        P.pe("transpose", pt[1][:, 0:128], scrB[:, 6656:6784], identb[:],
                             r=["biasq", "identb"], w=["pt1"])
                        P.dve("tensor_copy", out=biasT[:, j * 128:(j + 1) * 128], in_=pt[1][:, 0:128],
                              r=["pt1"], w=["biasT"])
                nkt = 4 * g + 4
                items = [(hp, kt) for hp in range(16) for kt in range(nkt)]
                info = {}

                def pass_cfg(hp):
                    isd = hp >= 8
                    hh = hp - 8 if isd else hp
                    base = (hh % 2) * 64
                    qc = (8 if isd else 0) + hh // 2
                    kc_ = (12 if isd else 4) + hh // 2
                    if isd:
                        vh, comp, dv = hh // 2, hh % 2, 128
                    else:
                        vh, comp, dv = hh, 0, 64
                    return isd, hh, base, qc, kc_, vh, comp, dv

                def emit_qk(idx):
                    nonlocal st_i, pt_i
                    hp, kt = items[idx]
                    isd, hh, base, qc, kc_, vh, comp, dv = pass_cfg(hp)
                    i = kt - 4 * g
                    c0 = 128 * max(i, 0)
                    ncol = 512 - c0
                    sp_ = st_i % 3
                    st_i += 1
                    ps_ = stb[sp_]
                    pk = stk[sp_]
                    mask_here = (not isd) and masked_g and (kt // 2) < (2 * g + 1) and atdbg not in (1, 3)
                    P.pe("matmul", ps_[:, 0:ncol], lhsT=qkT[base:base + 64, kc_, kt * 128:(kt + 1) * 128],
                         rhs=qkT[base:base + 64, qc, g * 512 + c0:(g + 1) * 512], start=True, stop=not mask_here,
                         r=[("qk", kc_, kt // 4), ("qk", qc, g)], w=[pk])
                    if mask_here:
                        n = kt // 2
                        col = base + 8 * hh + n
                        P.pe("matmul", ps_[:, 0:ncol],
                             lhsT=identb[base:base + 64, col:col + 1].broadcast_to([64, 128]),
                             rhs=biasT[base:base + 64, c0:512], start=False, stop=True,
                             r=["identb", "biasT"], w=[pk])
                    info[idx] = (ps_, pk, pt_i % 4, i, c0, ncol)
                    pt_i += 1

                pending = []

                def finalize1(hp, j):
                    isd, hh, base, qc, kc_, vh, comp, dv = pass_cfg(hp)
                    if True:
                        acc = pb[2 + j]
                        ak = f"pb{2 + j}"
                        rd = rd4[:, j:j + 1]
                        P.dve("reciprocal", out=rd, in_=acc[:, dv:dv + 1], r=[ak], w=[("rd", j)])
                        if isd and comp == 1:
                            P.dve("tensor_tensor", out=rd, in0=rd, in1=lamcol, op=ALU.mult, r=[("rd", j), "lamcol"], w=[("rd", j)])
                        P.dve("tensor_scalar", out=ocs[:, j, comp, 0:dv], in0=acc[:, 0:dv], scalar1=rd, scalar2=None,
                              op0=ALU.mult, r=[ak, ("rd", j)], w=[("ocs", comp)])

                def finalize2(hp):
                    isd, hh, base, qc, kc_, vh, comp, dv = pass_cfg(hp)
                    if isd and comp == 0:
                        return
                    o0 = ocs[:, :, 0, 0:dv]
                    o1 = ocs[:, :, 1, 0:dv]
                    if isd:
                        P.dve("tensor_tensor", out=o0, in0=o0, in1=o1, op=ALU.subtract, r=[("ocs", 0), ("ocs", 1)], w=[("ocs", 0)])
                    P.dve("tensor_tensor", out=o1, in0=o0, in1=o0, op=ALU.mult, r=[("ocs", 0)], w=[("ocs", 1)])
                    P.dve("tensor_reduce", out=ss4, in_=o1, axis=AX.X, op=ALU.add, r=[("ocs", 1)], w=["ss4"])
                    P.act("activation", out=ls4, in_=ss4, func=AF.Ln, scale=1.0 / dv, bias=eps_col, r=["ss4", "cst"], w=["ls4"])
                    P.act("activation", out=rs4, in_=ls4, func=AF.Exp, scale=-0.5, r=["ls4"], w=["rs4"])
                    P.dve("tensor_tensor", out=o1, in0=o0, in1=rs4.unsqueeze(2).broadcast_to([128, 4, dv]), op=ALU.mult,
                          r=[("ocs", 0), "rs4"], w=[("ocs", 1)])
                    if isd:
                        dst = otok[:, :, 512 + vh * 128:512 + (vh + 1) * 128]
                        gbc = subln_bc.unsqueeze(1).broadcast_to([128, 4, 128])
                        gk = "subln_bc"
                    else:
                        dst = otok[:, :, hh * 64:(hh + 1) * 64]
                        gbc = mon_bc.unsqueeze(1).broadcast_to([128, 4, 64])
                        gk = "mon_bc"
                    P.dve("tensor_tensor", out=dst, in0=o1, in1=gbc, op=ALU.mult, r=[("ocs", 1), gk], w=["otok"])

                def emit_rest(idx):
                    hp, kt = items[idx]
                    isd, hh, base, qc, kc_, vh, comp, dv = pass_cfg(hp)
                    ps_, pk, pti, i, c0, ncol = info[idx]
                    pT = pTb[pti]
                    ptk = ("pT", pti)
                    P.act("activation", out=pT[:, 0:ncol], in_=ps_[:, 0:ncol], func=AF.Exp, scale=0.125, r=[pk], w=[ptk])
                    if i >= 0:
                        P.dve("tensor_tensor", out=pT[:, 0:128], in0=pT[:, 0:128], in1=trimask[:], op=ALU.mult,
                              r=[ptk, "trimask"], w=[ptk])
                    for j in range(max(i, 0), 4):
                        off = (j - max(i, 0)) * 128
                        vr = Vd[:, kt, vh, :] if isd else Vm[:, kt, vh, :]
                        P.pe("matmul", pb[2 + j][:, 0:dv + 1], lhsT=pT[:, off:off + 128], rhs=vr, start=(kt == 0),
                             stop=(kt == 4 * g + j), r=[ptk, ("V", kt, isd)], w=[f"pb{2 + j}"])
                    if i >= 0:
                        finalize1(hp, i)
                    if kt == nkt - 1:
                        pending.append((idx + (3 if g >= 1 else 0), hp))
                    while pending and pending[0][0] <= idx:
                        finalize2(pending.pop(0)[1])

                emit_qk(0)
                emit_qk(1)
                for idx in range(len(items)):
                    if idx + 2 < len(items):
                        emit_qk(idx + 2)
                    emit_rest(idx)
                while pending:
                    finalize2(pending.pop(0)[1])
                for j in range(4):
                    t = 4 * g + j
                    for c in range(8):
                        P.pe("transpose", pt[1][:, c * 128:(c + 1) * 128], otok[:, j, c * 128:(c + 1) * 128],
                                                             identb[:], r=["otok", "identb"], w=["pt1"])
                    P.act("activation", out=oT[:, :, t * 128:(t + 1) * 128],
                                                      in_=pt[1][:].rearrange("p (c s) -> p c s", c=8), func=AF.Copy,
                          r=["pt1"], w=[("o", t)])
            P.barrier("C")
            dump(f"d_oT{l}", R1[:], [])
            ditems = [(hf, t) for hf in range(2) for t in range(NT)]

            def d_load(n):
                hf, t = ditems[n]
                P.dma("sp", scrF[:, (n % 4) * 512:(n % 4 + 1) * 512], xsrc[t * 128:(t + 1) * 128, hf * 512:(hf + 1) * 512],
                      w=[("xh", n % 4)])
            d_load(0)
            d_load(1)
            for n, (hf, t) in enumerate(ditems):
                if n + 2 < len(ditems):
                    d_load(n + 2)
                par = n % 2
                xh = scrF[:, (n % 4) * 512:(n % 4 + 1) * 512]
                yh = scrF[:, 2048 + par * 512:2048 + (par + 1) * 512]
                py = pb[par]
                for kc in range(8):
                    P.pe("matmul", py[:], lhsT=oT[:, kc, t * 128:(t + 1) * 128], rhs=wO[hf][:, kc, :], start=(kc == 0),
                         stop=(kc == 7), r=[("wA", hf), ("o", t)], w=[f"pb{par}"])
                P.dve("tensor_tensor", out=yh, in0=py[:], in1=gate_a[:, hf * 512:(hf + 1) * 512], op=ALU.mult,
                      r=[f"pb{par}", ("gate", 0, hf)], w=[("yh", par)])
                P.pool("tensor_tensor", out=xh, in0=xh, in1=yh, op=ALU.add, r=[("xh", n % 4), ("yh", par)], w=[("xh", n % 4)])
                P.dma("sp", xmid[l][t * 128:(t + 1) * 128, hf * 512:(hf + 1) * 512], xh, r=[("xh", n % 4)], w=[("xm", t, hf)])
            P.barrier("D")
            npiece = DFF // 256
            def wgu(slot):
                return (WS[:, slot * 4096:slot * 4096 + 2048].rearrange("p (k n) -> p k n", k=8),
                        WS[:, slot * 4096 + 2048:slot * 4096 + 4096].rearrange("p (k n) -> p k n", k=8))

            def load_gu(pi):
                if pi < npiece:
                    a, b = wgu(pi % 2)
                    load_w(a, wpiece(w_gate, l, pi * 256, 256), ("wg", pi % 2))
                    load_w(b, wpiece(w_up, l, pi * 256, 256), ("wu", pi % 2))
            load_gu(0)
            ln_phase(xmid[l], sc2, bi2, "sc2")
            P.barrier("E")
            wdv = w_down[l].rearrange("(j p) n -> p j n", p=128)
            wD0a = scrB[:, 1024:8192].rearrange("p (j n) -> p j n", j=14)
            wD0b = scrF[:, 0:2048].bitcast(BF16).rearrange("p (j n) -> p j n", j=8)
            for pi in range(npiece):
                slot = pi % 2
                wg, wu = wgu(slot)
                load_gu(pi + 1)
                if pi == 1:
                    load_w(wD0a, wdv[:, 0:14, 0:512], ("wD", 0))
                if pi == 3:
                    load_w(wD0b, wdv[:, 14:22, 0:512], ("wD", 3))
                for cc in range(2):
                    jf = 2 * pi + cc
                    for tg in range(NG):
                        par = (jf * NG + tg) % 2
                        pg, pu = pb[par], pb[2 + par]
                        tok = slice(tg * 512, (tg + 1) * 512)
                        for kc in range(8):
                            P.pe("matmul",
                                pg[:], lhsT=wg[:, kc, cc * 128:(cc + 1) * 128], rhs=hT[:, kc, tok], start=(kc == 0), stop=(kc == 7),
                                r=[("wg", slot)] + [("h", kc, 4 * tg + i) for i in range(4)], w=[f"pb{par}"])
                        for kc in range(8):
                            P.pe("matmul",
                                pu[:], lhsT=wu[:, kc, cc * 128:(cc + 1) * 128], rhs=hT[:, kc, tok], start=(kc == 0), stop=(kc == 7),
                                r=[("wu", slot)] + [("h", kc, 4 * tg + i) for i in range(4)], w=[f"pb{2 + par}"])
                        sg = scrB[:, par * 512:(par + 1) * 512]
                        P.act("activation", out=sg, in_=pg[:], func=AF.Silu, r=[f"pb{par}"], w=[("sg", par)])
                        P.dve("tensor_tensor", out=actT[:, jf, tok], in0=pu[:], in1=sg, op=ALU.mult,
                              r=[f"pb{2 + par}", ("sg", par)], w=[("a", jf, tg)])
            P.barrier("F")
            wD1a = WS[:].rearrange("p (j n) -> p j n", j=16)
            wD1b = R1[:, NFF * 512:NFF * 512 + 6 * 512].rearrange("p (j n) -> p j n", j=6)
            load_w(wD1a, wdv[:, 0:16, 512:1024], ("wD", 1))
            load_w(wD1b, wdv[:, 16:22, 512:1024], ("wD", 2))
            gF = R1[:, 0:6144].bitcast(F32)

            def wd_slice(hf, jf):
                if hf == 0:
                    if jf < 14:
                        return wD0a[:, jf, :], ("wD", 0)
                    return wD0b[:, jf - 14, :], ("wD", 3)
                if jf < 16:
                    return wD1a[:, jf, :], ("wD", 1)
                return wD1b[:, jf - 16, :], ("wD", 2)
            gitems = [(hf, t) for hf in range(2) for t in range(NT)]

            def g_load(n):
                hf, t = gitems[n]
                P.dma("sp", gF[:, (n % 4) * 512:(n % 4 + 1) * 512], xmid[l][t * 128:(t + 1) * 128, hf * 512:(hf + 1) * 512],
                      r=[("xm", t, hf)], w=[("xh", n % 4)])
            g_load(0)
            g_load(1)
            for n, (hf, t) in enumerate(gitems):
                if n + 2 < len(gitems):
                    g_load(n + 2)
                par = n % 2
                xh = gF[:, (n % 4) * 512:(n % 4 + 1) * 512]
                yh = gF[:, 2048 + par * 512:2048 + (par + 1) * 512]
                py = pb[par]
                for jf in range(NFF):
                    wsl, wkey = wd_slice(hf, jf)
                    P.pe("matmul", py[:], lhsT=actT[:, jf, t * 128:(t + 1) * 128], rhs=wsl, start=(jf == 0),
                         stop=(jf == NFF - 1), r=[wkey, ("a", jf, t // 4)], w=[f"pb{par}"])
                P.dve("tensor_tensor", out=yh, in0=py[:], in1=gate_f[:, hf * 512:(hf + 1) * 512], op=ALU.mult,
                      r=[f"pb{par}", ("gate", 1, hf)], w=[("yh", par)])
                P.pool("tensor_tensor", out=xh, in0=xh, in1=yh, op=ALU.add, r=[("xh", n % 4), ("yh", par)], w=[("xh", n % 4)])
                P.dma("sp", xdst[t * 128:(t + 1) * 128, hf * 512:(hf + 1) * 512], xh, r=[("xh", n % 4)], w=[("xo", t, hf)])
            P.barrier("G")

        if stop:
            P.ops = P.ops[:P.marks[stop]]
        with nc.Block() as block:
            bodies = P.bodies(sems)
            block.sync(bodies["sp"])
            block.tensor(bodies["pe"])
            block.scalar(bodies["act"])
            block.vector(bodies["dve"])
            block.gpsimd(bodies["pool"])
    return nc


def make_in_maps(inputs, S=2048, L=2, cores=8):
    f32 = np.float32
    x = np.asarray(inputs["x"], f32)
    c = np.asarray(inputs["c"], f32)
    pos = np.asarray(inputs["positions"], np.int32)
    cst, pm, bones = host_consts()

    def colsT(v, n):
        v = np.asarray(v, f32)
        return np.ascontiguousarray(v.reshape(v.shape[0], n, 128).transpose(0, 2, 1))

    def tile2(v):
        v = np.asarray(v, f32)
        return np.concatenate([v, v], axis=1)

    gcols = np.stack([tile2(inputs["moba_q_norm"]), tile2(inputs["moba_k_norm"]),
                      tile2(inputs["diff_q_norm"]), tile2(inputs["diff_k_norm"])], axis=2)
    shared = {
        "w_mod": np.ascontiguousarray(np.asarray(inputs["w_mod"], f32)[:L]),
        "b_modT": colsT(np.asarray(inputs["b_mod"])[:L], 48),
        "b_mod": np.ascontiguousarray(np.asarray(inputs["b_mod"], f32)[:L, None, :]),
        "nmixT": colsT(np.asarray(inputs["norm_mix"])[:L], 8),
        "nffnT": colsT(np.asarray(inputs["norm_ffn"])[:L], 8),
        "w_in": np.ascontiguousarray(np.asarray(inputs["w_in"], f32)[:L]),
        "gcols": np.ascontiguousarray(gcols[:L]),
        "mon": np.ascontiguousarray(np.asarray(inputs["moba_out_norm"], f32)[:L, None, :]),
        "subln": np.ascontiguousarray(np.asarray(inputs["diff_subln"], f32)[:L, None, :]),
        "dlam": np.ascontiguousarray(np.asarray(inputs["diff_lambda"], f32)[:L].reshape(L, 1, 256)),
        "w_out": np.ascontiguousarray(np.asarray(inputs["w_out"], f32)[:L]),
        "w_gate": np.ascontiguousarray(np.asarray(inputs["w_gate"], f32)[:L]),
        "w_up": np.ascontiguousarray(np.asarray(inputs["w_up"], f32)[:L]),
        "w_down": np.ascontiguousarray(np.asarray(inputs["w_down"], f32)[:L]),
        "cst": cst, "pmat": pm, "bones": bones,
    }
    maps = []
    for b in range(cores):
        m = dict(shared)
        m["x"] = np.ascontiguousarray(x[b, :S])
        m["cT"] = np.ascontiguousarray(c[b].reshape(8, 128).T)
        m["pos"] = np.ascontiguousarray(pos[b:b + 1, :S])
        maps.append(m)
    return maps


def kernel(**inputs):
    S, L = 2048, 2
    nc = build(S, L)
    maps = make_in_maps(inputs, S, L, 8)
    res = run_bass_kernel_spmd(nc, maps, core_ids=list(range(8)))
    return np.stack([np.asarray(r["y"], np.float32) for r in res.results], axis=0)
```
